# Optimizing a Trainium2 kernel written in Bass

```python
import math
import jax
import jax.numpy as jnp
from jax import lax
import numpy as np

D_MODEL = 1024
BATCH = 4
SEQ = 4096
DEPTH = 4

N_MIXERS = 4
N_META = 16
BLOCK = 128
PAD = BLOCK - N_META
ALPHA = (2.0 * DEPTH) ** 0.25
BETA = (8.0 * DEPTH) ** -0.25
LN_EPS = 1e-5
RMS_EPS = 1e-6
NEG_INF = -1e30

D_FF = 2816

SSD_D_INNER = 2 * D_MODEL
SSD_HEAD_DIM = 64
SSD_HEADS = SSD_D_INNER // SSD_HEAD_DIM
SSD_GROUPS = 8
SSD_STATE = 128
SSD_CONV = 4
SSD_CONV_CH = SSD_D_INNER + 2 * SSD_GROUPS * SSD_STATE
SSD_IN = SSD_D_INNER + SSD_CONV_CH + SSD_HEADS

DIFF_HEADS = 8
DIFF_HEAD_DIM = D_MODEL // (2 * DIFF_HEADS)
DIFF_V_DIM = 2 * DIFF_HEAD_DIM

REL_BUCKETS = 32
REL_MAX_DIST = 128

S5_GROUP = 16
S5_GROUPS = D_MODEL // S5_GROUP
S5_STATE = 64

MLA_HEADS = 16
MLA_Q_RANK = 384
MLA_KV_RANK = 256
MLA_NOPE = 64
MLA_ROPE = 32
MLA_V = 64
MLA_IN = MLA_Q_RANK + MLA_KV_RANK + MLA_ROPE
ROPE_BASE = 10000.0

N_SSD = (DEPTH + N_MIXERS - 1) // N_MIXERS
N_DIFF = (DEPTH + N_MIXERS - 2) // N_MIXERS
N_S5 = (DEPTH + N_MIXERS - 3) // N_MIXERS
N_MLA = (DEPTH + N_MIXERS - 4) // N_MIXERS

kernel_name = "hybrid_ssd_diffattn_s5_mla_trunk"


def layer_norm(x, g, b):
    xf = x.astype(jnp.float32)
    mu = jnp.mean(xf, -1, keepdims=True)
    var = jnp.mean(jnp.square(xf - mu), -1, keepdims=True)
    return ((xf - mu) * lax.rsqrt(var + LN_EPS) * g + b).astype(x.dtype)


def rms_norm(x, g):
    xf = x.astype(jnp.float32)
    return (xf * lax.rsqrt(jnp.mean(xf * xf, -1, keepdims=True) + RMS_EPS) * g).astype(x.dtype)


def swiglu(x, w1, w3, w2):
    return (jax.nn.silu(x @ w1) * (x @ w3)) @ w2


def front_pad(t):
    pad = [(0, 0)] * t.ndim
    pad[1] = (PAD, 0)
    return jnp.pad(t, pad)


def block_mask(blk, n_keys):
    q_pos = blk * BLOCK + jnp.arange(BLOCK)
    k_pos = jnp.arange(n_keys)
    valid = (k_pos[None, :] <= q_pos[:, None]) & (k_pos[None, :] >= PAD)
    return q_pos, k_pos, valid


def t5_bucket(dist):
    max_exact = REL_BUCKETS // 2
    d = jnp.maximum(dist, 0)
    df = jnp.maximum(d, max_exact).astype(jnp.float32)
    large = max_exact + (jnp.log(df / max_exact) / math.log(REL_MAX_DIST / max_exact)
                         * (REL_BUCKETS - max_exact)).astype(jnp.int32)
    large = jnp.minimum(large, REL_BUCKETS - 1)
    return jnp.where(d < max_exact, d, large)


def ssd_chunked(x, dt, A, Bm, Cm):
    x, dt, Bm, Cm = front_pad(x), front_pad(dt), front_pad(Bm), front_pad(Cm)
    b, lp, h, p = x.shape
    g, n = Bm.shape[2], Bm.shape[3]
    r = h // g
    c = lp // BLOCK
    a = (dt * A).reshape(b, c, BLOCK, g, r)
    xdt = (x * dt[..., None]).reshape(b, c, BLOCK, g, r, p)
    Bm = Bm.reshape(b, c, BLOCK, g, n).astype(xdt.dtype)
    Cm = Cm.reshape(b, c, BLOCK, g, n).astype(xdt.dtype)
    a_cum = jnp.cumsum(a, axis=2)
    seg = a_cum[:, :, :, None] - a_cum[:, :, None, :]
    causal = jnp.tril(jnp.ones((BLOCK, BLOCK), bool))[:, :, None, None]
    decay = jnp.exp(jnp.where(causal, seg, -jnp.inf))
    cb = jnp.einsum('bctgn,bcsgn->bctsg', Cm, Bm)
    y_diag = jnp.einsum('bctsgr,bcsgrp->bctgrp', cb[..., None] * decay, xdt)
    decay_end = jnp.exp(a_cum[:, :, -1:] - a_cum)
    states = jnp.einsum('bcsgn,bcsgrp->bcgrpn', Bm, xdt * decay_end[..., None])
    chunk_decay = jnp.exp(a_cum[:, :, -1])

    def step(s, inp):
        dec, st = inp
        return s * dec[..., None, None] + st, s

    _, s_in = lax.scan(step, jnp.zeros_like(states[:, 0]),
                       (jnp.moveaxis(chunk_decay, 1, 0).astype(states.dtype), jnp.moveaxis(states, 1, 0)))
    s_in = jnp.moveaxis(s_in, 0, 1)
    y_off = jnp.einsum('bctgn,bcgrpn->bctgrp', Cm, s_in) * jnp.exp(a_cum)[..., None]
    y = (y_diag + y_off).reshape(b, lp, h, p)
    return y[:, PAD:]


def mamba2_mixer(x, w_in, conv_w, conv_b, dt_bias, a_log, d_skip, norm_g, w_out):
    b, l, _ = x.shape
    zxbcdt = x @ w_in
    z, xbc, dt = jnp.split(zxbcdt, [SSD_D_INNER, SSD_D_INNER + SSD_CONV_CH], axis=-1)
    xbc = lax.conv_general_dilated(xbc, conv_w[:, None, :].astype(xbc.dtype), window_strides=(1,),
                                   padding=[(SSD_CONV - 1, 0)],
                                   dimension_numbers=('NWC', 'WIO', 'NWC'),
                                   feature_group_count=SSD_CONV_CH) + conv_b
    xbc = jax.nn.silu(xbc)
    xs, Bm, Cm = jnp.split(xbc, [SSD_D_INNER, SSD_D_INNER + SSD_GROUPS * SSD_STATE], axis=-1)
    dt = jax.nn.softplus((dt + dt_bias).astype(jnp.float32))
    A = -jnp.exp(a_log.astype(jnp.float32))
    xs = xs.reshape(b, l, SSD_HEADS, SSD_HEAD_DIM)
    y = ssd_chunked(xs, dt, A, Bm.reshape(b, l, SSD_GROUPS, SSD_STATE),
                    Cm.reshape(b, l, SSD_GROUPS, SSD_STATE))
    y = (y + d_skip[:, None] * xs).astype(x.dtype).reshape(b, l, SSD_D_INNER) * jax.nn.silu(z)
    gs = SSD_D_INNER // SSD_GROUPS
    y = rms_norm(y.reshape(b, l, SSD_GROUPS, gs), norm_g.reshape(SSD_GROUPS, gs)).reshape(b, l, SSD_D_INNER)
    return y @ w_out


def diff_attention(x, w_qkv, lam_q1, lam_k1, lam_q2, lam_k2, subln_g, w_out, rel_bias, lam_init):
    b, l, _ = x.shape
    h, d, e = DIFF_HEADS, DIFF_HEAD_DIM, DIFF_V_DIM
    qkv = front_pad(x @ w_qkv)
    lp = l + PAD
    q, k, v = jnp.split(qkv, [2 * h * d, 4 * h * d], axis=-1)
    q = q.reshape(b, lp, h, 2, d) * (d ** -0.5)
    k = k.reshape(b, lp, h, 2, d)
    v = v.reshape(b, lp, h, e)
    f32 = jnp.float32
    lam = (jnp.exp(jnp.sum(lam_q1.astype(f32) * lam_k1.astype(f32)))
           - jnp.exp(jnp.sum(lam_q2.astype(f32) * lam_k2.astype(f32))) + lam_init)
    nb = lp // BLOCK
    q_blocks = jnp.moveaxis(q.reshape(b, nb, BLOCK, h, 2, d), 1, 0)

    def one_block(args):
        qb, blk = args
        q_pos, k_pos, valid = block_mask(blk, lp)
        bias = rel_bias[t5_bucket(q_pos[:, None] - k_pos[None, :])]
        s = jnp.einsum('bqhmd,bkhmd->bhmqk', qb, k).astype(f32) \
            + jnp.transpose(bias, (2, 0, 1))[None, :, None].astype(f32)
        s = jnp.where(valid, s, NEG_INF)
        p = jax.nn.softmax(s, axis=-1)
        att = p[:, :, 0] - lam * p[:, :, 1]
        return jnp.einsum('bhqk,bkhe->bqhe', att.astype(v.dtype), v)

    o = lax.map(one_block, (q_blocks, jnp.arange(nb)))
    o = jnp.moveaxis(o, 0, 1).reshape(b, lp, h, e)[:, PAD:]
    o = rms_norm(o, subln_g) * (1.0 - lam_init)
    return o.reshape(b, l, h * e) @ w_out


def s5_mixer(x, lam_re, lam_im, log_step, b_re, b_im, c_re, c_im, d_skip, w_glu, b_glu):
    bsz, l, _ = x.shape
    f32 = jnp.float32
    u = x.reshape(bsz, l, S5_GROUPS, S5_GROUP).astype(f32)
    step = jnp.exp(log_step.astype(f32))[:, None]
    lr = jnp.minimum(lam_re.astype(f32), -1e-4)
    li = lam_im.astype(f32)
    mag = jnp.exp(lr * step)
    ar, ai = mag * jnp.cos(li * step), mag * jnp.sin(li * step)
    den = lr * lr + li * li
    cr = ((ar - 1.0) * lr + ai * li) / den
    ci = (ai * lr - (ar - 1.0) * li) / den
    br, bi = b_re.astype(f32), b_im.astype(f32)
    bbr = cr[..., None] * br - ci[..., None] * bi
    bbi = cr[..., None] * bi + ci[..., None] * br
    bu_r = jnp.einsum('gpc,blgc->lbgp', bbr, u)
    bu_i = jnp.einsum('gpc,blgc->lbgp', bbi, u)
    a_r = jnp.broadcast_to(ar[None, None], (l, 1) + ar.shape)
    a_i = jnp.broadcast_to(ai[None, None], (l, 1) + ai.shape)

    def combine(e1, e2):
        a1r, a1i, b1r, b1i = e1
        a2r, a2i, b2r, b2i = e2
        return (a2r * a1r - a2i * a1i, a2r * a1i + a2i * a1r,
                a2r * b1r - a2i * b1i + b2r, a2r * b1i + a2i * b1r + b2i)

    _, _, sr, si = lax.associative_scan(combine, (a_r, a_i, bu_r, bu_i), axis=0)
    y = (jnp.einsum('gcp,lbgp->blgc', c_re.astype(f32), sr)
         - jnp.einsum('gcp,lbgp->blgc', c_im.astype(f32), si)
         + d_skip.astype(f32) * u)
    y = jax.nn.gelu(y.reshape(bsz, l, D_MODEL)).astype(x.dtype)
    val, gate = jnp.split(y @ w_glu + b_glu, 2, axis=-1)
    return val * jax.nn.sigmoid(gate)


def apply_rope(t, cos, sin):
    t1, t2 = jnp.split(t, 2, axis=-1)
    return jnp.concatenate([t1 * cos - t2 * sin, t1 * sin + t2 * cos], axis=-1)


def mla_mixer(x, w_in, q_norm_g, kv_norm_g, w_uq, w_ukv, w_out):
    b, l, _ = x.shape
    h = MLA_HEADS
    f32 = jnp.float32
    c_q, c_kv, k_r = jnp.split(x @ w_in, [MLA_Q_RANK, MLA_Q_RANK + MLA_KV_RANK], axis=-1)
    q = (rms_norm(c_q, q_norm_g) @ w_uq).reshape(b, l, h, MLA_NOPE + MLA_ROPE)
    kv = (rms_norm(c_kv, kv_norm_g) @ w_ukv).reshape(b, l, h, MLA_NOPE + MLA_V)
    q_n, q_r = jnp.split(q, [MLA_NOPE], axis=-1)
    k_n, v = jnp.split(kv, [MLA_NOPE], axis=-1)
    pos = jnp.arange(l, dtype=f32)
    inv_freq = ROPE_BASE ** (-jnp.arange(0, MLA_ROPE, 2, dtype=f32) / MLA_ROPE)
    ang = pos[:, None] * inv_freq[None, :]
    cos, sin = jnp.cos(ang).astype(x.dtype), jnp.sin(ang).astype(x.dtype)
    q_r = apply_rope(q_r, cos[:, None], sin[:, None])
    k_r = apply_rope(k_r, cos, sin)
    scale = (MLA_NOPE + MLA_ROPE) ** -0.5
    q_n, q_r, k_n, k_r, v = (front_pad(t) for t in (q_n, q_r, k_n, k_r, v))
    lp = l + PAD
    nb = lp // BLOCK
    qn_blocks = jnp.moveaxis(q_n.reshape(b, nb, BLOCK, h, MLA_NOPE), 1, 0)
    qr_blocks = jnp.moveaxis(q_r.reshape(b, nb, BLOCK, h, MLA_ROPE), 1, 0)

    def one_block(args):
        qnb, qrb, blk = args
        _, _, valid = block_mask(blk, lp)
        s = (jnp.einsum('bqhd,bkhd->bhqk', qnb, k_n)
             + jnp.einsum('bqhr,bkr->bhqk', qrb, k_r)).astype(f32) * scale
        s = jnp.where(valid, s, NEG_INF)
        p = jax.nn.softmax(s, axis=-1).astype(v.dtype)
        return jnp.einsum('bhqk,bkhe->bqhe', p, v)

    o = lax.map(one_block, (qn_blocks, qr_blocks, jnp.arange(nb)))
    o = jnp.moveaxis(o, 0, 1).reshape(b, lp, h * MLA_V)[:, PAD:]
    return o @ w_out


def setup_inputs(seed: int = 0) -> dict:
    key = jax.random.key(seed)
    ks = iter(jax.random.split(key, 64))
    f32 = jnp.float32

    def nrm(shape, scale):
        return jax.random.normal(next(ks), shape, f32) * scale

    def unif(shape, lo, hi):
        return jax.random.uniform(next(ks), shape, f32, lo, hi)

    D, F = D_MODEL, D_FF
    inp = {}
    inp['x'] = nrm((BATCH, SEQ, D), 1.0)
    inp['meta'] = nrm((N_META, D), 1.0)
    inp['rel_bias'] = nrm((REL_BUCKETS, DIFF_HEADS), 0.5)
    inp['ln_g'] = 1.0 + nrm((DEPTH, 3, D), 0.02)
    inp['ln_b'] = nrm((DEPTH, 3, D), 0.02)
    inp['ffn_w1'] = nrm((DEPTH, 2, D, F), D ** -0.5)
    inp['ffn_w3'] = nrm((DEPTH, 2, D, F), D ** -0.5)
    inp['ffn_w2'] = nrm((DEPTH, 2, F, D), BETA * F ** -0.5)
    inp['ssd_w_in'] = nrm((N_SSD, D, SSD_IN), D ** -0.5)
    inp['ssd_conv_w'] = nrm((N_SSD, SSD_CONV, SSD_CONV_CH), SSD_CONV ** -0.5)
    inp['ssd_conv_b'] = nrm((N_SSD, SSD_CONV_CH), 0.02)
    dt0 = jnp.exp(unif((N_SSD, SSD_HEADS), math.log(1e-3), math.log(1e-1)))
    inp['ssd_dt_bias'] = dt0 + jnp.log(-jnp.expm1(-dt0))
    inp['ssd_a_log'] = jnp.log(unif((N_SSD, SSD_HEADS), 1.0, 16.0))
    inp['ssd_d'] = 1.0 + nrm((N_SSD, SSD_HEADS), 0.02)
    inp['ssd_norm_g'] = 1.0 + nrm((N_SSD, SSD_D_INNER), 0.02)
    inp['ssd_w_out'] = nrm((N_SSD, SSD_D_INNER, D), BETA * SSD_D_INNER ** -0.5)
    inp['diff_w_qkv'] = nrm((N_DIFF, D, 3 * D), D ** -0.5)
    inp['diff_lam_q1'] = nrm((N_DIFF, DIFF_HEAD_DIM), 0.1)
    inp['diff_lam_k1'] = nrm((N_DIFF, DIFF_HEAD_DIM), 0.1)
    inp['diff_lam_q2'] = nrm((N_DIFF, DIFF_HEAD_DIM), 0.1)
    inp['diff_lam_k2'] = nrm((N_DIFF, DIFF_HEAD_DIM), 0.1)
    inp['diff_subln_g'] = 1.0 + nrm((N_DIFF, DIFF_V_DIM), 0.02)
    inp['diff_w_out'] = nrm((N_DIFF, D, D), BETA * D ** -0.5)
    inp['s5_lam_re'] = -0.5 + nrm((N_S5, S5_GROUPS, S5_STATE), 0.01)
    inp['s5_lam_im'] = jnp.broadcast_to(math.pi * jnp.arange(S5_STATE, dtype=f32),
                                        (N_S5, S5_GROUPS, S5_STATE))
    inp['s5_log_step'] = unif((N_S5, S5_GROUPS), math.log(1e-3), math.log(1e-1))
    inp['s5_b_re'] = nrm((N_S5, S5_GROUPS, S5_STATE, S5_GROUP), (2 * S5_GROUP) ** -0.5)
    inp['s5_b_im'] = nrm((N_S5, S5_GROUPS, S5_STATE, S5_GROUP), (2 * S5_GROUP) ** -0.5)
    inp['s5_c_re'] = nrm((N_S5, S5_GROUPS, S5_GROUP, S5_STATE), S5_STATE ** -0.5)
    inp['s5_c_im'] = nrm((N_S5, S5_GROUPS, S5_GROUP, S5_STATE), S5_STATE ** -0.5)
    inp['s5_d'] = nrm((N_S5, S5_GROUPS, S5_GROUP), 1.0)
    inp['s5_w_glu'] = nrm((N_S5, D, 2 * D), BETA * D ** -0.5)
    inp['s5_b_glu'] = nrm((N_S5, 2 * D), 0.02)
    inp['mla_w_in'] = nrm((N_MLA, D, MLA_IN), D ** -0.5)
    inp['mla_q_norm_g'] = 1.0 + nrm((N_MLA, MLA_Q_RANK), 0.02)
    inp['mla_kv_norm_g'] = 1.0 + nrm((N_MLA, MLA_KV_RANK), 0.02)
    inp['mla_w_uq'] = nrm((N_MLA, MLA_Q_RANK, MLA_HEADS * (MLA_NOPE + MLA_ROPE)), MLA_Q_RANK ** -0.5)
    inp['mla_w_ukv'] = nrm((N_MLA, MLA_KV_RANK, MLA_HEADS * (MLA_NOPE + MLA_V)), MLA_KV_RANK ** -0.5)
    inp['mla_w_out'] = nrm((N_MLA, MLA_HEADS * MLA_V, D), BETA * (MLA_HEADS * MLA_V) ** -0.5)
    return inp


def reference(x, meta, rel_bias, ln_g, ln_b, ffn_w1, ffn_w3, ffn_w2,
              ssd_w_in, ssd_conv_w, ssd_conv_b, ssd_dt_bias, ssd_a_log, ssd_d, ssd_norm_g, ssd_w_out,
              diff_w_qkv, diff_lam_q1, diff_lam_k1, diff_lam_q2, diff_lam_k2, diff_subln_g, diff_w_out,
              s5_lam_re, s5_lam_im, s5_log_step, s5_b_re, s5_b_im, s5_c_re, s5_c_im, s5_d, s5_w_glu, s5_b_glu,
              mla_w_in, mla_q_norm_g, mla_kv_norm_g, mla_w_uq, mla_w_ukv, mla_w_out):
    b = x.shape[0]
    h = jnp.concatenate([jnp.broadcast_to(meta[None].astype(x.dtype), (b,) + meta.shape), x], axis=1)
    for i in range(DEPTH):
        kind, j = i % N_MIXERS, i // N_MIXERS
        h = layer_norm(ALPHA * h + 0.5 * swiglu(h, ffn_w1[i, 0], ffn_w3[i, 0], ffn_w2[i, 0]),
                       ln_g[i, 0], ln_b[i, 0])
        if kind == 0:
            m = mamba2_mixer(h, ssd_w_in[j], ssd_conv_w[j], ssd_conv_b[j], ssd_dt_bias[j],
                             ssd_a_log[j], ssd_d[j], ssd_norm_g[j], ssd_w_out[j])
        elif kind == 1:
            m = diff_attention(h, diff_w_qkv[j], diff_lam_q1[j], diff_lam_k1[j], diff_lam_q2[j],
                               diff_lam_k2[j], diff_subln_g[j], diff_w_out[j], rel_bias,
                               0.8 - 0.6 * math.exp(-0.3 * i))
        elif kind == 2:
            m = s5_mixer(h, s5_lam_re[j], s5_lam_im[j], s5_log_step[j], s5_b_re[j], s5_b_im[j],
                         s5_c_re[j], s5_c_im[j], s5_d[j], s5_w_glu[j], s5_b_glu[j])
        else:
            m = mla_mixer(h, mla_w_in[j], mla_q_norm_g[j], mla_kv_norm_g[j], mla_w_uq[j],
                          mla_w_ukv[j], mla_w_out[j])
        h = layer_norm(ALPHA * h + m.astype(h.dtype), ln_g[i, 1], ln_b[i, 1])
        h = layer_norm(ALPHA * h + 0.5 * swiglu(h, ffn_w1[i, 1], ffn_w3[i, 1], ffn_w2[i, 1]),
                       ln_g[i, 2], ln_b[i, 2])
    return h[:, N_META:]
```

```python
import math
from contextlib import ExitStack
import numpy as np
import ml_dtypes
import concourse.bass as bass
import concourse.mybir as mybir
from concourse.bass_utils import run_bass_kernel_spmd

F32 = mybir.dt.float32
BF16 = mybir.dt.bfloat16
AF = mybir.ActivationFunctionType
ALU = mybir.AluOpType
AX = mybir.AxisListType

D = 1024
DC = 8
FF = 2816
FC = 22
DEPTH = 4
ALPHA = (2.0 * DEPTH) ** 0.25
LN_EPS = 1e-5
RMS_EPS = 1e-6
NCORES = 8
DEBUG_OPS = False
NT = 2064
LSEQ = 4112
TCH = [(0, 512), (512, 512), (1024, 512), (1536, 512), (2048, 16)]


class Prog:
    ENGS = ("sp", "pe", "act", "dve", "pool")

    ARENA_F32 = 53000

    def __init__(self, nc, arena=False):
        self.nc = nc
        self.ops = []
        self.last_w = {}
        self.readers = {}
        self.es = ExitStack()
        self.ncnt = 0
        self.barriers = []
        self.arena = None
        if arena:
            self.arena = self.es.enter_context(nc.sbuf_tensor("arena", [128, self.ARENA_F32], F32))
            self.banks = [self.es.enter_context(nc.psum_tensor(f"bank{i}", [128, 512], F32)) for i in range(8)]
            self.aoff = 0
            self.nbank = 0

    def phase_begin(self):
        if DEBUG_OPS:
            print("phase end: arena bytes", self.aoff, "banks", self.nbank, "ops", len(self.ops))
        self.barriers.append(len(self.ops))
        self.last_w = {}
        self.readers = {}
        self.aoff = 0
        self.nbank = 0

    def sbuf(self, shape, dt, name=None):
        self.ncnt += 1
        if self.arena is None:
            return self.es.enter_context(self.nc.sbuf_tensor(name or f"sb{self.ncnt}", list(shape), dt))
        esz = 2 if dt == BF16 else 4
        n = 1
        for s in shape[1:]:
            n *= s
        nbytes = ((n * esz + 31) // 32) * 32
        w0 = self.aoff // 4
        self.aoff += nbytes
        assert self.aoff <= self.ARENA_F32 * 4, f"arena overflow allocating {name} {shape}: {self.aoff}"
        v = self.arena[0:shape[0], w0:w0 + nbytes // 4]
        if dt != F32:
            v = v.bitcast(dt)
        v = v[:, 0:n]
        if len(shape) == 3:
            v = v.rearrange("p (a b) -> p a b", b=shape[2])
        elif len(shape) == 4:
            v = v.rearrange("p (a b c) -> p a b c", b=shape[2], c=shape[3])
        return v

    def psum(self, shape, dt, name=None):
        self.ncnt += 1
        if self.arena is None:
            return self.es.enter_context(self.nc.psum_tensor(name or f"ps{self.ncnt}", list(shape), dt))
        assert dt == F32 and self.nbank < 8
        b = self.banks[self.nbank]
        self.nbank += 1
        return b[0:shape[0], 0:shape[1]]

    def cc_allgather(self, in_ap, out_ap, reads=(), writes=()):
        rg = [[0, 1], [2, 3], [4, 5], [6, 7]]
        return self.op("pool", lambda e: e.collective_compute("AllGather", ALU.bypass, replica_groups=rg,
                                                              ins=[in_ap.opt()], outs=[out_ap.opt()]),
                       reads, writes, group="__cc__")

    def op(self, eng, fn, reads=(), writes=(), group=None):
        i = len(self.ops)
        deps = set()
        for k in reads:
            w = self.last_w.get(k)
            if w is not None:
                deps.add(w)
        for k in writes:
            w = self.last_w.get(k)
            if w is not None:
                deps.add(w)
            for r in self.readers.get(k, ()):
                deps.add(r)
        for k in reads:
            self.readers.setdefault(k, []).append(i)
        for k in writes:
            self.last_w[k] = i
            self.readers[k] = []
        deps.discard(i)
        self.ops.append(dict(eng=eng, fn=fn, deps=sorted(deps), group=group))
        if DEBUG_OPS:
            import traceback
            self.ops[-1]["where"] = "".join(traceback.format_stack(limit=5)[:-1])
        return i

    def dma(self, eng, out, in_, reads=(), writes=(), group=None, slow=False):
        assert group is not None
        if slow:
            return self.op(eng, lambda e: e.dma_start(out=out, in_=in_, allow_slow_non_contiguous=True), reads, writes, group)
        return self.op(eng, lambda e: e.dma_start(out=out, in_=in_), reads, writes, group)


    def mm(self, out, lhsT, rhs, start, stop, reads=(), writes=()):
        return self.op("pe", lambda e: e.matmul(out, lhsT, rhs, start=start, stop=stop), reads, writes)

    def tr(self, out, in_, ident, reads=(), writes=()):
        return self.op("pe", lambda e: e.transpose(out, in_, ident), reads, writes)

    def act(self, out, in_, func, bias=None, scale=None, reads=(), writes=(), accum_out=None):
        kw = {}
        if bias is not None:
            kw["bias"] = bias
        if scale is not None:
            kw["scale"] = scale
        if accum_out is not None:
            kw["accum_out"] = accum_out
        return self.op("act", lambda e: e.activation(out, in_, func, **kw), reads, writes)

    def tt(self, eng, out, in0, in1, op, reads=(), writes=()):
        return self.op(eng, lambda e: e.tensor_tensor(out, in0, in1, op), reads, writes)

    def ts(self, eng, out, in0, s1, s2, op0, op1=None, reads=(), writes=(), accum_out=None):
        def f(e):
            kw = {}
            if accum_out is not None:
                kw["accum_out"] = accum_out
            if op1 is None:
                return e.tensor_scalar(out, in0, s1, None, op0, **kw)
            return e.tensor_scalar(out, in0, s1, s2, op0, op1, **kw)
        return self.op(eng, f, reads, writes)

    def stt(self, eng, out, in0, scalar, in1, op0, op1, reads=(), writes=()):
        return self.op(eng, lambda e: e.scalar_tensor_tensor(out, in0, scalar, in1, op0, op1), reads, writes)

    def copy(self, eng, out, in_, reads=(), writes=()):
        if eng == "act":
            return self.op("act", lambda e: e.copy(out, in_), reads, writes)
        return self.op(eng, lambda e: e.tensor_copy(out, in_), reads, writes)

    def memset(self, eng, ap, val, writes=()):
        return self.op(eng, lambda e: e.memset(ap, val), (), writes)

    NDS = 40

    def emit(self):
        nc = self.nc
        ops = self.ops
        needed = [False] * len(ops)

        def skip(p, o):
            return (p["group"] is None and o["group"] is None and p["eng"] == "pe" and o["eng"] == "pe")

        for o in ops:
            for d in o["deps"]:
                if not skip(ops[d], o):
                    needed[d] = True
        for b in self.barriers:
            for e in self.ENGS:
                for i in range(b - 1, -1, -1):
                    if ops[i]["eng"] == e and ops[i]["group"] is None:
                        needed[i] = True
                        break
        ecount = {e: 0 for e in self.ENGS}
        ndma = 0
        ncc = 0
        for i, o in enumerate(ops):
            o["idx"] = i
            if o["group"] == "__cc__":
                ncc += 1
                o["sem"] = ("cc",)
                o["val"] = ncc
            elif o["group"] is not None:
                o["sem"] = ("d", ndma % self.NDS)
                o["val"] = 16 * (ndma // self.NDS + 1)
                o["dn"] = ndma
                ndma += 1
            else:
                o["sem"] = ("e", o["eng"])
                if needed[i]:
                    ecount[o["eng"]] += 1
                    o["val"] = ecount[o["eng"]]
                else:
                    o["val"] = None
        sems = {}
        for e in self.ENGS:
            sems[("e", e)] = self.es.enter_context(nc.semaphore(f"sem_{e}"))
        for k in range(min(self.NDS, max(ndma, 1))):
            sems[("d", k)] = self.es.enter_context(nc.semaphore(f"semd_{k}"))
        sems[("cc",)] = self.es.enter_context(nc.semaphore("sem_cc"))
        bvals = []
        for b in self.barriers:
            vals = {}
            for o in ops[:b]:
                if o["val"] is not None:
                    vals[o["sem"]] = max(vals.get(o["sem"], 0), o["val"])
            bvals.append(vals)
        per = {e: [o for o in ops if o["eng"] == e] for e in self.ENGS}
        final = {}
        for o in ops:
            if o["group"] is not None:
                final[o["sem"]] = o["val"]
        final.pop(("cc",), None) if ncc == 0 else None

        def run(engobj, ename):
            waited = {}
            nb = 0
            for o in per[ename]:
                while nb < len(self.barriers) and o["idx"] >= self.barriers[nb]:
                    for key, val in bvals[nb].items():
                        if key == ("e", "pe") and ename == "pe":
                            continue
                        if waited.get(key, 0) < val:
                            engobj.wait_ge(sems[key], val)
                            waited[key] = val
                    nb += 1
                for d in o["deps"]:
                    p = ops[d]
                    if skip(p, o):
                        continue
                    key, val = p["sem"], p["val"]
                    if waited.get(key, 0) < val:
                        engobj.wait_ge(sems[key], val)
                        waited[key] = val
                if o["group"] is not None and o["group"] != "__cc__" and o["dn"] >= self.NDS:
                    key, val = o["sem"], o["val"] - 16
                    if waited.get(key, 0) < val:
                        engobj.wait_ge(sems[key], val)
                        waited[key] = val
                try:
                    ins = o["fn"](engobj)
                except Exception:
                    print("FAILED OP", o.get("idx"), o["eng"], o.get("where", ""))
                    raise
                if o["group"] == "__cc__":
                    ins.then_inc(sems[o["sem"]])
                elif o["group"] is not None:
                    ins.then_inc(sems[o["sem"]], 16)
                elif o["val"] is not None:
                    ins.then_inc(sems[o["sem"]], 1)
            if ename == "sp":
                for key, v in final.items():
                    if waited.get(key, 0) < v:
                        engobj.wait_ge(sems[key], v)

        with nc.Block() as block:
            @block.sync
            def _(e):
                run(e, "sp")

            @block.tensor
            def _(e):
                run(e, "pe")

            @block.scalar
            def _(e):
                run(e, "act")

            @block.vector
            def _(e):
                run(e, "dve")

            @block.gpsimd
            def _(e):
                run(e, "pool")
        self.es.close()


def bcast_rows(ap, n):
    return ap.partition_broadcast(n)


class TPhase:
    def __init__(self, P, nc):
        self.P = P
        self.nc = nc
        self.h32 = P.sbuf([128, DC, NT], F32, "h32")
        self.hbf = P.sbuf([128, DC, NT], BF16, "hbf")
        self.wbuf = P.sbuf([128, 16384], BF16, "wbuf")
        self.gbuf = P.sbuf([128, 12384], BF16, "gbuf")
        self.w13s = [P.sbuf([128, 2, DC, 128], F32, f"w13s{i}") for i in range(2)]
        self.w13b = [P.sbuf([128, 2, DC, 128], BF16, f"w13b{i}") for i in range(2)]
        self.w2s = [P.sbuf([128, 1024], F32, f"w2s{i}") for i in range(2)]
        self.sq = [P.sbuf([128, 512], F32, f"sq{i}") for i in range(2)]
        self.t1 = [P.sbuf([128, 512], F32, f"t1_{i}") for i in range(2)]
        self.sg = [P.sbuf([128, 512], F32, f"sg{i}") for i in range(2)]
        self.st_m = P.sbuf([128, 512], F32, "st_m")
        self.st_r = P.sbuf([128, 512], F32, "st_r")
        self.st_v = P.sbuf([128, 512], F32, "st_v")
        self.ones = P.sbuf([128, 128], F32, "ones")
        self.lng = P.sbuf([128, 3 * DC], F32, "lng")
        self.lnb = P.sbuf([128, 3 * DC], F32, "lnb")
        self.bglu = P.sbuf([128, 16], F32, "bglu")
        self.ps_a = [P.psum([128, 512], F32, f"ps_a{i}") for i in range(2)]
        self.ps_b = [P.psum([128, 512], F32, f"ps_b{i}") for i in range(2)]
        self.ps_o = [P.psum([128, 512], F32, f"ps_o{i}") for i in range(2)]
        self.ps_m = P.psum([128, 512], F32, "ps_m")
        self.ps_q = P.psum([128, 512], F32, "ps_q")
        self.cnt = 0
        self.c_w13 = 0
        self.c_up = 0
        P.memset("pool", self.ones[:], 1.0 / D, writes=[("ones",)])

    def nxt(self):
        self.cnt += 1
        return self.cnt - 1

    def load_h(self, hT_dram):
        P = self.P
        for c in range(DC):
            P.dma("sp", self.h32[:, c, :], hT_dram[c * 128:(c + 1) * 128, :],
                  writes=[("h32", c, t) for t in range(5)], group=f"hload{c % 4}")
            for ti, (t0, tn) in enumerate(TCH):
                eng = "act" if (ti % 2 == 0) else "dve"
                P.copy(eng, self.hbf[:, c, t0:t0 + tn], self.h32[:, c, t0:t0 + tn],
                       reads=[("h32", c, ti)], writes=[("hbf", c, ti)])

    def load_ln(self, g_dram, b_dram, nsets):
        P = self.P
        for s in range(nsets):
            P.dma("sp", self.lng[:, s * DC:(s + 1) * DC], g_dram[s, :].rearrange("(c p) -> p c", p=128),
                  writes=[("lng",)], group="const", slow=True)
            P.dma("sp", self.lnb[:, s * DC:(s + 1) * DC], b_dram[s, :].rearrange("(c p) -> p c", p=128),
                  writes=[("lnb",)], group="const", slow=True)

    def scale_h(self, alpha):
        P = self.P
        for c in range(DC):
            for ti, (t0, tn) in enumerate(TCH):
                P.ts("pool", self.h32[:, c, t0:t0 + tn], self.h32[:, c, t0:t0 + tn], float(alpha), None, ALU.mult,
                     reads=[("h32", c, ti)], writes=[("h32", c, ti)])

    def layer_norm(self, lnset):
        P = self.P
        for ti, (t0, tn) in enumerate(TCH):
            for c in range(DC):
                k = self.nxt() % 2
                sq = self.sq[k]
                P.act(sq[:, 0:tn], self.h32[:, c, t0:t0 + tn], AF.Square,
                      reads=[("h32", c, ti)], writes=[("sq", k)])
                P.mm(self.ps_m[:, 0:tn], self.ones[:], self.h32[:, c, t0:t0 + tn], c == 0, c == DC - 1,
                     reads=[("h32", c, ti), ("ones",)], writes=[("ps_m",)])
                P.mm(self.ps_q[:, 0:tn], self.ones[:], sq[:, 0:tn], c == 0, c == DC - 1,
                     reads=[("sq", k), ("ones",)], writes=[("ps_q",)])
            P.copy("act", self.st_m[:, 0:tn], self.ps_m[:, 0:tn], reads=[("ps_m",)], writes=[("st_m",)])
            P.tt("dve", self.st_v[:, 0:tn], self.st_m[:, 0:tn], self.st_m[:, 0:tn], ALU.mult,
                 reads=[("st_m",)], writes=[("st_v",)])
            P.tt("dve", self.st_v[:, 0:tn], self.ps_q[:, 0:tn], self.st_v[:, 0:tn], ALU.subtract,
                 reads=[("ps_q",), ("st_v",)], writes=[("st_v",)])
            P.ts("dve", self.st_v[:, 0:tn], self.st_v[:, 0:tn], LN_EPS, None, ALU.add,
                 reads=[("st_v",)], writes=[("st_v",)])
            P.act(self.st_v[:, 0:tn], self.st_v[:, 0:tn], AF.Sqrt, reads=[("st_v",)], writes=[("st_v",)])
            P.op("dve", lambda e, o=self.st_r[:, 0:tn], i=self.st_v[:, 0:tn]: e.reciprocal(o, i),
                 reads=[("st_v",)], writes=[("st_r",)])
            for c in range(DC):
                k = self.nxt() % 2
                t1 = self.t1[k]
                P.tt("dve", t1[:, 0:tn], self.h32[:, c, t0:t0 + tn], self.st_m[:, 0:tn], ALU.subtract,
                     reads=[("h32", c, ti), ("st_m",)], writes=[("t1", k)])
                P.tt("dve", t1[:, 0:tn], t1[:, 0:tn], self.st_r[:, 0:tn], ALU.mult,
                     reads=[("t1", k), ("st_r",)], writes=[("t1", k)])
                col = lnset * DC + c
                P.act(self.h32[:, c, t0:t0 + tn], t1[:, 0:tn], AF.Identity,
                      bias=self.lnb[:, col:col + 1], scale=self.lng[:, col:col + 1],
                      reads=[("t1", k), ("lng",), ("lnb",)], writes=[("h32", c, ti)])
                P.act(self.hbf[:, c, t0:t0 + tn], t1[:, 0:tn], AF.Identity,
                      bias=self.lnb[:, col:col + 1], scale=self.lng[:, col:col + 1],
                      reads=[("t1", k), ("lng",), ("lnb",)], writes=[("hbf", c, ti)])

    def ffn(self, w1, w3, w2):
        P = self.P
        groups = [(0, 6), (6, 6), (12, 5), (17, 5)]
        GSTR = 2064
        for gi, (f0, nf) in enumerate(groups):
            wsl = gi % 2
            for j in range(nf):
                fc = f0 + j
                k = self.nxt() % 2
                P.dma("sp", self.w2s[k][:], w2[fc * 128:(fc + 1) * 128, :],
                      writes=[("w2s", k)], group=f"w2s{k}")
                off = wsl * 8192 + j * 1024
                P.copy("act", self.wbuf[:, off:off + 1024], self.w2s[k][:],
                       reads=[("w2s", k)], writes=[("wbuf", wsl, j)])
            for j in range(nf):
                fc = f0 + j
                s = self.c_w13 % 2
                self.c_w13 += 1
                P.dma("sp", self.w13s[s][:, 0, :, :], w1[:, fc * 128:(fc + 1) * 128].rearrange("(c p) f -> p c f", p=128),
                      writes=[("w13s", s, 0)], group=f"w13s{s}a")
                P.dma("sp", self.w13s[s][:, 1, :, :], w3[:, fc * 128:(fc + 1) * 128].rearrange("(c p) f -> p c f", p=128),
                      writes=[("w13s", s, 1)], group=f"w13s{s}b")
                P.copy("act", self.w13b[s][:, 0, :, :], self.w13s[s][:, 0, :, :],
                       reads=[("w13s", s, 0)], writes=[("w13b", s, 0)])
                P.copy("pool", self.w13b[s][:, 1, :, :], self.w13s[s][:, 1, :, :],
                       reads=[("w13s", s, 1)], writes=[("w13b", s, 1)])
                for ti, (t0, tn) in enumerate(TCH):
                    kk = self.c_up % 2
                    self.c_up += 1
                    pa, pb, sg = self.ps_a[kk], self.ps_b[kk], self.sg[kk]
                    for c in range(DC):
                        P.mm(pa[:, 0:tn], self.w13b[s][:, 0, c, :], self.hbf[:, c, t0:t0 + tn], c == 0, c == DC - 1,
                             reads=[("w13b", s, 0), ("hbf", c, ti)], writes=[("ps_a", kk)])
                    for c in range(DC):
                        P.mm(pb[:, 0:tn], self.w13b[s][:, 1, c, :], self.hbf[:, c, t0:t0 + tn], c == 0, c == DC - 1,
                             reads=[("w13b", s, 1), ("hbf", c, ti)], writes=[("ps_b", kk)])
                    P.act(sg[:, 0:tn], pa[:, 0:tn], AF.Silu, reads=[("ps_a", kk)], writes=[("sg", kk)])
                    goff = j * GSTR + t0
                    P.stt("dve", self.gbuf[:, goff:goff + tn], sg[:, 0:tn], 0.5, pb[:, 0:tn], ALU.mult, ALU.mult,
                          reads=[("sg", kk), ("ps_b", kk)], writes=[("gbuf", j, ti)])
            for ti, (t0, tn) in enumerate(TCH):
                for c in range(DC):
                    kk = self.nxt() % 2
                    po = self.ps_o[kk]
                    for j in range(nf):
                        off = wsl * 8192 + j * 1024 + c * 128
                        goff = j * GSTR + t0
                        P.mm(po[:, 0:tn], self.wbuf[:, off:off + 128], self.gbuf[:, goff:goff + tn], j == 0, j == nf - 1,
                             reads=[("wbuf", wsl, j), ("gbuf", j, ti)], writes=[("ps_o", kk)])
                    P.stt("dve", self.h32[:, c, t0:t0 + tn], self.h32[:, c, t0:t0 + tn], float(ALPHA) if gi == 0 else 1.0, po[:, 0:tn], ALU.mult, ALU.add,
                          reads=[("ps_o", kk), ("h32", c, ti)], writes=[("h32", c, ti)])

    selv = None

    def load_o(self, oT_dram, j, t0, tn):
        P = self.P
        gk = [("gbuf", jj, tt_) for jj in range(6) for tt_ in range(5)] + [("gbo", j)]
        if self.selv is None:
            P.dma("sp", self.gbuf[:, j * 512:j * 512 + tn], oT_dram[j * 128:(j + 1) * 128, t0:t0 + tn], writes=gk, group="gbo")
            return
        k = self.nxt() % 2
        stg = self.t1[k][:].bitcast(BF16)
        nh = len(oT_dram)
        srcj = oT_dram[j % nh][(j // nh) * 128:(j // nh) * 128 + 128, :]
        P.dma("sp", stg[:, 0:tn], srcj[:, t0:t0 + tn], reads=[("oall", j % nh)], writes=[("t1", k)], group="gbo")
        P.dma("sp", stg[:, 512:512 + tn], srcj[:, 2048 + t0:2048 + t0 + tn], reads=[("oall", j % nh)], writes=[("t1B", k)], group="gbo")
        P.ts("dve", stg[:, 0:tn], stg[:, 0:tn], self.selv[:, 0:1], None, ALU.mult, reads=[("t1", k), ("selv",)], writes=[("t1", k)])
        P.stt("dve", self.gbuf[:, j * 512:j * 512 + tn], stg[:, 512:512 + tn], self.selv[:, 1:2], stg[:, 0:tn], ALU.mult, ALU.add,
              reads=[("t1", k), ("t1B", k), ("selv",)], writes=gk)

    def outproj(self, oT_dram, w_out, kc):
        P = self.P
        for j in range(kc):
            k = self.nxt() % 2
            P.dma("sp", self.w2s[k][:], w_out[j * 128:(j + 1) * 128, :], writes=[("w2s", k)], group=f"w2s{k}")
            P.copy("act", self.wbuf[:, j * 1024:(j + 1) * 1024], self.w2s[k][:], reads=[("w2s", k)],
                   writes=[("wbuf", 0, jj) for jj in range(6)] + [("wbuf", 1, jj) for jj in range(6)] + [("wbo", j)])
        for ti, (t0, tn) in enumerate(TCH):
            for j in range(kc):
                self.load_o(oT_dram, j, t0, tn)
            for c in range(DC):
                kk = self.nxt() % 2
                po = self.ps_o[kk]
                for j in range(kc):
                    P.mm(po[:, 0:tn], self.wbuf[:, j * 1024 + c * 128:j * 1024 + (c + 1) * 128], self.gbuf[:, j * 512:j * 512 + tn],
                         j == 0, j == kc - 1, reads=[("wbo", j), ("gbo", j)], writes=[("ps_o", kk)])
                P.stt("dve", self.h32[:, c, t0:t0 + tn], self.h32[:, c, t0:t0 + tn], float(ALPHA), po[:, 0:tn], ALU.mult, ALU.add,
                      reads=[("ps_o", kk), ("h32", c, ti)], writes=[("h32", c, ti)])
            P.op("pool", lambda e, a=self.sq[0][0:1, 0:1]: e.memset(a, 0.0),
                 reads=[("gbo", j) for j in range(kc)], writes=[("gbuf", jj, tt_) for jj in range(6) for tt_ in range(5)] + [("sq", 0)])
        P.op("pool", lambda e, a=self.sq[0][0:1, 0:1]: e.memset(a, 0.0),
             reads=[("wbo", j) for j in range(kc)], writes=[("wbuf", 0, jj) for jj in range(6)] + [("wbuf", 1, jj) for jj in range(6)] + [("sq", 0)])

    def glu(self, oT_dram, w_glu, b_glu):
        P = self.P
        P.dma("sp", self.bglu[:], b_glu.rearrange("(c p) -> p c", p=128), writes=[("bglu",)], group="const", slow=True)
        wv = self.wbuf[:].rearrange("p (k f) -> p k f", f=2048)
        for j in range(8):
            for hf in range(2):
                k = self.nxt() % 2
                P.dma("sp", self.w2s[k][:], w_glu[j * 128:(j + 1) * 128, hf * 1024:(hf + 1) * 1024], writes=[("w2s", k)], group=f"w2s{k}")
                P.copy("act", wv[:, j, hf * 1024:(hf + 1) * 1024], self.w2s[k][:], reads=[("w2s", k)],
                       writes=[("wbuf", 0, jj) for jj in range(6)] + [("wbuf", 1, jj) for jj in range(6)] + [("wbo", j, hf)])
        for ti, (t0, tn) in enumerate(TCH):
            for j in range(8):
                self.load_o(oT_dram, j, t0, tn)
            for c in range(DC):
                kk = self.nxt() % 2
                pa, pb, sg, t1 = self.ps_a[kk], self.ps_b[kk], self.sg[kk], self.t1[kk]
                for j in range(8):
                    P.mm(pa[:, 0:tn], wv[:, j, c * 128:(c + 1) * 128], self.gbuf[:, j * 512:j * 512 + tn], j == 0, j == 7,
                         reads=[("wbo", j, 0), ("gbo", j)], writes=[("ps_a", kk)])
                for j in range(8):
                    P.mm(pb[:, 0:tn], wv[:, j, 1024 + c * 128:1024 + (c + 1) * 128], self.gbuf[:, j * 512:j * 512 + tn], j == 0, j == 7,
                         reads=[("wbo", j, 1), ("gbo", j)], writes=[("ps_b", kk)])
                P.act(sg[:, 0:tn], pb[:, 0:tn], AF.Sigmoid, bias=self.bglu[:, 8 + c:9 + c], reads=[("ps_b", kk), ("bglu",)], writes=[("sg", kk)])
                P.stt("dve", t1[:, 0:tn], pa[:, 0:tn], self.bglu[:, c:c + 1], sg[:, 0:tn], ALU.add, ALU.mult,
                      reads=[("ps_a", kk), ("sg", kk), ("bglu",)], writes=[("t1", kk)])
                P.stt("dve", self.h32[:, c, t0:t0 + tn], self.h32[:, c, t0:t0 + tn], float(ALPHA), t1[:, 0:tn], ALU.mult, ALU.add,
                      reads=[("t1", kk), ("h32", c, ti)], writes=[("h32", c, ti)])
            P.op("pool", lambda e, a=self.sq[0][0:1, 0:1]: e.memset(a, 0.0),
                 reads=[("gbo", j) for j in range(8)], writes=[("gbuf", jj, tt_) for jj in range(6) for tt_ in range(5)] + [("sq", 0)])
        P.op("pool", lambda e, a=self.sq[0][0:1, 0:1]: e.memset(a, 0.0),
             reads=[("wbo", j, hf) for j in range(8) for hf in range(2)],
             writes=[("wbuf", 0, jj) for jj in range(6)] + [("wbuf", 1, jj) for jj in range(6)] + [("sq", 0)])

    def store_h(self, out32, outbf):
        P = self.P
        for c in range(DC):
            if out32 is not None:
                P.dma("sp", out32[c * 128:(c + 1) * 128, :], self.h32[:, c, :],
                      reads=[("h32", c, t) for t in range(5)], group="out")
            if outbf is not None:
                dst = outbf[c // 2][(c % 2) * 128:(c % 2) * 128 + 128, :] if isinstance(outbf, list) else outbf[c * 128:(c + 1) * 128, :]
                P.dma("sp", dst, self.hbf[:, c, :],
                      reads=[("hbf", c, t) for t in range(5)], writes=[("xin", c)], group="out")


def build_t0():
    nc = bass.Bass("TRN2", target_bir_lowering=False)
    hT = nc.dram_tensor("hT", [D, NT], F32, kind="ExternalInput").ap()
    w1 = nc.dram_tensor("w1", [D, FF], F32, kind="ExternalInput").ap()
    w3 = nc.dram_tensor("w3", [D, FF], F32, kind="ExternalInput").ap()
    w2 = nc.dram_tensor("w2", [FF, D], F32, kind="ExternalInput").ap()
    lg = nc.dram_tensor("ln_g", [1, D], F32, kind="ExternalInput").ap()
    lb = nc.dram_tensor("ln_b", [1, D], F32, kind="ExternalInput").ap()
    o32 = nc.dram_tensor("o32", [D, NT], F32, kind="ExternalOutput").ap()
    obf = nc.dram_tensor("obf", [D, NT], BF16, kind="ExternalOutput").ap()
    P = Prog(nc)
    T = TPhase(P, nc)
    T.load_ln(lg, lb, 1)
    T.load_h(hT)
    T.ffn(w1, w3, w2)
    T.layer_norm(0)
    T.store_h(o32, obf)
    P.emit()
    return nc


QCH = [(0, 16)] + [(16 + 512 * i, 512) for i in range(8)]
KBL = [(0, 16)] + [(16 + 128 * i, 128) for i in range(32)]
NEG = -30000.0


def load_cast_w(P, dst_bf, src_dram, stg, rows_chunks, cols, tagkey, eng="act", colblk=1024):
    k = 0
    for c in range(rows_chunks):
        for c0 in range(0, cols, colblk):
            cn = min(colblk, cols - c0)
            s = k % len(stg)
            k += 1
            P.dma("sp", stg[s][:, 0:cn], src_dram[c * 128:(c + 1) * 128, c0:c0 + cn],
                  writes=[("stg", id(stg[s]))], group=f"stg{id(stg[s])}")
            P.copy(eng, dst_bf[:, c, c0:c0 + cn], stg[s][:, 0:cn],
                   reads=[("stg", id(stg[s]))], writes=[tagkey])


class Ctx:
    def __init__(self, nc, P, pre, xall=None, oin=None, selv=None):
        self.nc, self.P, self.pre, self.xall, self.oin, self.selv = nc, P, pre, xall, oin, selv

    def x_rows(self, rank, c):
        k, w = c // 2, (c % 2) * 128
        return self.xall[k][rank * 256 + w:rank * 256 + w + 128, :]


class ORows:
    def __init__(self, oT, chunks=None):
        self.oT, self.chunks = oT, chunks

    def rows(self, r0, n):
        if self.chunks is None:
            return self.oT[r0:r0 + n, :]
        j, w = r0 // 128, r0 % 128
        assert w + n <= 128
        return self.chunks[j][w:w + n, :]


def _load_seq(P, ctx, dst, nch, hbT, keyfn):
    for c in range(nch):
        if ctx is None:
            P.dma("sp", dst[:, c, :], hbT[c * 128:(c + 1) * 128, :], writes=keyfn(c, 0) + keyfn(c, 1), group="hb")
        else:
            P.dma("sp", dst[:, c, 0:NT], ctx.x_rows(0, c), reads=[("xall", c // 2)], writes=keyfn(c, 0), group="hb")
            P.dma("sp", dst[:, c, NT:LSEQ], ctx.x_rows(1, c)[:, 16:NT], reads=[("xall", c // 2)], writes=keyfn(c, 1), group="hb")


class AttnCore:
    def __init__(self, P, nmaps, E, s_depth=2):
        self.P = P
        self.nmaps = nmaps
        self.E = E
        self.sd = s_depth
        self.psS = [[P.psum([128, 512], F32, f"psS{m}_{i}") for i in range(s_depth)] for m in range(nmaps)]
        self.po = [P.psum([128, 512], F32, f"po{m}") for m in range(nmaps)]
        self.pn = [P.psum([128, 512], F32, f"pn{m}") for m in range(nmaps)]
        self.PT = [[P.sbuf([128, 512], BF16, f"PT{m}_{i}") for i in range(4)] for m in range(nmaps)]
        self.tmp = [[P.sbuf([128, 256], F32, f"atmp{m}_{i}") for i in range(2)] for m in range(nmaps)]
        self.onesb = P.sbuf([128, 128], BF16, "onesb")
        P.memset("pool", self.onesb[:], 1.0, writes=[("onesb",)])
        self.kS = [0] * nmaps
        self.kP = [0] * nmaps
        self.kT = [0] * nmaps

    def chunk(self, c, kT, qT, kd_base, kd, V, vkeys, sc, tiles, far_bias, keysK, keysQ, keysV, pv_extra=None):
        P = self.P
        E = self.E
        qs, nq = QCH[c]
        if c == 0:
            kbs = [0]
        else:
            kbs = list(range(0, 4 * c + 1))
        qb0 = 4 * (c - 1) + 1
        work = []
        for kb in kbs:
            ks, nk = KBL[kb]
            last = (kb == kbs[-1])
            specials = []
            if c == 0:
                cs = 0
                specials.append((0, 16, tiles["md"][0:16, 0:16]))
                cf = 16
            elif kb == 0:
                cs = 0
                cf = 0
                if c == 1 and tiles.get("mq1") is not None:
                    specials.append((0, 128, tiles["mq1"][0:16, 0:128]))
                    cf = 128
            else:
                i0 = max(0, kb - qb0)
                cs = 128 * i0
                d0 = qb0 + i0 - kb
                cf = cs
                if d0 == 0:
                    w = 128
                    if tiles["prev"] and cs + 256 <= nq:
                        w = 256
                    specials.append((cs, w, tiles["dd"][:, 0:w]))
                    cf = cs + w
                elif d0 == 1 and tiles["prev"]:
                    specials.append((cs, 128, tiles["dd"][:, 128:256]))
                    cf = cs + 128
            for m in range(self.nmaps):
                bS = self.kS[m] % self.sd
                self.kS[m] += 1
                bP = self.kP[m] % 4
                self.kP[m] += 1
                ps = self.psS[m][bS]
                PT = self.PT[m][bP]
                pkey = ("psS", m, bS)
                tkey = ("PT", m, bP)

                def emit_S(kb=kb, m=m, ps=ps, pkey=pkey, nk=nk, cs=cs):
                    P.mm(ps[0:nk, cs:nq], kT(m, kb), qT(m)[:, cs:nq], True, True,
                         reads=keysK(m, kb) + keysQ(m), writes=[pkey])

                def emit_post(kb=kb, m=m, ps=ps, PT=PT, pkey=pkey, tkey=tkey, nk=nk, cs=cs, cf=cf, specials=specials, last=last):
                    for (c0, w, tap) in specials:
                        bT = self.kT[m] % 2
                        self.kT[m] += 1
                        t = self.tmp[m][bT]
                        P.stt("dve", t[0:nk, 0:w], ps[0:nk, c0:c0 + w], float(sc), tap, ALU.mult, ALU.add,
                              reads=[pkey, ("tiles",)], writes=[("atmp", m, bT)])
                        P.act(PT[0:nk, c0:c0 + w], t[0:nk, 0:w], AF.Exp, reads=[("atmp", m, bT)], writes=[tkey])
                    if cf < nq:
                        fb = far_bias[m] if isinstance(far_bias, (list, tuple)) else far_bias
                        if fb is not None:
                            P.act(PT[0:nk, cf:nq], ps[0:nk, cf:nq], AF.Exp, bias=fb[0:nk, :], scale=float(sc),
                                  reads=[pkey, ("tiles",)], writes=[tkey])
                        else:
                            P.act(PT[0:nk, cf:nq], ps[0:nk, cf:nq], AF.Exp, scale=float(sc), reads=[pkey], writes=[tkey])
                    P.mm(self.po[m][0:E, cs:nq], V(kb), PT[0:nk, cs:nq], kb == 0, last,
                         reads=[tkey] + keysV(kb), writes=[("po", m)])
                    P.mm(self.pn[m][0:E, cs:nq], self.onesb[0:nk, 0:E], PT[0:nk, cs:nq], kb == 0, last,
                         reads=[tkey, ("onesb",)], writes=[("pn", m)])
                work.append((emit_S, emit_post))
        LA = self.nmaps * (self.sd - 1)
        for i in range(len(work) + LA):
            if i < len(work):
                work[i][0]()
            if i - LA >= 0:
                work[i - LA][1]()


def build_mla(ctx=None):
    nc = bass.Bass("TRN2", target_bir_lowering=False) if ctx is None else ctx.nc
    pre = "" if ctx is None else ctx.pre
    dt = lambda n, s, d=F32: nc.dram_tensor(pre + n, s, d, kind="ExternalInput").ap()
    hbT = dt("hbT", [D, LSEQ], BF16) if ctx is None else None
    w_in = dt("w_in_ext", [D, 832])
    w_uq = dt("w_uq", [384, 768])
    w_uqs = dt("w_uq_sw", [384, 768])
    w_uk = dt("w_uk", [256, 512])
    w_uv = dt("w_uv", [256, 512])
    qg = dt("qg", [384])
    kvg = dt("kvg", [256])
    ropeC = dt("ropeC", [32, LSEQ])
    ropeS = dt("ropeS", [32, LSEQ])
    mask_dd = dt("mask_dd", [128, 128])
    oT = ORows(nc.dram_tensor("oT", [512, LSEQ], BF16, kind="ExternalOutput").ap()) if ctx is None else ORows(None, ctx.oin)
    P = Prog(nc) if ctx is None else ctx.P
    big = P.sbuf([128, 8, LSEQ], BF16, "big")
    cqn = P.sbuf([128, 3, LSEQ], BF16, "cqn")
    ckvn = P.sbuf([128, 2, LSEQ], BF16, "ckvn")
    Vt = P.sbuf([128, 33, 512], BF16, "Vt")
    winb = P.sbuf([128, 8, 832], BF16, "winb")
    wuqb = P.sbuf([128, 3, 768], BF16, "wuqb")
    wuqsb = P.sbuf([128, 3, 768], BF16, "wuqsb")
    wukb = P.sbuf([128, 2, 512], BF16, "wukb")
    wuvb = P.sbuf([128, 2, 512], BF16, "wuvb")
    stg = [P.sbuf([128, 1024], F32, f"stg{i}") for i in range(1)]
    c32 = [P.sbuf([128, 512], F32, f"c32_{i}") for i in range(5)]
    rs = [P.sbuf([128, 512], F32, f"rs{i}") for i in range(1)]
    tC = P.sbuf([128, 512], F32, "tC")
    tS = P.sbuf([128, 512], F32, "tS")
    r1 = P.sbuf([128, 512], F32, "r1")
    r2 = P.sbuf([128, 512], F32, "r2")
    qTt = [P.sbuf([128, 512], BF16, f"qT{i}") for i in range(2)]
    osb = [P.sbuf([128, 512], BF16, f"osb{i}") for i in range(2)]
    rcp = P.sbuf([128, 512], F32, "rcp")
    onesq = P.sbuf([128, 128], F32, "onesq")
    oneskv = P.sbuf([128, 128], F32, "oneskv")
    gq = P.sbuf([128, 3], F32, "gq")
    gkv = P.sbuf([128, 2], F32, "gkv")
    mdd = P.sbuf([128, 128], F32, "mdd")
    A = AttnCore(P, 1, 64, s_depth=3)
    psx = [P.psum([128, 512], F32, f"psx{i}") for i in range(2)]
    pss = P.psum([128, 512], F32, "pss")
    P.memset("pool", onesq[:], 1.0 / 384, writes=[("onesq",)])
    P.memset("pool", oneskv[:], 1.0 / 256, writes=[("oneskv",)])
    P.dma("sp", gq[:], qg.rearrange("(c p) -> p c", p=128), writes=[("gq",)], group="const", slow=True)
    P.dma("sp", gkv[:], kvg.rearrange("(c p) -> p c", p=128), writes=[("gkv",)], group="const", slow=True)
    P.dma("sp", mdd[:], mask_dd, writes=[("tiles",)], group="const")
    _load_seq(P, ctx, big, 8, hbT, lambda c, part: [("big", c, t) for t in (range(0, 5) if part == 0 else range(5, 9))])
    load_cast_w(P, winb, w_in, stg, 8, 832, ("winb",))
    load_cast_w(P, wuqb, w_uq, stg, 3, 768, ("wuqb",))
    load_cast_w(P, wuqsb, w_uqs, stg, 3, 768, ("wuqsb",))
    load_cast_w(P, wukb, w_uk, stg, 2, 512, ("wukb",))
    load_cast_w(P, wuvb, w_uv, stg, 2, 512, ("wuvb",))
    cnt = [0]

    def nxt():
        cnt[0] += 1
        return cnt[0] - 1

    pxc = [0]

    def nxp():
        pxc[0] += 1
        return pxc[0] - 1

    for ti, (t0, tn) in enumerate(QCH):
        for grp, (f0, nf, ones_t, okey, gt, dst) in enumerate([(0, 3, onesq, ("onesq",), gq, cqn), (3, 2, oneskv, ("oneskv",), gkv, ckvn)]):
            for f in range(nf):
                kx = nxp() % 2
                ps = psx[kx]
                for c in range(8):
                    P.mm(ps[:, 0:tn], winb[:, c, (f0 + f) * 128:(f0 + f + 1) * 128], big[:, c, t0:t0 + tn], c == 0, c == 7,
                         reads=[("winb",), ("big", c, ti)], writes=[("psx", kx)])
                cc = c32[f0 + f]
                P.copy("act", cc[:, 0:tn], ps[:, 0:tn], reads=[("psx", kx)], writes=[("c32", f0 + f)])
                ks = nxt() % 2
                sqt = [r1, r2][ks]
                sqk = [("r1",), ("r2",)][ks]
                P.tt("dve", sqt[:, 0:tn], cc[:, 0:tn], cc[:, 0:tn], ALU.mult, reads=[("c32", f0 + f)], writes=[sqk])
                P.mm(pss[:, 0:tn], ones_t[:], sqt[:, 0:tn], f == 0, f == nf - 1,
                     reads=[sqk, okey], writes=[("pss",)])
            kr = 0
            P.ts("dve", rs[kr][:, 0:tn], pss[:, 0:tn], RMS_EPS, None, ALU.add, reads=[("pss",)], writes=[("rs", kr)])
            P.act(rs[kr][:, 0:tn], rs[kr][:, 0:tn], AF.Sqrt, reads=[("rs", kr)], writes=[("rs", kr)])
            P.op("dve", lambda e, o=rs[kr][:, 0:tn]: e.reciprocal(o, o), reads=[("rs", kr)], writes=[("rs", kr)])
            for f in range(nf):
                cc = c32[f0 + f]
                P.tt("dve", cc[:, 0:tn], cc[:, 0:tn], rs[kr][:, 0:tn], ALU.mult, reads=[("c32", f0 + f), ("rs", kr)], writes=[("c32", f0 + f)])
                P.act(dst[:, f, t0:t0 + tn], cc[:, 0:tn], AF.Identity, scale=gt[:, f:f + 1],
                      reads=[("c32", f0 + f), ("gq",), ("gkv",)], writes=[(id(dst), f, ti)])
        P.dma("sp", tC[64:96, 0:tn], ropeC[:, t0:t0 + tn], writes=[("tC",)], group="tC")
        P.dma("sp", tS[64:96, 0:tn], ropeS[:, t0:t0 + tn], writes=[("tS",)], group="tS")
        ka, kb_ = nxp() % 2, None
        psa = psx[ka]
        for c in range(8):
            P.mm(psa[0:96, 0:tn], winb[:, c, 640:736], big[:, c, t0:t0 + tn], c == 0, c == 7,
                 reads=[("winb",), ("big", c, ti)], writes=[("psx", ka)])
        P.tt("dve", r1[64:96, 0:tn], psa[64:96, 0:tn], tC[64:96, 0:tn], ALU.mult, reads=[("psx", ka), ("tC",)], writes=[("r1",)])
        kb_ = nxp() % 2
        psb = psx[kb_]
        for c in range(8):
            P.mm(psb[0:96, 0:tn], winb[:, c, 736:832], big[:, c, t0:t0 + tn], c == 0, c == 7,
                 reads=[("winb",), ("big", c, ti)], writes=[("psx", kb_)])
        P.tt("dve", r2[64:96, 0:tn], psb[64:96, 0:tn], tS[64:96, 0:tn], ALU.mult, reads=[("psx", kb_), ("tS",)], writes=[("r2",)])
        P.tt("dve", r1[64:96, 0:tn], r1[64:96, 0:tn], r2[64:96, 0:tn], ALU.add, reads=[("r1",), ("r2",)], writes=[("r1",)])
        for h in range(8):
            P.copy("pool" if h % 2 else "act", big[64:96, h, t0:t0 + tn], r1[64:96, 0:tn],
                   reads=[("r1",)] + [("big", cc_, ti) for cc_ in range(8)], writes=[("bigr", h, ti)])
    for h in range(8):
        for ti, (t0, tn) in enumerate(QCH):
            kx = nxp() % 2
            ps = psx[kx]
            for kc in range(2):
                P.mm(ps[0:64, 0:tn], wukb[:, kc, h * 64:(h + 1) * 64], ckvn[:, kc, t0:t0 + tn], kc == 0, kc == 1,
                     reads=[("wukb",), (id(ckvn), kc, ti)], writes=[("psx", kx)])
            P.copy("act" if h % 2 else "dve", big[0:64, h, t0:t0 + tn], ps[0:64, 0:tn],
                   reads=[("psx", kx)] + [("big", cc_, ti) for cc_ in range(8)], writes=[("bign", h, ti)])
    for kb, (ks, nk) in enumerate(KBL):
        kx = nxp() % 2
        ps = psx[kx]
        ti = 0 if kb == 0 else 1 + (kb - 1) // 4
        for kc in range(2):
            P.mm(ps[0:nk, 0:512], ckvn[:, kc, ks:ks + nk], wuvb[:, kc, :], kc == 0, kc == 1,
                 reads=[("wuvb",), (id(ckvn), kc, ti)], writes=[("psx", kx)])
        P.copy("act" if kb % 2 else "dve", Vt[0:nk, kb, :], ps[0:nk, 0:512], reads=[("psx", kx)], writes=[("Vt", kb)])
    sc = (64 + 32) ** -0.5
    tiles = dict(dd=mdd, prev=False, md=mdd, mq1=None)
    for c, (qs, nq) in enumerate(QCH):
        P.dma("sp", tC[64:96, 0:nq], ropeC[:, qs:qs + nq], writes=[("tC",)], group="tC")
        P.dma("sp", tS[64:96, 0:nq], ropeS[:, qs:qs + nq], writes=[("tS",)], group="tS")
        for h in range(8):
            ka = nxp() % 2
            psa = psx[ka]
            for kc in range(3):
                P.mm(psa[0:96, 0:nq], wuqb[:, kc, h * 96:(h + 1) * 96], cqn[:, kc, qs:qs + nq], kc == 0, kc == 2,
                     reads=[("wuqb",), (id(cqn), kc, c)], writes=[("psx", ka)])
            kb_ = nxp() % 2
            psb = psx[kb_]
            for kc in range(3):
                P.mm(psb[0:96, 0:nq], wuqsb[:, kc, h * 96:(h + 1) * 96], cqn[:, kc, qs:qs + nq], kc == 0, kc == 2,
                     reads=[("wuqsb",), (id(cqn), kc, c)], writes=[("psx", kb_)])
            qi = nxt() % 2
            qt = qTt[qi]
            P.copy("act", qt[0:64, 0:nq], psa[0:64, 0:nq], reads=[("psx", ka)], writes=[("qT", qi, 0)])
            P.tt("dve", r1[64:96, 0:nq], psa[64:96, 0:nq], tC[64:96, 0:nq], ALU.mult, reads=[("psx", ka), ("tC",)], writes=[("r1",)])
            P.tt("dve", r2[64:96, 0:nq], psb[64:96, 0:nq], tS[64:96, 0:nq], ALU.mult, reads=[("psx", kb_), ("tS",)], writes=[("r2",)])
            P.tt("dve", qt[64:96, 0:nq], r1[64:96, 0:nq], r2[64:96, 0:nq], ALU.add, reads=[("r1",), ("r2",)], writes=[("qT", qi, 1)])
            A.chunk(c,
                    kT=lambda m, kb, h=h: big[0:96, h, KBL[kb][0]:KBL[kb][0] + KBL[kb][1]],
                    qT=lambda m, qt=qt, nq=nq: qt[0:96, 0:nq],
                    kd_base=0, kd=96,
                    V=lambda kb, h=h: Vt[0:KBL[kb][1], kb, h * 64:(h + 1) * 64], vkeys=None,
                    sc=sc, tiles=tiles, far_bias=None,
                    keysK=lambda m, kb, h=h: [("bign", h, 0 if kb == 0 else 1 + (kb - 1) // 4), ("bigr", h, 0 if kb == 0 else 1 + (kb - 1) // 4)],
                    keysQ=lambda m, qi=qi: [("qT", qi, 0), ("qT", qi, 1)],
                    keysV=lambda kb: [("Vt", kb)])
            P.op("dve", lambda e, o=rcp[0:64, 0:nq], i=A.pn[0][0:64, 0:nq]: e.reciprocal(o, i), reads=[("pn", 0)], writes=[("rcp",)])
            oi = nxt() % 2
            P.tt("dve", osb[oi][0:64, 0:nq], A.po[0][0:64, 0:nq], rcp[0:64, 0:nq], ALU.mult,
                 reads=[("po", 0), ("rcp",)], writes=[("osb", oi)])
            P.dma("sp", oT.rows(h * 64, 64)[:, qs:qs + nq], osb[oi][0:64, 0:nq], reads=[("osb", oi)], writes=[("oin", h // 2, h, c)], group=f"oout{oi}")
    if ctx is None:
        P.emit()
        return nc


def build_diff(lam_init, ctx=None):
    nc = bass.Bass("TRN2", target_bir_lowering=False) if ctx is None else ctx.nc
    pre = "" if ctx is None else ctx.pre
    dt = lambda n, s, d=F32: nc.dram_tensor(pre + n, s, d, kind="ExternalInput").ap()
    hbT = dt("hbT", [D, LSEQ], BF16) if ctx is None else None
    w_q = dt("w_q", [D, 512]); w_k = dt("w_k", [D, 512]); w_v = dt("w_v", [D, 512])
    lam4 = dt("lam4", [4, 64]); subg = dt("subg", [128]); rb = dt("rb", [32 * 4])
    m_dd = dt("m_dd", [32, 128, 256]); m_md = dt("m_md", [32, 16, 16]); m_mq1 = dt("m_mq1", [32, 16, 128])
    n_dd = dt("n_dd", [128, 256]); n_md = dt("n_md", [16, 16])
    oT = ORows(nc.dram_tensor("oT", [512, LSEQ], BF16, kind="ExternalOutput").ap()) if ctx is None else ORows(None, ctx.oin)
    P = Prog(nc) if ctx is None else ctx.P
    big = P.sbuf([128, 8, LSEQ], BF16, "big")
    qTt = P.sbuf([128, 4, LSEQ], BF16, "qTt")
    kTt = P.sbuf([128, 4, LSEQ], BF16, "kTt")
    Vt = P.sbuf([128, 33, 512], BF16, "Vt")
    wb1 = P.sbuf([128, 8, 512], BF16, "wb1")
    wb = [wb1, wb1, wb1]
    stg = [P.sbuf([128, 512], F32, "stg0")]
    Tdd = [P.sbuf([128, 256], F32, f"Tdd{h}") for h in range(4)]
    Tmd = [P.sbuf([16, 16], F32, f"Tmd{h}") for h in range(4)]
    Tmq = [P.sbuf([16, 128], F32, f"Tmq{h}") for h in range(4)]
    mk = P.sbuf([128, 256], F32, "mk")
    mk2 = P.sbuf([16, 16], F32, "mk2")
    mk3 = P.sbuf([16, 128], F32, "mk3")
    rbb = P.sbuf([128, 128], F32, "rbb")
    lamb = P.sbuf([128, 4, 64], F32, "lamb")
    lt = P.sbuf([128, 64], F32, "lt")
    l1 = P.sbuf([128, 1], F32, "l1"); l2 = P.sbuf([128, 1], F32, "l2"); nlam = P.sbuf([128, 1], F32, "nlam")
    gs = P.sbuf([128, 1], F32, "gs")
    ones32 = P.sbuf([128, 128], F32, "ones32")
    u = P.sbuf([128, 512], F32, "u"); t_ = P.sbuf([128, 512], F32, "t_"); rc = P.sbuf([128, 512], F32, "rc")
    osb = [P.sbuf([128, 512], BF16, f"osb{i}") for i in range(2)]
    A = AttnCore(P, 2, 128)
    P.memset("pool", ones32[:], 1.0 / 128, writes=[("ones32",)])
    P.dma("sp", rbb[:], rb.partition_broadcast(128), writes=[("rbb",)], group="const")
    P.dma("sp", lamb[:], lam4.rearrange("a b -> (a b)").partition_broadcast(128), writes=[("lamb",)], group="const")
    P.dma("sp", gs[:], subg.rearrange("(p o) -> p o", o=1), writes=[("gs",)], group="const", slow=True)
    P.ts("dve", gs[:], gs[:], float(1.0 - lam_init), None, ALU.mult, reads=[("gs",)], writes=[("gs",)])
    for i, dst in enumerate([l1, l2]):
        P.tt("dve", lt[:], lamb[:, 2 * i, :], lamb[:, 2 * i + 1, :], ALU.mult, reads=[("lamb",)], writes=[("lt",)])
        P.op("dve", lambda e, o=dst[:], i_=lt[:]: e.reduce_sum(o, i_, AX.X), reads=[("lt",)], writes=[("l", i)])
        P.act(dst[:], dst[:], AF.Exp, reads=[("l", i)], writes=[("l", i)])
    P.tt("dve", nlam[:], l2[:], l1[:], ALU.subtract, reads=[("l", 0), ("l", 1)], writes=[("nlam",)])
    P.ts("dve", nlam[:], nlam[:], float(-lam_init), None, ALU.add, reads=[("nlam",)], writes=[("tiles",)])
    for h in range(4):
        P.dma("sp", Tdd[h][:], n_dd, writes=[("Tdd", h)], group="const")
        P.dma("sp", Tmd[h][:], n_md, writes=[("Tmd", h)], group="const")
        P.memset("pool", Tmq[h][:], 0.0, writes=[("Tmq", h)])
    for b in range(32):
        P.dma("sp", mk[:], m_dd[b], writes=[("mk",)], group="mk")
        P.dma("sp", mk2[:], m_md[b], writes=[("mk2",)], group="mk2")
        P.dma("sp", mk3[:], m_mq1[b], writes=[("mk3",)], group="mk3")
        for h in range(4):
            col = b * 4 + h
            P.stt("dve", Tdd[h][:], mk[:], rbb[:, col:col + 1], Tdd[h][:], ALU.mult, ALU.add,
                  reads=[("mk",), ("rbb",), ("Tdd", h)], writes=[("Tdd", h)])
            P.stt("dve", Tmd[h][:], mk2[:], rbb[0:16, col:col + 1], Tmd[h][:], ALU.mult, ALU.add,
                  reads=[("mk2",), ("rbb",), ("Tmd", h)], writes=[("Tmd", h)])
            P.stt("dve", Tmq[h][:], mk3[:], rbb[0:16, col:col + 1], Tmq[h][:], ALU.mult, ALU.add,
                  reads=[("mk3",), ("rbb",), ("Tmq", h)], writes=[("Tmq", h)])
    for h in range(4):
        P.copy("pool", Tmq[h][:], Tmq[h][:], reads=[("Tdd", h), ("Tmd", h), ("Tmq", h)], writes=[("tiles",), ("Tmq", h)])
    _load_seq(P, ctx, big, 8, hbT, lambda c, part: [("big", c, t) for t in (range(0, 5) if part == 0 else range(5, 9))])
    cnt = [0]

    def nxt():
        cnt[0] += 1
        return cnt[0] - 1

    for wi, dstT in [(0, qTt), (1, kTt)]:
        load_cast_w(P, wb1, [w_q, w_k][wi], stg, 8, 512, ("wb",), colblk=512)
        for h in range(4):
            for ti, (t0, tn) in enumerate(QCH):
                kk = nxt()
                m_, i_ = kk % 2, (kk // 2) % 2
                ps = A.psS[m_][i_]
                for c in range(8):
                    P.mm(ps[:, 0:tn], wb[wi][:, c, h * 128:(h + 1) * 128], big[:, c, t0:t0 + tn], c == 0, c == 7,
                         reads=[("wb",), ("big", c, ti)], writes=[("psS", m_, i_)])
                P.copy("act" if kk % 2 else "dve", dstT[:, h, t0:t0 + tn], ps[:, 0:tn],
                       reads=[("psS", m_, i_)], writes=[(id(dstT), h, ti)])
    load_cast_w(P, wb1, w_v, stg, 8, 512, ("wb",), colblk=512)
    for kb, (ks, nk) in enumerate(KBL):
        kk = nxt()
        m_, i_ = kk % 2, (kk // 2) % 2
        ps = A.psS[m_][i_]
        ti = 0 if kb == 0 else 1 + (kb - 1) // 4
        for c in range(8):
            P.mm(ps[0:nk, 0:512], big[:, c, ks:ks + nk], wb[2][:, c, :], c == 0, c == 7,
                 reads=[("wb",), ("big", c, ti)], writes=[("psS", m_, i_)])
        P.copy("act" if kb % 2 else "dve", Vt[0:nk, kb, :], ps[0:nk, 0:512], reads=[("psS", m_, i_)], writes=[("Vt", kb)])
    sc = 64 ** -0.5
    for c, (qs, nq) in enumerate(QCH):
        for h in range(4):
            tiles = dict(dd=Tdd[h], prev=True, md=Tmd[h], mq1=Tmq[h])
            fb = rbb[:, 31 * 4 + h:31 * 4 + h + 1]
            A.chunk(c,
                    kT=lambda m, kb, h=h: kTt[64 * m:64 * m + 64, h, KBL[kb][0]:KBL[kb][0] + KBL[kb][1]],
                    qT=lambda m, h=h, qs=qs, nq=nq: qTt[64 * m:64 * m + 64, h, qs:qs + nq],
                    kd_base=0, kd=64,
                    V=lambda kb, h=h: Vt[0:KBL[kb][1], kb, h * 128:(h + 1) * 128], vkeys=None,
                    sc=sc, tiles=tiles, far_bias=fb,
                    keysK=lambda m, kb, h=h: [(id(kTt), h, 0 if kb == 0 else 1 + (kb - 1) // 4)],
                    keysQ=lambda m, h=h, c=c: [(id(qTt), h, c)],
                    keysV=lambda kb: [("Vt", kb)])
            P.op("dve", lambda e, o=rc[:, 0:nq], i=A.pn[0][:, 0:nq]: e.reciprocal(o, i), reads=[("pn", 0)], writes=[("rc",)])
            P.tt("dve", u[:, 0:nq], A.po[0][:, 0:nq], rc[:, 0:nq], ALU.mult, reads=[("po", 0), ("rc",)], writes=[("u",)])
            P.op("dve", lambda e, o=rc[:, 0:nq], i=A.pn[1][:, 0:nq]: e.reciprocal(o, i), reads=[("pn", 1), ("rc",)], writes=[("rc",)])
            P.tt("dve", t_[:, 0:nq], A.po[1][:, 0:nq], rc[:, 0:nq], ALU.mult, reads=[("po", 1), ("rc",)], writes=[("t_",)])
            P.stt("dve", u[:, 0:nq], t_[:, 0:nq], nlam[:, 0:1], u[:, 0:nq], ALU.mult, ALU.add,
                  reads=[("t_",), ("u",), ("tiles",)], writes=[("u",)])
            P.act(t_[:, 0:nq], u[:, 0:nq], AF.Square, reads=[("u",)], writes=[("t_",)])
            pst = A.psS[0][0]
            P.mm(pst[:, 0:nq], ones32[:], t_[:, 0:nq], True, True, reads=[("t_",), ("ones32",)], writes=[("psS", 0, 0)])
            P.ts("dve", rc[:, 0:nq], pst[:, 0:nq], RMS_EPS, None, ALU.add, reads=[("psS", 0, 0)], writes=[("rc",)])
            P.act(rc[:, 0:nq], rc[:, 0:nq], AF.Sqrt, reads=[("rc",)], writes=[("rc",)])
            P.op("dve", lambda e, o=rc[:, 0:nq]: e.reciprocal(o, o), reads=[("rc",)], writes=[("rc",)])
            P.tt("dve", u[:, 0:nq], u[:, 0:nq], rc[:, 0:nq], ALU.mult, reads=[("u",), ("rc",)], writes=[("u",)])
            oi = nxt() % 2
            P.act(osb[oi][:, 0:nq], u[:, 0:nq], AF.Identity, scale=gs[:, 0:1], reads=[("u",), ("gs",)], writes=[("osb", oi)])
            P.dma("sp", oT.rows(h * 128, 128)[:, qs:qs + nq], osb[oi][:, 0:nq], reads=[("osb", oi)], writes=[("oin", h, h, c)], group=f"oout{oi}")
    if ctx is None:
        P.emit()
        return nc


def build_s5(ctx=None):
    TWO_PI = 2.0 * math.pi
    nc = bass.Bass("TRN2", target_bir_lowering=False) if ctx is None else ctx.nc
    pre = "" if ctx is None else ctx.pre
    dt = lambda n, s, d=F32: nc.dram_tensor(pre + n, s, d, kind="ExternalInput").ap()
    hbh = dt("hbh", [512, LSEQ], BF16) if ctx is None else None
    pl3 = dt("pl3", [3, 128, 16]); bc3 = dt("bc3", [3, 2048])
    bpad = dt("bpad", [2, 128, 2048]); cpad = dt("cpad", [2, 128, 2048]); dpad = dt("dpad", [128, 512]); iota = dt("iota", [129])
    oT = ORows(nc.dram_tensor("oT", [512, LSEQ], BF16, kind="ExternalOutput").ap()) if ctx is None else ORows(None, ctx.oin)
    P = Prog(nc) if ctx is None else ctx.P
    hb = P.sbuf([128, 4, LSEQ], BF16, "hb")
    W = 2048
    lrB = P.sbuf([128, W], F32, "lrB"); liB = P.sbuf([128, W], F32, "liB"); stB = P.sbuf([128, W], F32, "stB")
    magB = P.sbuf([128, W], F32, "magB"); thB = P.sbuf([128, W], F32, "thB"); cB = P.sbuf([128, W], F32, "cB"); sB = P.sbuf([128, W], F32, "sB")
    crB = P.sbuf([128, W], F32, "crB"); ciB = P.sbuf([128, W], F32, "ciB"); tB = P.sbuf([128, W], F32, "tB")
    bre = P.sbuf([128, W], F32, "bre"); bim = P.sbuf([128, W], F32, "bim")
    BBr = P.sbuf([128, W], BF16, "BBr"); BBi = P.sbuf([128, W], BF16, "BBi")
    Cr = P.sbuf([128, W], BF16, "Cr"); Ci = P.sbuf([128, W], BF16, "Ci"); Dd = P.sbuf([128, 512], BF16, "Dd")
    lrP = P.sbuf([128, 16], F32, "lrP"); liP = P.sbuf([128, 16], F32, "liP"); stP = P.sbuf([128, 16], F32, "stP")
    magP = P.sbuf([128, 16], F32, "magP"); thP = P.sbuf([128, 16], F32, "thP")
    io = P.sbuf([128, 129], F32, "io")
    ang = P.sbuf([128, 129], F32, "ang")
    cosT = P.sbuf([128, 16, 129], F32, "cosT"); sinT = P.sbuf([128, 16, 129], F32, "sinT"); nsinT = P.sbuf([128, 16, 129], F32, "nsinT")
    magT = P.sbuf([128, 16, 128], F32, "magT")
    car = P.sbuf([128, 16], F32, "car"); cai = P.sbuf([128, 16], F32, "cai")
    tmp1 = P.sbuf([128, 4], F32, "tmp1")
    T1 = [P.sbuf([128, 128], F32, f"T1_{i}") for i in range(2)]; T2 = [P.sbuf([128, 128], F32, f"T2_{i}") for i in range(2)]
    T3 = [P.sbuf([128, 128], F32, f"T3_{i}") for i in range(2)]; T4 = [P.sbuf([128, 128], F32, f"T4_{i}") for i in range(2)]
    xr = [P.sbuf([128, 128], F32, f"xr{i}") for i in range(2)]; xi = [P.sbuf([128, 128], F32, f"xi{i}") for i in range(2)]
    wr = [P.sbuf([128, 128], F32, f"wr{i}") for i in range(2)]; wi = [P.sbuf([128, 128], F32, f"wi{i}") for i in range(2)]
    sr = [P.sbuf([128, 128], BF16, f"sr{i}") for i in range(2)]; si = [P.sbuf([128, 128], BF16, f"si{i}") for i in range(2)]
    og = [P.sbuf([128, 128], BF16, f"og{i}") for i in range(2)]
    psr = [P.psum([128, 128], F32, f"psr{i}") for i in range(2)]; psi = [P.psum([128, 128], F32, f"psi{i}") for i in range(2)]
    psy = [P.psum([128, 128], F32, f"psy{i}") for i in range(2)]

    def load_bc(dst, row, key):
        P.dma("sp", dst[:], bc3[row].partition_broadcast(128), writes=[key], group="const")

    load_bc(lrB, 0, ("lrB",)); load_bc(liB, 1, ("liB",)); load_bc(stB, 2, ("stB",))
    P.dma("sp", lrP[:], pl3[0], writes=[("lrP",)], group="const")
    P.dma("sp", liP[:], pl3[1], writes=[("liP",)], group="const")
    P.dma("sp", stP[:], pl3[2], writes=[("stP",)], group="const")
    P.dma("sp", io[:], iota.partition_broadcast(128), writes=[("io",)], group="const")
    if ctx is None:
        for c in range(4):
            P.dma("sp", hb[:, c, :], hbh[c * 128:(c + 1) * 128, :], writes=[("hb", c)], group="hb")
    else:
        hstA = P.sbuf([128, NT], BF16, "hstA"); hstB = P.sbuf([128, NT], BF16, "hstB")
        for c in range(4):
            for part, (rk, c0, d0, n_) in enumerate([(0, 0, 0, NT), (1, 16, NT, NT - 16)]):
                P.dma("sp", hstA[:, 0:n_], ctx.x_rows(rk, c)[:, c0:c0 + n_], writes=[("hstA",)], group="hb")
                P.dma("sp", hstB[:, 0:n_], ctx.x_rows(rk, c + 4)[:, c0:c0 + n_], writes=[("hstB",)], group="hb")
                P.ts("dve", hstA[:, 0:n_], hstA[:, 0:n_], ctx.selv[:, 0:1], None, ALU.mult, reads=[("hstA",), ("selv",)], writes=[("hstA",)])
                P.stt("dve", hb[:, c, d0:d0 + n_], hstB[:, 0:n_], ctx.selv[:, 1:2], hstA[:, 0:n_], ALU.mult, ALU.add,
                      reads=[("hstA",), ("hstB",), ("selv",)], writes=[("hb", c)])

    I32 = mybir.dt.int32
    ki_s = P.sbuf([128, 129], I32, "ki_s")
    km_s = P.sbuf([128, 129], F32, "km_s")
    pP = P.sbuf([128, 16], F32, "pP")

    def sin_of(th, off, prescale, dst, tmp, kt, kd, ktmp, small):
        if small:
            ki, km, kk = ki_s[:], km_s[:], ("sc_tmp",)
        else:
            ki, km, kk = bre[:].bitcast(I32), bim[:], ("bre",)
        P.ts("dve", tmp, th, float(prescale), float(off), ALU.mult, ALU.add, reads=[kt, ktmp], writes=[ktmp])
        P.ts("dve", km, tmp, 1.0 / TWO_PI, None, ALU.mult, reads=[ktmp, kk], writes=[kk])
        P.copy("dve", ki, km, reads=[kk], writes=[kk])
        P.copy("dve", km, ki, reads=[kk], writes=[kk])
        P.stt("dve", tmp, km, -TWO_PI, tmp, ALU.mult, ALU.add, reads=[kk, ktmp], writes=[ktmp])
        P.ts("dve", km, tmp, math.pi, None, ALU.is_gt, reads=[ktmp, kk], writes=[kk])
        P.stt("dve", tmp, km, -TWO_PI, tmp, ALU.mult, ALU.add, reads=[kk, ktmp], writes=[ktmp])
        P.ts("dve", km, tmp, -math.pi, None, ALU.is_lt, reads=[ktmp, kk], writes=[kk])
        P.stt("dve", tmp, km, TWO_PI, tmp, ALU.mult, ALU.add, reads=[kk, ktmp], writes=[ktmp])
        P.act(dst, tmp, AF.Sin, reads=[ktmp], writes=[kd])

    def expm1_horner(x, p, kx, kp):
        P.ts("dve", p, x, 1.0 / 8, 1.0, ALU.mult, ALU.add, reads=[kx, kp], writes=[kp])
        for kdiv in (7, 6, 5, 4, 3, 2):
            P.tt("dve", p, p, x, ALU.mult, reads=[kx, kp], writes=[kp])
            P.ts("dve", p, p, 1.0 / kdiv, 1.0, ALU.mult, ALU.add, reads=[kp], writes=[kp])
        P.tt("dve", x, p, x, ALU.mult, reads=[kx, kp], writes=[kx])

    P.ts("dve", lrP[:], lrP[:], -1e-4, None, ALU.min, reads=[("lrP",)], writes=[("lrP",)])
    P.act(stP[:], stP[:], AF.Exp, reads=[("stP",)], writes=[("stP",)])
    P.tt("dve", magP[:], lrP[:], stP[:], ALU.mult, reads=[("lrP",), ("stP",)], writes=[("magP",)])
    expm1_horner(magP[:], pP[:], ("magP",), ("pP",))
    P.ts("dve", magP[:], magP[:], 1.0, None, ALU.add, reads=[("magP",)], writes=[("magP",)])
    P.tt("dve", thP[:], liP[:], stP[:], ALU.mult, reads=[("liP",), ("stP",)], writes=[("thP",)])
    P.ts("dve", lrB[:], lrB[:], -1e-4, None, ALU.min, reads=[("lrB",)], writes=[("lrB",)])
    P.act(stB[:], stB[:], AF.Exp, reads=[("stB",)], writes=[("stB",)])
    P.tt("dve", magB[:], lrB[:], stB[:], ALU.mult, reads=[("lrB",), ("stB",)], writes=[("magB",)])
    expm1_horner(magB[:], crB[:], ("magB",), ("crB",))
    P.tt("dve", thB[:], liB[:], stB[:], ALU.mult, reads=[("liB",), ("stB",)], writes=[("thB",)])
    sin_of(thB[:], 0.0, 1.0, sB[:], tB[:], ("thB",), ("sB",), ("tB",), False)
    sin_of(thB[:], 0.5 * math.pi, 1.0, cB[:], tB[:], ("thB",), ("cB",), ("tB",), False)
    sin_of(thB[:], 0.0, 0.5, crB[:], tB[:], ("thB",), ("crB",), ("tB",), False)
    P.tt("dve", crB[:], crB[:], crB[:], ALU.mult, reads=[("crB",)], writes=[("crB",)])
    P.ts("dve", crB[:], crB[:], -2.0, None, ALU.mult, reads=[("crB",)], writes=[("crB",)])
    P.tt("dve", cB[:], cB[:], magB[:], ALU.mult, reads=[("cB",), ("magB",)], writes=[("cB",)])
    P.tt("dve", cB[:], cB[:], crB[:], ALU.add, reads=[("cB",), ("crB",)], writes=[("cB",)])
    P.stt("dve", sB[:], magB[:], 1.0, sB[:], ALU.add, ALU.mult, reads=[("sB",), ("magB",)], writes=[("sB",)])
    P.tt("dve", magB[:], lrB[:], lrB[:], ALU.mult, reads=[("lrB",), ("magB",), ("sB",), ("cB",)], writes=[("magB",)])
    P.tt("dve", tB[:], liB[:], liB[:], ALU.mult, reads=[("liB",), ("tB",)], writes=[("tB",)])
    P.tt("dve", magB[:], magB[:], tB[:], ALU.add, reads=[("magB",), ("tB",)], writes=[("magB",)])
    P.op("dve", lambda e, o=magB[:]: e.reciprocal(o, o), reads=[("magB",)], writes=[("magB",)])
    P.tt("dve", crB[:], cB[:], lrB[:], ALU.mult, reads=[("cB",), ("lrB",), ("crB",)], writes=[("crB",)])
    P.tt("dve", tB[:], sB[:], liB[:], ALU.mult, reads=[("sB",), ("liB",), ("tB",)], writes=[("tB",)])
    P.tt("dve", crB[:], crB[:], tB[:], ALU.add, reads=[("crB",), ("tB",)], writes=[("crB",)])
    P.tt("dve", crB[:], crB[:], magB[:], ALU.mult, reads=[("crB",), ("magB",)], writes=[("crB",)])
    P.tt("dve", ciB[:], sB[:], lrB[:], ALU.mult, reads=[("sB",), ("lrB",)], writes=[("ciB",)])
    P.tt("dve", tB[:], cB[:], liB[:], ALU.mult, reads=[("cB",), ("liB",), ("tB",)], writes=[("tB",)])
    P.tt("dve", ciB[:], ciB[:], tB[:], ALU.subtract, reads=[("ciB",), ("tB",)], writes=[("ciB",)])
    P.tt("dve", ciB[:], ciB[:], magB[:], ALU.mult, reads=[("ciB",), ("magB",)], writes=[("ciB",)])
    P.dma("sp", bre[:], bpad[0], writes=[("bre",)], group="const3")
    P.dma("sp", bim[:], bpad[1], reads=[("bre",)], writes=[("bim",)], group="const3")
    P.tt("dve", tB[:], crB[:], bre[:], ALU.mult, reads=[("crB",), ("bre",), ("tB",)], writes=[("tB",)])
    P.tt("dve", thB[:], ciB[:], bim[:], ALU.mult, reads=[("ciB",), ("bim",), ("thB",)], writes=[("thB",)])
    P.tt("dve", BBr[:], tB[:], thB[:], ALU.subtract, reads=[("tB",), ("thB",)], writes=[("BBr",)])
    P.tt("dve", tB[:], crB[:], bim[:], ALU.mult, reads=[("crB",), ("bim",), ("tB",), ("BBr",)], writes=[("tB",)])
    P.tt("dve", thB[:], ciB[:], bre[:], ALU.mult, reads=[("ciB",), ("bre",), ("thB",), ("BBr",)], writes=[("thB",)])
    P.tt("dve", BBi[:], tB[:], thB[:], ALU.add, reads=[("tB",), ("thB",)], writes=[("BBi",)])
    P.dma("sp", bre[:], cpad[0], reads=[("BBr",), ("BBi",)], writes=[("bre",)], group="const2")
    P.dma("sp", bim[:], cpad[1], reads=[("BBr",), ("BBi",)], writes=[("bim",)], group="const2")
    P.copy("pool", Cr[:], bre[:], reads=[("bre",)], writes=[("Cr",)])
    P.copy("pool", Ci[:], bim[:], reads=[("bim",)], writes=[("Ci",)])
    P.dma("sp", lrB[:, 0:512], dpad, reads=[("crB",), ("ciB",)], writes=[("lrB",)], group="const2")
    P.copy("pool", Dd[:], lrB[:, 0:512], reads=[("lrB",)], writes=[("Dd",)])
    for j in range(16):
        P.ts("dve", ang[:], io[:], thP[:, j:j + 1], None, ALU.mult, reads=[("io",), ("thP",), ("ang",)], writes=[("ang",)])
        sin_of(ang[:], 0.0, 1.0, sinT[:, j, :], nsinT[:, j, :], ("ang",), ("sinT", j), ("nsinT", j), True)
        sin_of(ang[:], 0.5 * math.pi, 1.0, cosT[:, j, :], nsinT[:, j, :], ("ang",), ("cosT", j), ("nsinT", j), True)
        P.ts("pool", nsinT[:, j, :], sinT[:, j, :], -1.0, None, ALU.mult, reads=[("sinT", j), ("nsinT", j)], writes=[("nsinT", j)])
        P.memset("pool", magT[:, j, :], 1.0, writes=[("magT", j)])
        P.ts("pool", magT[:, j, :], magT[:, j, :], magP[:, j:j + 1], None, ALU.mult, reads=[("magT", j), ("magP",)], writes=[("magT", j)])
    P.memset("pool", car[:], 0.0, writes=[("car", j) for j in range(16)])
    P.memset("pool", cai[:], 0.0, writes=[("cai", j) for j in range(16)])
    k = 0
    work = []
    for bi, (t0, tn) in enumerate(KBL):
        for j in range(16):
            ch = j // 4
            b2 = k % 2
            k += 1
            c_, s_, ns_ = cosT[:, j, 0:tn], sinT[:, j, 0:tn], nsinT[:, j, 0:tn]
            tk = [("cosT", j), ("sinT", j), ("nsinT", j)]

            def stage_a(bi=bi, t0=t0, tn=tn, j=j, ch=ch, b2=b2, c_=c_, s_=s_, tk=tk):
                P.mm(psr[b2][:, 0:tn], BBr[:, j * 128:(j + 1) * 128], hb[:, ch, t0:t0 + tn], True, True,
                     reads=[("BBr",), ("hb", ch)], writes=[("psr", b2)])
                P.mm(psi[b2][:, 0:tn], BBi[:, j * 128:(j + 1) * 128], hb[:, ch, t0:t0 + tn], True, True,
                     reads=[("BBi",), ("hb", ch)], writes=[("psi", b2)])
                P.tt("dve", T1[b2][:, 0:tn], psr[b2][:, 0:tn], c_, ALU.mult, reads=[("psr", b2)] + tk, writes=[("T1", b2)])
                P.tt("dve", T2[b2][:, 0:tn], psi[b2][:, 0:tn], s_, ALU.mult, reads=[("psi", b2)] + tk, writes=[("T2", b2)])
                P.tt("dve", T3[b2][:, 0:tn], psi[b2][:, 0:tn], c_, ALU.mult, reads=[("psi", b2)] + tk, writes=[("T3", b2)])
                P.tt("dve", T4[b2][:, 0:tn], psr[b2][:, 0:tn], s_, ALU.mult, reads=[("psr", b2)] + tk, writes=[("T4", b2)])
                P.tt("pool", xr[b2][:, 0:tn], T1[b2][:, 0:tn], T2[b2][:, 0:tn], ALU.add, reads=[("T1", b2), ("T2", b2)], writes=[("xr", b2)])
                P.tt("pool", xi[b2][:, 0:tn], T3[b2][:, 0:tn], T4[b2][:, 0:tn], ALU.subtract, reads=[("T3", b2), ("T4", b2)], writes=[("xi", b2)])

            def stage_b(bi=bi, t0=t0, tn=tn, j=j, ch=ch, b2=b2, c_=c_, s_=s_, ns_=ns_, tk=tk):
                P.op("dve", lambda e, o=wr[b2][:, 0:tn], d0=magT[:, j, 0:tn], d1=xr[b2][:, 0:tn], ini=car[:, j:j + 1]:
                     e.tensor_tensor_scan(o, d0, d1, ini, ALU.mult, ALU.add),
                     reads=[("magT", j), ("xr", b2), ("car", j)], writes=[("wr", b2)])
                P.op("dve", lambda e, o=wi[b2][:, 0:tn], d0=magT[:, j, 0:tn], d1=xi[b2][:, 0:tn], ini=cai[:, j:j + 1]:
                     e.tensor_tensor_scan(o, d0, d1, ini, ALU.mult, ALU.add),
                     reads=[("magT", j), ("xi", b2), ("cai", j)], writes=[("wi", b2)])
                er, ei = cosT[:, j, tn:tn + 1], sinT[:, j, tn:tn + 1]
                wl_r, wl_i = wr[b2][:, tn - 1:tn], wi[b2][:, tn - 1:tn]
                P.tt("dve", tmp1[:, 0:1], wl_i, ei, ALU.mult, reads=[("wi", b2)] + tk, writes=[("tmp1", 0)])
                P.stt("dve", car[:, j:j + 1], wl_r, er, tmp1[:, 0:1], ALU.mult, ALU.subtract,
                      reads=[("wr", b2), ("tmp1", 0)] + tk, writes=[("car", j)])
                P.tt("dve", tmp1[:, 1:2], wl_r, ei, ALU.mult, reads=[("wr", b2)] + tk, writes=[("tmp1", 1)])
                P.stt("dve", cai[:, j:j + 1], wl_i, er, tmp1[:, 1:2], ALU.mult, ALU.add,
                      reads=[("wi", b2), ("tmp1", 1)] + tk, writes=[("cai", j)])
                P.tt("pool", T1[b2][:, 0:tn], wr[b2][:, 0:tn], c_, ALU.mult, reads=[("wr", b2)] + tk, writes=[("T1", b2)])
                P.tt("pool", T2[b2][:, 0:tn], wi[b2][:, 0:tn], s_, ALU.mult, reads=[("wi", b2)] + tk, writes=[("T2", b2)])
                P.tt("pool", sr[b2][:, 0:tn], T1[b2][:, 0:tn], T2[b2][:, 0:tn], ALU.subtract, reads=[("T1", b2), ("T2", b2)], writes=[("sr", b2)])
                P.tt("dve", T3[b2][:, 0:tn], wr[b2][:, 0:tn], ns_, ALU.mult, reads=[("wr", b2)] + tk, writes=[("T3", b2)])
                P.tt("pool", T4[b2][:, 0:tn], wi[b2][:, 0:tn], c_, ALU.mult, reads=[("wi", b2)] + tk, writes=[("T4", b2)])
                P.tt("pool", si[b2][:, 0:tn], T3[b2][:, 0:tn], T4[b2][:, 0:tn], ALU.subtract, reads=[("T3", b2), ("T4", b2)], writes=[("si", b2)])
                yb = (bi * 4 + ch) % 2
                if j % 4 == 0:
                    P.mm(psy[yb][:, 0:tn], Dd[:, ch * 128:(ch + 1) * 128], hb[:, ch, t0:t0 + tn], True, False,
                         reads=[("Dd",), ("hb", ch)], writes=[("psy", yb)])
                P.mm(psy[yb][:, 0:tn], Cr[:, j * 128:(j + 1) * 128], sr[b2][:, 0:tn], False, False,
                     reads=[("Cr",), ("sr", b2)], writes=[("psy", yb)])
                P.mm(psy[yb][:, 0:tn], Ci[:, j * 128:(j + 1) * 128], si[b2][:, 0:tn], False, j % 4 == 3,
                     reads=[("Ci",), ("si", b2)], writes=[("psy", yb)])
                if j % 4 == 3:
                    P.act(og[yb][:, 0:tn], psy[yb][:, 0:tn], AF.Gelu, reads=[("psy", yb)], writes=[("og", yb)])
                    P.dma("sp", oT.rows(ch * 128, 128)[:, t0:t0 + tn], og[yb][:, 0:tn], reads=[("og", yb)], writes=[("oin", ch, ch, bi)], group=f"oout{yb}")
            work.append((stage_a, stage_b))
    for i_ in range(len(work) + 1):
        if i_ < len(work):
            work[i_][0]()
        if i_ >= 1:
            work[i_ - 1][1]()
    if ctx is None:
        P.emit()
        return nc


def build_ssd(ctx=None):
    nc = bass.Bass("TRN2", target_bir_lowering=False) if ctx is None else ctx.nc
    pre = "" if ctx is None else ctx.pre
    dt = lambda n, s, d=F32: nc.dram_tensor(pre + n, s, d, kind="ExternalInput").ap()
    hbT = dt("hbT", [D, LSEQ], BF16) if ctx is None else None
    w_z = dt("w_z", [D, 1024]); w_x = dt("w_x", [D, 1024]); w_B = dt("w_B", [D, 512]); w_C = dt("w_C", [D, 512]); w_dt = dt("w_dt", [D, 16])
    convw = dt("convw", [4, 2048]); convb = dt("convb", [2048])
    dtb = dt("dtb", [16]); alog = dt("alog", [16]); dskip = dt("dskip", [16]); ng = dt("ng", [1024])
    ident_d = dt("ident", [128, 128]); negm_d = dt("negm", [128, 128]); sel_d = dt("sel", [16, 2048])
    oT = ORows(nc.dram_tensor("oT", [1024, LSEQ], BF16, kind="ExternalOutput").ap()) if ctx is None else ORows(None, ctx.oin)
    P = Prog(nc) if ctx is None else ctx.P
    hb = P.sbuf([128, 8, LSEQ], BF16, "hb")
    praw = P.sbuf([128, LSEQ + 3], F32, "praw")
    PIECE = 1040
    PCS = [(0, 1040), (1040, 1024), (2064, 1024), (3088, 1024)]
    acc = P.sbuf([128, PIECE], F32, "acc")
    cvo = P.sbuf([128, PIECE], BF16, "cvo")
    BT = P.sbuf([128, LSEQ], BF16, "BT"); CT = P.sbuf([128, LSEQ], BF16, "CT")
    V = P.sbuf([128, 33, 256], BF16, "V")
    acT = P.sbuf([16, LSEQ], F32, "acT")
    dtp = P.sbuf([16, PIECE], F32, "dtp"); atp = P.sbuf([16, PIECE], F32, "atp"); onesr = P.sbuf([16, PIECE], F32, "onesr")
    dt_k = P.sbuf([128, 33, 16], F32, "dt_k"); nac_k = P.sbuf([128, 33, 16], F32, "nac_k")
    bcS = [P.sbuf([128, 512], F32, f"bcS{i}") for i in range(4)]
    E = [P.sbuf([128, 512], F32, f"E{i}") for i in range(3)]
    PT = [P.sbuf([128, 512], BF16, f"PT{i}") for i in range(4)]
    cnt_q, cnt_e, cnt_p = [0], [0], [0]
    Dt = P.sbuf([128, 16, 128], BF16, "Dt")
    wst = P.sbuf([128, 8, 128], F32, "wst")
    wtb = P.sbuf([128, 8, 128], BF16, "wtb")
    wzb = P.sbuf([128, 8, 256], BF16, "wzb")
    wdtb = P.sbuf([128, 8, 16], BF16, "wdtb")
    wdts = P.sbuf([128, 8, 16], F32, "wdts")
    sz = P.sbuf([128, 512], F32, "sz"); sq = P.sbuf([128, 512], F32, "sqq"); rstd = P.sbuf([128, 512], F32, "rstd")
    osb = [P.sbuf([128, 512], BF16, f"osb{i}") for i in range(2)]
    ident = P.sbuf([128, 128], F32, "identS"); identb = P.sbuf([128, 128], BF16, "identb"); negm = P.sbuf([128, 128], F32, "negmS")
    sel = P.sbuf([16, 2048], F32, "selS")
    cw = P.sbuf([128, 16, 4], F32, "cw"); cb = P.sbuf([128, 16], F32, "cbias")
    dtb_t = P.sbuf([16, 1], F32, "dtb_t"); A_t = P.sbuf([16, 1], F32, "A_t"); one_t = P.sbuf([16, 1], F32, "one_t")
    dbc = P.sbuf([128, 16], F32, "dbc"); ngt = P.sbuf([64, 16], F32, "ngt"); ones64 = P.sbuf([64, 64], F32, "ones64")
    car = P.sbuf([16, 1], F32, "carS")
    po = [P.psum([128, 512], F32, f"po{i}") for i in range(4)]
    pcb = [P.psum([128, 512], F32, f"pcb{i}") for i in range(2)]
    psx = P.psum([128, 512], F32, "psx")
    pst = P.psum([128, 512], F32, "pst")
    psxb = psx[:].bitcast(BF16)

    P.dma("sp", ident[:], ident_d, writes=[("ident",)], group="const")
    P.dma("sp", negm[:], negm_d, writes=[("negm",)], group="const")
    P.dma("sp", sel[:], sel_d, writes=[("sel",)], group="const")
    for k_ in range(4):
        P.dma("sp", cw[:, :, k_], convw[k_].rearrange("(c p) -> p c", p=128), writes=[("cw",)], group="const", slow=True)
    P.dma("sp", cb[:], convb.rearrange("(c p) -> p c", p=128), writes=[("cbias",)], group="const", slow=True)
    P.dma("sp", dtb_t[:], dtb.rearrange("(p o) -> p o", o=1), writes=[("dtb",)], group="const", slow=True)
    P.dma("sp", A_t[:], alog.rearrange("(p o) -> p o", o=1), writes=[("A",)], group="const", slow=True)
    P.dma("sp", dbc[:], dskip.partition_broadcast(128), writes=[("dbc",)], group="const")
    P.dma("sp", ngt[:], ng.rearrange("(h p) -> p h", p=64), writes=[("ngt",)], group="const", slow=True)
    P.copy("pool", identb[:], ident[:], reads=[("ident",)], writes=[("identb",)])
    P.memset("pool", ones64[:], 1.0 / 256, writes=[("ones64",)])
    P.memset("pool", one_t[:], 1.0, writes=[("one_t",)])
    P.memset("pool", onesr[:], 1.0, writes=[("onesr",)])
    P.memset("pool", praw[:, 0:3], 0.0, writes=[("praw0",)])
    P.memset("pool", car[:], 0.0, writes=[("car",)])
    P.act(A_t[:], A_t[:], AF.Exp, reads=[("A",)], writes=[("A",)])
    P.ts("dve", A_t[:], A_t[:], -1.0, None, ALU.mult, reads=[("A",)], writes=[("A",)])
    for h in range(16):
        P.ts("dve", Dt[:, h, :], ident[:], dbc[:, h:h + 1], None, ALU.mult, reads=[("ident",), ("dbc",)], writes=[("Dt",)])
    _load_seq(P, ctx, hb, 8, hbT, lambda c, part: [("hb", c)])
    HBK = [("hb", c) for c in range(8)]
    P.dma("sp", wdts[:], w_dt.rearrange("(c p) f -> p c f", p=128), writes=[("wdts",)], group="const", slow=True)
    P.copy("pool", wdtb[:], wdts[:], reads=[("wdts",)], writes=[("wdtb",)])
    for pi, (p0, pn) in enumerate(PCS):
        for s0 in range(0, pn, 512):
            sn = min(512, pn - s0)
            for c in range(8):
                P.mm(psx[0:16, 0:sn], wdtb[:, c, :], hb[:, c, p0 + s0:p0 + s0 + sn], c == 0, c == 7,
                     reads=[("wdtb",), ("hb", c)], writes=[("psx",)])
            P.act(dtp[:, s0:s0 + sn], psx[0:16, 0:sn], AF.Exp, bias=dtb_t[:, 0:1], reads=[("psx",), ("dtb",)], writes=[("dtp",)])
        P.act(dtp[:, 0:pn], dtp[:, 0:pn], AF.Ln, bias=one_t[:, 0:1], reads=[("dtp",), ("one_t",)], writes=[("dtp",)])
        P.ts("dve", atp[:, 0:pn], dtp[:, 0:pn], A_t[:, 0:1], None, ALU.mult, reads=[("dtp",), ("A",)], writes=[("atp",)])
        P.op("dve", lambda e, o=acT[:, p0:p0 + pn], d0=onesr[:, 0:pn], d1=atp[:, 0:pn], ini=car[:, 0:1]:
             e.tensor_tensor_scan(o, d0, d1, ini, ALU.mult, ALU.add),
             reads=[("onesr",), ("atp",), ("car",)], writes=[("acT", pi)])
        P.copy("dve", car[:], acT[:, p0 + pn - 1:p0 + pn], reads=[("acT", pi)], writes=[("car",)])
        for kb, (ks, nk) in enumerate(KBL):
            if not (p0 <= ks < p0 + pn):
                continue
            assert ks + nk <= p0 + pn
            P.tr(psx[0:nk, 0:16], dtp[:, ks - p0:ks - p0 + nk], ident[0:16, 0:16], reads=[("dtp",), ("ident",)], writes=[("psx",)])
            P.copy("act", dt_k[0:nk, kb, :], psx[0:nk, 0:16], reads=[("psx",)], writes=[("dt_k", kb)])
            P.tr(psx[0:nk, 0:16], acT[:, ks:ks + nk], ident[0:16, 0:16], reads=[("acT", pi), ("ident",)], writes=[("psx",)])
            P.ts("dve", nac_k[0:nk, kb, :], psx[0:nk, 0:16], -1.0, None, ALU.mult, reads=[("psx",)], writes=[("nac_k", kb)])
    ACK = [("acT", pi) for pi in range(4)]

    def proj_conv(wsrc, col0, f, out_fn):
        P.dma("sp", wst[:], wsrc[:, col0:col0 + 128].rearrange("(c p) f -> p c f", p=128), writes=[("wst",)], group="wst")
        P.copy("act", wtb[:], wst[:], reads=[("wst",)], writes=[("wtb",)])
        for ti, (t0, tn) in enumerate(QCH):
            kq = ti % 2
            for c in range(8):
                P.mm(pcb[kq][:, 0:tn], wtb[:, c, :], hb[:, c, t0:t0 + tn], c == 0, c == 7,
                     reads=[("wtb",), ("hb", c)], writes=[("pcb", kq)])
            P.copy("act", praw[:, 3 + t0:3 + t0 + tn], pcb[kq][:, 0:tn], reads=[("pcb", kq), ("praw0",)], writes=[("praw", ti)])
        PK = [("praw", ti) for ti in range(9)] + [("praw0",)]
        for pi, (p0, pn) in enumerate(PCS):
            P.ts("dve", acc[:, 0:pn], praw[:, p0:p0 + pn], cw[:, f, 0:1], None, ALU.mult, reads=PK + [("cw",)], writes=[("acc",)])
            for k in (1, 2, 3):
                P.stt("dve", acc[:, 0:pn], praw[:, p0 + k:p0 + k + pn], cw[:, f, k:k + 1], acc[:, 0:pn], ALU.mult, ALU.add,
                      reads=PK + [("cw",), ("acc",)], writes=[("acc",)])
            out_fn(pi, p0, pn)

    ncount = [0]

    def nx():
        ncount[0] += 1
        return ncount[0] - 1

    for g in range(4):
        def outB(pi, p0, pn, g=g):
            P.act(BT[:, p0:p0 + pn], acc[:, 0:pn], AF.Silu, bias=cb[:, 8 + g:9 + g], reads=[("acc",), ("cbias",)], writes=[("BT", pi)])

        def outC(pi, p0, pn, g=g):
            P.act(CT[:, p0:p0 + pn], acc[:, 0:pn], AF.Silu, bias=cb[:, 12 + g:13 + g], reads=[("acc",), ("cbias",)], writes=[("CT", pi)])
        proj_conv(w_B, g * 128, 8 + g, outB)
        proj_conv(w_C, g * 128, 12 + g, outC)
        for xc in range(2):
            f = 2 * g + xc

            def outX(pi, p0, pn, f=f, xc=xc):
                P.act(cvo[:, 0:pn], acc[:, 0:pn], AF.Silu, bias=cb[:, f:f + 1], reads=[("acc",), ("cbias",)], writes=[("cvo",)])
                for kb, (ks, nk) in enumerate(KBL):
                    if not (p0 <= ks < p0 + pn):
                        continue
                    P.tr(psxb[0:nk, 0:128], cvo[:, ks - p0:ks - p0 + nk], identb[:], reads=[("cvo",), ("identb",)], writes=[("psx",)])
                    P.copy("act" if kb % 2 else "dve", V[0:nk, kb, xc * 128:(xc + 1) * 128], psxb[0:nk, 0:128],
                           reads=[("psx",)], writes=[("V", kb)])
            proj_conv(w_x, f * 128, f, outX)
        for hf in range(2):
            P.dma("sp", wst[:], w_z[:, g * 256 + hf * 128:g * 256 + (hf + 1) * 128].rearrange("(c p) f -> p c f", p=128),
                  writes=[("wst",)], group="wst")
            P.copy("act", wzb[:, :, hf * 128:(hf + 1) * 128], wst[:], reads=[("wst",)], writes=[("wzb",)])
        BK = [("BT", pi) for pi in range(4)]
        CK = [("CT", pi) for pi in range(4)]
        for c, (qs, nq) in enumerate(QCH):
            kbs = [0] if c == 0 else list(range(0, 4 * c + 1))
            qb0 = 4 * (c - 1) + 1
            for hl in range(4):
                h = 4 * g + hl
                P.mm(psx[:, 0:nq], sel[:, h * 128:(h + 1) * 128], acT[:, qs:qs + nq], True, True,
                     reads=[("sel",)] + ACK, writes=[("psx",)])
                P.copy("act", bcS[hl][:, 0:nq], psx[:, 0:nq], reads=[("psx",)], writes=[("bcS", hl)])
            work = []
            for kb in kbs:
                ks, nk = KBL[kb]
                last = (kb == kbs[-1])
                if c == 0:
                    cs, dw = 0, 16
                elif kb == 0:
                    cs, dw = 0, 0
                else:
                    i0 = max(0, kb - qb0)
                    cs = 128 * i0
                    dw = 128 if (qb0 + i0 - kb) == 0 else 0
                kq = cnt_q[0] % 2
                cnt_q[0] += 1

                def emit_cb(kb=kb, ks=ks, nk=nk, cs=cs, kq=kq):
                    P.mm(pcb[kq][0:nk, cs:nq], BT[:, ks:ks + nk], CT[:, qs + cs:qs + nq], True, True,
                         reads=BK + CK, writes=[("pcb", kq)])

                def emit_rest(kb=kb, ks=ks, nk=nk, cs=cs, dw=dw, kq=kq, last=last):
                    for hl in range(4):
                        h = 4 * g + hl
                        ke = cnt_e[0] % 3
                        cnt_e[0] += 1
                        kp = cnt_p[0] % 4
                        cnt_p[0] += 1
                        Et, PTt = E[ke], PT[kp]
                        bias = nac_k[0:nk, kb, h:h + 1]
                        if dw:
                            P.tt("dve", Et[0:nk, cs:cs + dw], bcS[hl][0:nk, cs:cs + dw], negm[0:nk, 0:dw], ALU.add,
                                 reads=[("bcS", hl), ("negm",)], writes=[("E", ke)])
                            P.act(Et[0:nk, cs:cs + dw], Et[0:nk, cs:cs + dw], AF.Exp, bias=bias, reads=[("E", ke), ("nac_k", kb)], writes=[("E", ke)])
                        if cs + dw < nq:
                            P.act(Et[0:nk, cs + dw:nq], bcS[hl][0:nk, cs + dw:nq], AF.Exp, bias=bias,
                                  reads=[("bcS", hl), ("nac_k", kb)], writes=[("E", ke)])
                        P.stt("dve", PTt[0:nk, cs:nq], pcb[kq][0:nk, cs:nq], dt_k[0:nk, kb, h:h + 1], Et[0:nk, cs:nq], ALU.mult, ALU.mult,
                              reads=[("pcb", kq), ("dt_k", kb), ("E", ke)], writes=[("PT", kp)])
                        P.mm(po[hl][0:64, cs:nq], V[0:nk, kb, hl * 64:(hl + 1) * 64], PTt[0:nk, cs:nq], kb == kbs[0], False,
                             reads=[("PT", kp), ("V", kb)], writes=[("po", hl)])
                        if dw:
                            P.mm(po[hl][0:64, cs:cs + dw], V[0:nk, kb, hl * 64:(hl + 1) * 64], Dt[0:nk, h, 0:dw], False, last,
                                 reads=[("Dt",), ("V", kb)], writes=[("po", hl)])
                work.append((emit_cb, emit_rest))
            for i_ in range(len(work) + 1):
                if i_ < len(work):
                    work[i_][0]()
                if i_ >= 1:
                    work[i_ - 1][1]()
            for hl in range(4):
                h = 4 * g + hl
                for cc in range(8):
                    P.mm(psx[0:64, 0:nq], wzb[:, cc, hl * 64:(hl + 1) * 64], hb[:, cc, qs:qs + nq], cc == 0, cc == 7,
                         reads=[("wzb",), ("hb", cc)], writes=[("psx",)])
                P.act(sz[0:64, 0:nq], psx[0:64, 0:nq], AF.Silu, reads=[("psx",)], writes=[("sz",)])
                P.tt("dve", bcS[hl][0:64, 0:nq], po[hl][0:64, 0:nq], sz[0:64, 0:nq], ALU.mult,
                     reads=[("po", hl), ("sz",), ("bcS", hl)], writes=[("bcS", hl)])
                P.act(sq[0:64, 0:nq], bcS[hl][0:64, 0:nq], AF.Square, reads=[("bcS", hl)], writes=[("sqq",)])
                P.mm(pst[0:64, 0:nq], ones64[:], sq[0:64, 0:nq], hl == 0, hl == 3, reads=[("sqq",), ("ones64",)], writes=[("pst",)])
            P.ts("dve", rstd[0:64, 0:nq], pst[0:64, 0:nq], RMS_EPS, None, ALU.add, reads=[("pst",)], writes=[("rstd",)])
            P.act(rstd[0:64, 0:nq], rstd[0:64, 0:nq], AF.Sqrt, reads=[("rstd",)], writes=[("rstd",)])
            P.op("dve", lambda e, o=rstd[0:64, 0:nq]: e.reciprocal(o, o), reads=[("rstd",)], writes=[("rstd",)])
            for hl in range(4):
                h = 4 * g + hl
                oi = nx() % 2
                P.tt("dve", bcS[hl][0:64, 0:nq], bcS[hl][0:64, 0:nq], rstd[0:64, 0:nq], ALU.mult,
                     reads=[("bcS", hl), ("rstd",)], writes=[("bcS", hl)])
                P.act(osb[oi][0:64, 0:nq], bcS[hl][0:64, 0:nq], AF.Identity, scale=ngt[:, h:h + 1],
                      reads=[("bcS", hl), ("ngt",)], writes=[("osb", oi)])
                P.dma("sp", oT.rows(h * 64, 64)[:, qs:qs + nq], osb[oi][0:64, 0:nq], reads=[("osb", oi)], writes=[("oin", h // 2, h, c)], group=f"oout{oi}")
    if ctx is None:
        P.emit()
        return nc


def ssd_inputs(d, hbT, hh):
    w = d['ssd_w_in'][0]
    DI = 2048
    sl = slice(hh * 1024, (hh + 1) * 1024)
    w_z = w[:, 0:DI][:, sl]
    w_x = w[:, DI:2 * DI][:, sl]
    w_B = w[:, 2 * DI:2 * DI + 1024][:, hh * 512:(hh + 1) * 512]
    w_C = w[:, 2 * DI + 1024:2 * DI + 2048][:, hh * 512:(hh + 1) * 512]
    w_dt = w[:, 2 * DI + 2048:][:, hh * 16:(hh + 1) * 16]
    cw = d['ssd_conv_w'][0]; cbv = d['ssd_conv_b'][0]
    convw = np.concatenate([cw[:, 0:DI][:, sl], cw[:, DI:DI + 1024][:, hh * 512:(hh + 1) * 512], cw[:, DI + 1024:][:, hh * 512:(hh + 1) * 512]], 1)
    convb = np.concatenate([cbv[0:DI][sl], cbv[DI:DI + 1024][hh * 512:(hh + 1) * 512], cbv[DI + 1024:][hh * 512:(hh + 1) * 512]])
    kk = np.arange(128)[:, None]; qq = np.arange(128)[None, :]
    negm = np.where(kk <= qq, 0.0, NEG).astype(np.float32)
    sel = np.zeros((16, 16, 128), np.float32)
    for h in range(16):
        sel[h, h, :] = 1.0
    c = np.ascontiguousarray
    return dict(hbT=hbT, w_z=c(w_z), w_x=c(w_x), w_B=c(w_B), w_C=c(w_C), w_dt=c(w_dt), convw=c(convw), convb=c(convb),
                dtb=c(d['ssd_dt_bias'][0][hh * 16:(hh + 1) * 16]), alog=c(d['ssd_a_log'][0][hh * 16:(hh + 1) * 16]),
                dskip=c(d['ssd_d'][0][hh * 16:(hh + 1) * 16]), ng=c(d['ssd_norm_g'][0][sl]),
                ident=np.eye(128, dtype=np.float32), negm=negm, sel=sel.reshape(16, 2048))


def build_t(prev, n_ffn, want_bf):
    nc = bass.Bass("TRN2", target_bir_lowering=False)
    dt = lambda n, s, d=F32: nc.dram_tensor(n, s, d, kind="ExternalInput").ap()
    hT = dt("hT", [D, NT])
    nln = n_ffn + (1 if prev else 0)
    lg = dt("ln_g", [nln, D]); lb = dt("ln_b", [nln, D])
    ws = [(dt(f"w1_{i}", [D, FF]), dt(f"w3_{i}", [D, FF]), dt(f"w2_{i}", [FF, D])) for i in range(n_ffn)]
    if prev in ("lin8", "lin16"):
        kc = 8 if prev == "lin8" else 16
        oT = dt("oT", [kc * 128, NT], BF16); w_out = dt("w_out", [kc * 128, D])
    elif prev == "glu":
        oT = dt("oT", [D, NT], BF16); w_glu = dt("w_glu", [D, 2 * D]); b_glu = dt("b_glu", [2 * D])
    o32 = nc.dram_tensor("o32", [D, NT], F32, kind="ExternalOutput").ap()
    obf = nc.dram_tensor("obf", [D, NT], BF16, kind="ExternalOutput").ap() if want_bf else None
    P = Prog(nc)
    T = TPhase(P, nc)
    T.load_ln(lg, lb, nln)
    T.load_h(hT)
    s = 0
    if prev in ("lin8", "lin16"):
        T.outproj(oT, w_out, kc); T.layer_norm(s); s += 1
    elif prev == "glu":
        T.glu(oT, w_glu, b_glu); T.layer_norm(s); s += 1
    for i in range(n_ffn):
        T.ffn(*ws[i]); T.layer_norm(s); s += 1
    T.store_h(o32, obf)
    P.emit()
    return nc


def s5_inputs(d, hbT, hh):
    G0 = hh * 32
    lam_re = d['s5_lam_re'][0][G0:G0 + 32]; lam_im = d['s5_lam_im'][0][G0:G0 + 32]
    lstep = np.repeat(d['s5_log_step'][0][G0:G0 + 32][:, None], 64, 1)

    def pl(a):
        return np.ascontiguousarray(a.reshape(16, 2, 64).transpose(1, 2, 0).reshape(128, 16))
    pl3 = np.stack([pl(lam_re), pl(lam_im), pl(lstep)]).astype(np.float32)
    bc3 = np.stack([lam_re.reshape(-1), lam_im.reshape(-1), lstep.reshape(-1)]).astype(np.float32)
    bpad = np.zeros((2, 128, 16, 128), np.float32)
    cpad = np.zeros((2, 128, 16, 128), np.float32)
    for k, (bn, cn) in enumerate([('s5_b_re', 's5_c_re'), ('s5_b_im', 's5_c_im')]):
        B = d[bn][0][G0:G0 + 32]
        C = d[cn][0][G0:G0 + 32]
        for g in range(32):
            j, gl, g8 = g // 2, g % 2, g % 8
            bpad[k, g8 * 16:(g8 + 1) * 16, j, gl * 64:(gl + 1) * 64] = B[g].T
            cpad[k, gl * 64:(gl + 1) * 64, j, g8 * 16:(g8 + 1) * 16] = C[g].T
    dv = d['s5_d'][0][G0:G0 + 32].reshape(4, 128)
    dpad = np.zeros((128, 4, 128), np.float32)
    for c in range(4):
        dpad[np.arange(128), c, np.arange(128)] = dv[c]
    return dict(hbh=None if hbT is None else np.ascontiguousarray(hbT[hh * 512:(hh + 1) * 512]), pl3=pl3, bc3=bc3,
                bpad=bpad.reshape(2, 128, 2048), cpad=cpad.reshape(2, 128, 2048), dpad=dpad.reshape(128, 512),
                iota=np.arange(129, dtype=np.float32))


_DBG = None


def _run(nc, in_maps):
    res = run_bass_kernel_spmd(nc, in_maps, core_ids=list(range(NCORES)))
    if _DBG is not None:
        _DBG.append(res.results)
    return res.results


def _seq_from_shards(shards):
    return [np.ascontiguousarray(np.concatenate([shards[2 * b], shards[2 * b + 1][:, 16:]], axis=1)) for b in range(4)]


def _shards_from_seq(seqs_halves):
    out = []
    for r in range(NCORES):
        b, half = r // 2, r % 2
        full = np.concatenate([seqs_halves[2 * b], seqs_halves[2 * b + 1]], axis=0)
        if half == 0:
            out.append(np.ascontiguousarray(full[:, 0:NT]))
        else:
            z = np.zeros((full.shape[0], 16), full.dtype)
            out.append(np.ascontiguousarray(np.concatenate([z, full[:, NT:]], axis=1)))
    return out


def _bucket_table(n):
    dist = np.arange(n)
    max_exact = 16
    df = np.maximum(dist, max_exact).astype(np.float32)
    large = max_exact + (np.log(df / np.float32(max_exact)) / np.float32(math.log(128 / max_exact)) * np.float32(32 - max_exact)).astype(np.int32)
    large = np.minimum(large, 31)
    return np.where(dist < max_exact, dist, large)


def diff_consts():
    tab = _bucket_table(400)
    ki = np.arange(128)[:, None]; qi = np.arange(256)[None, :]
    dist = qi - ki
    bk = tab[np.maximum(dist, 0)]
    m_dd = np.stack([((bk == b) & (dist >= 0)) for b in range(32)]).astype(np.float32)
    n_dd = np.where(dist >= 0, 0.0, NEG).astype(np.float32)
    k2 = np.arange(16)[:, None]; q2 = np.arange(16)[None, :]
    d2 = q2 - k2
    b2 = tab[np.maximum(d2, 0)]
    m_md = np.stack([((b2 == b) & (d2 >= 0)) for b in range(32)]).astype(np.float32)
    n_md = np.where(d2 >= 0, 0.0, NEG).astype(np.float32)
    q3 = np.arange(128)[None, :]
    d3 = (16 + q3) - k2
    b3 = tab[d3]
    m_mq1 = np.stack([(b3 == b) for b in range(32)]).astype(np.float32)
    return dict(m_dd=m_dd, n_dd=n_dd, m_md=m_md, n_md=n_md, m_mq1=m_mq1)


def diff_inputs(d, hbT, hh, consts):
    w = d['diff_w_qkv'][0]
    sl = slice(hh * 512, (hh + 1) * 512)
    lam4 = np.stack([d['diff_lam_q1'][0], d['diff_lam_k1'][0], d['diff_lam_q2'][0], d['diff_lam_k2'][0]])
    rb = np.ascontiguousarray(d['rel_bias'][:, hh * 4:(hh + 1) * 4]).reshape(-1)
    c = np.ascontiguousarray
    r = dict(hbT=hbT, w_q=c(w[:, 0:1024][:, sl]), w_k=c(w[:, 1024:2048][:, sl]), w_v=c(w[:, 2048:3072][:, sl]),
             lam4=c(lam4), subg=d['diff_subln_g'][0], rb=rb)
    r.update(consts)
    return r


def mla_inputs(d, hbT, hh):
    w_in = d['mla_w_in'][0]
    kr = w_in[:, 640:672]
    krs = np.concatenate([kr[:, 16:], kr[:, :16]], 1)
    z = np.zeros((1024, 64), np.float32)
    w_in_ext = np.concatenate([w_in[:, :640], z, kr, z, krs], 1)
    wq = d['mla_w_uq'][0].reshape(384, 16, 96)[:, hh * 8:(hh + 1) * 8]
    wqs = np.concatenate([wq[:, :, :64], wq[:, :, 80:96], wq[:, :, 64:80]], 2)
    wkv = d['mla_w_ukv'][0].reshape(256, 16, 128)[:, hh * 8:(hh + 1) * 8]
    pos = np.arange(LSEQ, dtype=np.float32)
    inv = (np.float32(10000.0) ** (-np.arange(0, 32, 2, dtype=np.float32) / np.float32(32))).astype(np.float32)
    ang = pos[None, :] * inv[:, None]
    cos = np.cos(ang).astype(np.float32); sin = np.sin(ang).astype(np.float32)
    ropeC = np.concatenate([cos, cos], 0); ropeS = np.concatenate([-sin, sin], 0)
    kk = np.arange(128)[:, None]; qq = np.arange(128)[None, :]
    mask = np.where(kk <= qq, 0.0, NEG).astype(np.float32)
    c = np.ascontiguousarray
    return dict(hbT=hbT, w_in_ext=c(w_in_ext), w_uq=c(wq.reshape(384, 768)), w_uq_sw=c(wqs.reshape(384, 768)),
                w_uk=c(wkv[:, :, :64].reshape(256, 512)), w_uv=c(wkv[:, :, 64:].reshape(256, 512)),
                qg=d['mla_q_norm_g'][0], kvg=d['mla_kv_norm_g'][0], ropeC=c(ropeC), ropeS=c(ropeS), mask_dd=mask)


MIX = ["ssd", "diff", "s5", "mla"]
MIX_FH = {"ssd": 1024, "diff": 512, "s5": 512, "mla": 512}
MIX_PREV = {"ssd": ("lin16", 16), "diff": ("lin8", 8), "s5": ("glu", 8), "mla": ("lin8", 8)}


def build_fused(nstage=9):
    nc = bass.Bass("TRN2", target_bir_lowering=False)
    ein = lambda n, s, d=F32: nc.dram_tensor(n, s, d, kind="ExternalInput").ap()
    hT = ein("hT", [D, NT])
    selv_d = ein("selv", [128, 2])
    out32 = nc.dram_tensor("out32", [D, NT], F32, kind="ExternalOutput").ap()
    hres = nc.dram_tensor("hres", [D, NT], F32).ap()
    xin = [[nc.dram_tensor(f"xin{i}_{k}", [256, NT], BF16).ap() for k in range(4)] for i in range(4)]
    xall = [[nc.dram_tensor(f"xall{i}_{k}", [512, NT], BF16).ap() for k in range(4)] for i in range(4)]
    oin = [[nc.dram_tensor(f"oin{i}_{j}", [128, LSEQ], BF16).ap() for j in range(MIX_FH[MIX[i]] // 128)] for i in range(4)]
    oall = [[nc.dram_tensor(f"oall{i}_{j}", [256, LSEQ], BF16).ap() for j in range(MIX_FH[MIX[i]] // 128)] for i in range(4)]
    P = Prog(nc, arena=True)

    def t_phase(ti):
        pre = f"t{ti}_"
        last = (ti == 4)
        n_ffn = 1 if (ti == 0 or last) else 2
        nln = n_ffn + (0 if ti == 0 else 1)
        lg = ein(pre + "ln_g", [nln, D]); lb = ein(pre + "ln_b", [nln, D])
        ws = [(ein(pre + f"w1_{k}", [D, FF]), ein(pre + f"w3_{k}", [D, FF]), ein(pre + f"w2_{k}", [FF, D])) for k in range(n_ffn)]
        T = TPhase(P, nc)
        T.load_ln(lg, lb, nln)
        s = 0
        if ti == 0:
            T.load_h(hT)
        else:
            T.load_h(hres)
            sv = P.sbuf([128, 2], F32, "selv")
            P.dma("sp", sv[:], selv_d, writes=[("selv",)], group="const")
            T.selv = sv
            prev, kc = MIX_PREV[MIX[ti - 1]]
            if prev == "glu":
                w_glu = ein(pre + "w_glu", [D, 2 * D]); b_glu = ein(pre + "b_glu", [2 * D])
                T.glu(oall[ti - 1], w_glu, b_glu)
            else:
                w_out = ein(pre + "w_out", [kc * 128, D])
                T.outproj(oall[ti - 1], w_out, kc)
            T.layer_norm(s); s += 1
        for k in range(n_ffn):
            T.ffn(*ws[k]); T.layer_norm(s); s += 1
        if last or (2 * ti + 1 >= nstage):
            T.store_h(out32, None)
        else:
            T.store_h(hres, xin[ti])
            for k in range(4):
                P.cc_allgather(xin[ti][k], xall[ti][k], reads=[("xin", 2 * k), ("xin", 2 * k + 1)], writes=[("xall", k)])

    t_phase(0)
    for i in range(4):
        if 2 * i + 1 >= nstage:
            break
        P.phase_begin()
        pre = f"m{i}_"
        sv = P.sbuf([128, 2], F32, "selv")
        P.dma("sp", sv[:], selv_d, writes=[("selv",)], group="const")
        ctx = Ctx(nc, P, pre, xall=xall[i], oin=oin[i], selv=sv)
        if MIX[i] == "ssd":
            build_ssd(ctx)
        elif MIX[i] == "diff":
            build_diff(0.8 - 0.6 * math.exp(-0.3 * i), ctx)
        elif MIX[i] == "s5":
            build_s5(ctx)
        else:
            build_mla(ctx)
        for j in range(len(oin[i])):
            okeys = [k for k in P.last_w if isinstance(k, tuple) and len(k) > 1 and k[0] == "oin" and k[1] == j]
            P.cc_allgather(oin[i][j], oall[i][j], reads=okeys, writes=[("oall", j)])
        P.phase_begin()
        t_phase(i + 1)
    P.emit()
    return nc


def fused_inputs(d, r, hs, consts, nstage=9):
    c = np.ascontiguousarray
    b, half = r // 2, r % 2
    m = dict(hT=hs[r], selv=np.tile(np.array([[1.0, 0.0]] if half == 0 else [[0.0, 1.0]], np.float32), (128, 1)))

    def ffnw(pre, i, j, k):
        m[pre + f"w1_{k}"] = c(d['ffn_w1'][i, j]); m[pre + f"w3_{k}"] = c(d['ffn_w3'][i, j]); m[pre + f"w2_{k}"] = c(d['ffn_w2'][i, j])

    m["t0_ln_g"] = c(d['ln_g'][0, 0:1]); m["t0_ln_b"] = c(d['ln_b'][0, 0:1])
    ffnw("t0_", 0, 0, 0)
    for i in range(DEPTH):
        if 2 * i + 1 >= nstage:
            break
        pre = f"m{i}_"
        if i == 0:
            mi = ssd_inputs(d, None, half)
        elif i == 1:
            mi = diff_inputs(d, None, half, consts)
        elif i == 2:
            mi = s5_inputs(d, None, half)
        else:
            mi = mla_inputs(d, None, half)
        for k_, v in mi.items():
            if k_ in ("hbT", "hbh"):
                continue
            m[pre + k_] = v
        tp = f"t{i + 1}_"
        last = (i == DEPTH - 1)
        lng = [d['ln_g'][i, 1], d['ln_g'][i, 2]] + ([] if last else [d['ln_g'][i + 1, 0]])
        lnb = [d['ln_b'][i, 1], d['ln_b'][i, 2]] + ([] if last else [d['ln_b'][i + 1, 0]])
        m[tp + "ln_g"] = c(np.stack(lng)); m[tp + "ln_b"] = c(np.stack(lnb))
        ffnw(tp, i, 1, 0)
        if not last:
            ffnw(tp, i + 1, 0, 1)
        if i == 0:
            m[tp + "w_out"] = c(d['ssd_w_out'][0])
        elif i == 1:
            m[tp + "w_out"] = c(d['diff_w_out'][0])
        elif i == 2:
            m[tp + "w_glu"] = c(d['s5_w_glu'][0]); m[tp + "b_glu"] = c(d['s5_b_glu'][0])
        else:
            m[tp + "w_out"] = c(d['mla_w_out'][0])
    return m


NSTAGE = 9


def kernel(**inputs):
    d = {k: np.asarray(v) for k, v in inputs.items()}
    x, meta = d['x'], d['meta']
    c = np.ascontiguousarray
    hs = []
    for r in range(NCORES):
        b, half = r // 2, r % 2
        if half == 0:
            t = np.concatenate([meta, x[b, :2048]], 0)
        else:
            t = np.concatenate([np.zeros_like(meta), x[b, 2048:]], 0)
        hs.append(c(t.T))
    consts = diff_consts()
    nc = build_fused(NSTAGE)
    ims = [fused_inputs(d, r, hs, consts, NSTAGE) for r in range(NCORES)]
    res = _run(nc, ims)
    if _DBG is not None:
        return None
    out = np.empty((4, 4096, D), np.float32)
    for r in range(NCORES):
        b, half = r // 2, r % 2
        out[b, half * 2048:(half + 1) * 2048, :] = res[r]['out32'][:, 16:].T
    return out
```

```python
import math
from contextlib import ExitStack
import numpy as np
import ml_dtypes
import concourse.bass as bass
import concourse.mybir as mybir
from concourse.bass_utils import run_bass_kernel_spmd

F32 = mybir.dt.float32
BF16 = mybir.dt.bfloat16
AF = mybir.ActivationFunctionType
ALU = mybir.AluOpType
AX = mybir.AxisListType

D = 1024
DC = 8
FF = 2816
FC = 22
DEPTH = 4
ALPHA = (2.0 * DEPTH) ** 0.25
LN_EPS = 1e-5
RMS_EPS = 1e-6
NCORES = 8
DEBUG_OPS = False
NT = 2064
LSEQ = 4112
TCH = [(0, 512), (512, 512), (1024, 512), (1536, 512), (2048, 16)]


class Prog:
    ENGS = ("sp", "pe", "act", "dve", "pool")

    ARENA_F32 = 53000

    def __init__(self, nc, arena=False):
        self.nc = nc
        self.ops = []
        self.last_w = {}
        self.readers = {}
        self.es = ExitStack()
        self.ncnt = 0
        self.barriers = []
        self.arena = None
        if arena:
            self.arena = self.es.enter_context(nc.sbuf_tensor("arena", [128, self.ARENA_F32], F32))
            self.banks = [self.es.enter_context(nc.psum_tensor(f"bank{i}", [128, 512], F32)) for i in range(8)]
            self.aoff = 0
            self.nbank = 0

    def phase_begin(self):
        if DEBUG_OPS:
            print("phase end: arena bytes", self.aoff, "banks", self.nbank, "ops", len(self.ops))
        self.barriers.append(len(self.ops))
        self.last_w = {}
        self.readers = {}
        self.aoff = 0
        self.nbank = 0

    def sbuf(self, shape, dt, name=None):
        self.ncnt += 1
        if self.arena is None:
            return self.es.enter_context(self.nc.sbuf_tensor(name or f"sb{self.ncnt}", list(shape), dt))
        esz = 2 if dt == BF16 else 4
        n = 1
        for s in shape[1:]:
            n *= s
        nbytes = ((n * esz + 31) // 32) * 32
        w0 = self.aoff // 4
        self.aoff += nbytes
        assert self.aoff <= self.ARENA_F32 * 4, f"arena overflow allocating {name} {shape}: {self.aoff}"
        v = self.arena[0:shape[0], w0:w0 + nbytes // 4]
        if dt != F32:
            v = v.bitcast(dt)
        v = v[:, 0:n]
        if len(shape) == 3:
            v = v.rearrange("p (a b) -> p a b", b=shape[2])
        elif len(shape) == 4:
            v = v.rearrange("p (a b c) -> p a b c", b=shape[2], c=shape[3])
        return v

    def psum(self, shape, dt, name=None):
        self.ncnt += 1
        if self.arena is None:
            return self.es.enter_context(self.nc.psum_tensor(name or f"ps{self.ncnt}", list(shape), dt))
        assert dt == F32 and self.nbank < 8
        b = self.banks[self.nbank]
        self.nbank += 1
        return b[0:shape[0], 0:shape[1]]

    def cc_allgather(self, in_ap, out_ap, reads=(), writes=()):
        rg = [[0, 1], [2, 3], [4, 5], [6, 7]]
        return self.op("pool", lambda e: e.collective_compute("AllGather", ALU.bypass, replica_groups=rg,
                                                              ins=[in_ap.opt()], outs=[out_ap.opt()]),
                       reads, writes, group="__cc__")

    def op(self, eng, fn, reads=(), writes=(), group=None):
        i = len(self.ops)
        deps = set()
        for k in reads:
            w = self.last_w.get(k)
            if w is not None:
                deps.add(w)
        for k in writes:
            w = self.last_w.get(k)
            if w is not None:
                deps.add(w)
            for r in self.readers.get(k, ()):
                deps.add(r)
        for k in reads:
            self.readers.setdefault(k, []).append(i)
        for k in writes:
            self.last_w[k] = i
            self.readers[k] = []
        deps.discard(i)
        self.ops.append(dict(eng=eng, fn=fn, deps=sorted(deps), group=group))
        if DEBUG_OPS:
            import traceback
            self.ops[-1]["where"] = "".join(traceback.format_stack(limit=5)[:-1])
        return i

    def dma(self, eng, out, in_, reads=(), writes=(), group=None, slow=False):
        assert group is not None
        if slow:
            return self.op(eng, lambda e: e.dma_start(out=out, in_=in_, allow_slow_non_contiguous=True), reads, writes, group)
        return self.op(eng, lambda e: e.dma_start(out=out, in_=in_), reads, writes, group)


    def mm(self, out, lhsT, rhs, start, stop, reads=(), writes=()):
        return self.op("pe", lambda e: e.matmul(out, lhsT, rhs, start=start, stop=stop), reads, writes)

    def tr(self, out, in_, ident, reads=(), writes=()):
        return self.op("pe", lambda e: e.transpose(out, in_, ident), reads, writes)

    def act(self, out, in_, func, bias=None, scale=None, reads=(), writes=(), accum_out=None):
        kw = {}
        if bias is not None:
            kw["bias"] = bias
        if scale is not None:
            kw["scale"] = scale
        if accum_out is not None:
            kw["accum_out"] = accum_out
        return self.op("act", lambda e: e.activation(out, in_, func, **kw), reads, writes)

    def tt(self, eng, out, in0, in1, op, reads=(), writes=()):
        return self.op(eng, lambda e: e.tensor_tensor(out, in0, in1, op), reads, writes)

    def ts(self, eng, out, in0, s1, s2, op0, op1=None, reads=(), writes=(), accum_out=None):
        def f(e):
            kw = {}
            if accum_out is not None:
                kw["accum_out"] = accum_out
            if op1 is None:
                return e.tensor_scalar(out, in0, s1, None, op0, **kw)
            return e.tensor_scalar(out, in0, s1, s2, op0, op1, **kw)
        return self.op(eng, f, reads, writes)

    def stt(self, eng, out, in0, scalar, in1, op0, op1, reads=(), writes=()):
        return self.op(eng, lambda e: e.scalar_tensor_tensor(out, in0, scalar, in1, op0, op1), reads, writes)

    def copy(self, eng, out, in_, reads=(), writes=()):
        if eng == "act":
            return self.op("act", lambda e: e.copy(out, in_), reads, writes)
        return self.op(eng, lambda e: e.tensor_copy(out, in_), reads, writes)

    def memset(self, eng, ap, val, writes=()):
        return self.op(eng, lambda e: e.memset(ap, val), (), writes)

    NDS = 40

    def emit(self):
        nc = self.nc
        ops = self.ops
        needed = [False] * len(ops)

        def skip(p, o):
            return (p["group"] is None and o["group"] is None and p["eng"] == "pe" and o["eng"] == "pe")

        for o in ops:
            for d in o["deps"]:
                if not skip(ops[d], o):
                    needed[d] = True
        for b in self.barriers:
            for e in self.ENGS:
                for i in range(b - 1, -1, -1):
                    if ops[i]["eng"] == e and ops[i]["group"] is None:
                        needed[i] = True
                        break
        ecount = {e: 0 for e in self.ENGS}
        ndma = 0
        ncc = 0
        for i, o in enumerate(ops):
            o["idx"] = i
            if o["group"] == "__cc__":
                ncc += 1
                o["sem"] = ("cc",)
                o["val"] = ncc
            elif o["group"] is not None:
                o["sem"] = ("d", ndma % self.NDS)
                o["val"] = 16 * (ndma // self.NDS + 1)
                o["dn"] = ndma
                ndma += 1
            else:
                o["sem"] = ("e", o["eng"])
                if needed[i]:
                    ecount[o["eng"]] += 1
                    o["val"] = ecount[o["eng"]]
                else:
                    o["val"] = None
        sems = {}
        for e in self.ENGS:
            sems[("e", e)] = self.es.enter_context(nc.semaphore(f"sem_{e}"))
        for k in range(min(self.NDS, max(ndma, 1))):
            sems[("d", k)] = self.es.enter_context(nc.semaphore(f"semd_{k}"))
        sems[("cc",)] = self.es.enter_context(nc.semaphore("sem_cc"))
        bvals = []
        for b in self.barriers:
            vals = {}
            for o in ops[:b]:
                if o["val"] is not None:
                    vals[o["sem"]] = max(vals.get(o["sem"], 0), o["val"])
            bvals.append(vals)
        per = {e: [o for o in ops if o["eng"] == e] for e in self.ENGS}
        final = {}
        for o in ops:
            if o["group"] is not None:
                final[o["sem"]] = o["val"]
        final.pop(("cc",), None) if ncc == 0 else None

        def run(engobj, ename):
            waited = {}
            nb = 0
            for o in per[ename]:
                while nb < len(self.barriers) and o["idx"] >= self.barriers[nb]:
                    for key, val in bvals[nb].items():
                        if key == ("e", "pe") and ename == "pe":
                            continue
                        if waited.get(key, 0) < val:
                            engobj.wait_ge(sems[key], val)
                            waited[key] = val
                    nb += 1
                for d in o["deps"]:
                    p = ops[d]
                    if skip(p, o):
                        continue
                    key, val = p["sem"], p["val"]
                    if waited.get(key, 0) < val:
                        engobj.wait_ge(sems[key], val)
                        waited[key] = val
                if o["group"] is not None and o["group"] != "__cc__" and o["dn"] >= self.NDS:
                    key, val = o["sem"], o["val"] - 16
                    if waited.get(key, 0) < val:
                        engobj.wait_ge(sems[key], val)
                        waited[key] = val
                try:
                    ins = o["fn"](engobj)
                except Exception:
                    print("FAILED OP", o.get("idx"), o["eng"], o.get("where", ""))
                    raise
                if o["group"] == "__cc__":
                    ins.then_inc(sems[o["sem"]])
                elif o["group"] is not None:
                    ins.then_inc(sems[o["sem"]], 16)
                elif o["val"] is not None:
                    ins.then_inc(sems[o["sem"]], 1)
            if ename == "sp":
                for key, v in final.items():
                    if waited.get(key, 0) < v:
                        engobj.wait_ge(sems[key], v)

        with nc.Block() as block:
            @block.sync
            def _(e):
                run(e, "sp")

            @block.tensor
            def _(e):
                run(e, "pe")

            @block.scalar
            def _(e):
                run(e, "act")

            @block.vector
            def _(e):
                run(e, "dve")

            @block.gpsimd
            def _(e):
                run(e, "pool")
        self.es.close()


def bcast_rows(ap, n):
    return ap.partition_broadcast(n)


class TPhase:
    def __init__(self, P, nc):
        self.P = P
        self.nc = nc
        self.h32 = P.sbuf([128, DC, NT], F32, "h32")
        self.hbf = P.sbuf([128, DC, NT], BF16, "hbf")
        self.wbuf = P.sbuf([128, 16384], BF16, "wbuf")
        self.gbuf = P.sbuf([128, 12384], BF16, "gbuf")
        self.w13s = [P.sbuf([128, 2, DC, 128], F32, f"w13s{i}") for i in range(2)]
        self.w13b = [P.sbuf([128, 2, DC, 128], BF16, f"w13b{i}") for i in range(2)]
        self.w2s = [P.sbuf([128, 1024], F32, f"w2s{i}") for i in range(2)]
        self.sq = [P.sbuf([128, 512], F32, f"sq{i}") for i in range(2)]
        self.t1 = [P.sbuf([128, 512], F32, f"t1_{i}") for i in range(2)]
        self.sg = [P.sbuf([128, 512], F32, f"sg{i}") for i in range(2)]
        self.st_m = P.sbuf([128, 512], F32, "st_m")
        self.st_r = P.sbuf([128, 512], F32, "st_r")
        self.st_v = P.sbuf([128, 512], F32, "st_v")
        self.ones = P.sbuf([128, 128], F32, "ones")
        self.lng = P.sbuf([128, 3 * DC], F32, "lng")
        self.lnb = P.sbuf([128, 3 * DC], F32, "lnb")
        self.bglu = P.sbuf([128, 16], F32, "bglu")
        self.ps_a = [P.psum([128, 512], F32, f"ps_a{i}") for i in range(2)]
        self.ps_b = [P.psum([128, 512], F32, f"ps_b{i}") for i in range(2)]
        self.ps_o = [P.psum([128, 512], F32, f"ps_o{i}") for i in range(2)]
        self.ps_m = P.psum([128, 512], F32, "ps_m")
        self.ps_q = P.psum([128, 512], F32, "ps_q")
        self.cnt = 0
        self.c_w13 = 0
        self.c_up = 0
        P.memset("pool", self.ones[:], 1.0 / D, writes=[("ones",)])

    def nxt(self):
        self.cnt += 1
        return self.cnt - 1

    def load_h(self, hT_dram):
        P = self.P
        for c in range(DC):
            P.dma("sp", self.h32[:, c, :], hT_dram[c * 128:(c + 1) * 128, :],
                  writes=[("h32", c, t) for t in range(5)], group=f"hload{c % 4}")
            for ti, (t0, tn) in enumerate(TCH):
                eng = "act" if (ti % 2 == 0) else "dve"
                P.copy(eng, self.hbf[:, c, t0:t0 + tn], self.h32[:, c, t0:t0 + tn],
                       reads=[("h32", c, ti)], writes=[("hbf", c, ti)])

    def load_ln(self, g_dram, b_dram, nsets):
        P = self.P
        for s in range(nsets):
            P.dma("sp", self.lng[:, s * DC:(s + 1) * DC], g_dram[s, :].rearrange("(c p) -> p c", p=128),
                  writes=[("lng",)], group="const", slow=True)
            P.dma("sp", self.lnb[:, s * DC:(s + 1) * DC], b_dram[s, :].rearrange("(c p) -> p c", p=128),
                  writes=[("lnb",)], group="const", slow=True)

    def scale_h(self, alpha):
        P = self.P
        for c in range(DC):
            for ti, (t0, tn) in enumerate(TCH):
                P.ts("pool", self.h32[:, c, t0:t0 + tn], self.h32[:, c, t0:t0 + tn], float(alpha), None, ALU.mult,
                     reads=[("h32", c, ti)], writes=[("h32", c, ti)])

    def layer_norm(self, lnset):
        P = self.P
        for ti, (t0, tn) in enumerate(TCH):
            for c in range(DC):
                k = self.nxt() % 2
                sq = self.sq[k]
                P.act(sq[:, 0:tn], self.h32[:, c, t0:t0 + tn], AF.Square,
                      reads=[("h32", c, ti)], writes=[("sq", k)])
                P.mm(self.ps_m[:, 0:tn], self.ones[:], self.h32[:, c, t0:t0 + tn], c == 0, c == DC - 1,
                     reads=[("h32", c, ti), ("ones",)], writes=[("ps_m",)])
                P.mm(self.ps_q[:, 0:tn], self.ones[:], sq[:, 0:tn], c == 0, c == DC - 1,
                     reads=[("sq", k), ("ones",)], writes=[("ps_q",)])
            P.copy("act", self.st_m[:, 0:tn], self.ps_m[:, 0:tn], reads=[("ps_m",)], writes=[("st_m",)])
            P.tt("dve", self.st_v[:, 0:tn], self.st_m[:, 0:tn], self.st_m[:, 0:tn], ALU.mult,
                 reads=[("st_m",)], writes=[("st_v",)])
            P.tt("dve", self.st_v[:, 0:tn], self.ps_q[:, 0:tn], self.st_v[:, 0:tn], ALU.subtract,
                 reads=[("ps_q",), ("st_v",)], writes=[("st_v",)])
            P.ts("dve", self.st_v[:, 0:tn], self.st_v[:, 0:tn], LN_EPS, None, ALU.add,
                 reads=[("st_v",)], writes=[("st_v",)])
            P.act(self.st_v[:, 0:tn], self.st_v[:, 0:tn], AF.Sqrt, reads=[("st_v",)], writes=[("st_v",)])
            P.op("dve", lambda e, o=self.st_r[:, 0:tn], i=self.st_v[:, 0:tn]: e.reciprocal(o, i),
                 reads=[("st_v",)], writes=[("st_r",)])
            for c in range(DC):
                k = self.nxt() % 2
                t1 = self.t1[k]
                P.tt("dve", t1[:, 0:tn], self.h32[:, c, t0:t0 + tn], self.st_m[:, 0:tn], ALU.subtract,
                     reads=[("h32", c, ti), ("st_m",)], writes=[("t1", k)])
                P.tt("dve", t1[:, 0:tn], t1[:, 0:tn], self.st_r[:, 0:tn], ALU.mult,
                     reads=[("t1", k), ("st_r",)], writes=[("t1", k)])
                col = lnset * DC + c
                P.act(self.h32[:, c, t0:t0 + tn], t1[:, 0:tn], AF.Identity,
                      bias=self.lnb[:, col:col + 1], scale=self.lng[:, col:col + 1],
                      reads=[("t1", k), ("lng",), ("lnb",)], writes=[("h32", c, ti)])
                P.act(self.hbf[:, c, t0:t0 + tn], t1[:, 0:tn], AF.Identity,
                      bias=self.lnb[:, col:col + 1], scale=self.lng[:, col:col + 1],
                      reads=[("t1", k), ("lng",), ("lnb",)], writes=[("hbf", c, ti)])

    def ffn(self, w1, w3, w2):
        P = self.P
        groups = [(0, 6), (6, 6), (12, 5), (17, 5)]
        GSTR = 2064
        for gi, (f0, nf) in enumerate(groups):
            wsl = gi % 2
            for j in range(nf):
                fc = f0 + j
                k = self.nxt() % 2
                P.dma("sp", self.w2s[k][:], w2[fc * 128:(fc + 1) * 128, :],
                      writes=[("w2s", k)], group=f"w2s{k}")
                off = wsl * 8192 + j * 1024
                P.copy("act", self.wbuf[:, off:off + 1024], self.w2s[k][:],
                       reads=[("w2s", k)], writes=[("wbuf", wsl, j)])
            for j in range(nf):
                fc = f0 + j
                s = self.c_w13 % 2
                self.c_w13 += 1
                P.dma("sp", self.w13s[s][:, 0, :, :], w1[:, fc * 128:(fc + 1) * 128].rearrange("(c p) f -> p c f", p=128),
                      writes=[("w13s", s, 0)], group=f"w13s{s}a")
                P.dma("sp", self.w13s[s][:, 1, :, :], w3[:, fc * 128:(fc + 1) * 128].rearrange("(c p) f -> p c f", p=128),
                      writes=[("w13s", s, 1)], group=f"w13s{s}b")
                P.copy("act", self.w13b[s][:, 0, :, :], self.w13s[s][:, 0, :, :],
                       reads=[("w13s", s, 0)], writes=[("w13b", s, 0)])
                P.copy("pool", self.w13b[s][:, 1, :, :], self.w13s[s][:, 1, :, :],
                       reads=[("w13s", s, 1)], writes=[("w13b", s, 1)])
                for ti, (t0, tn) in enumerate(TCH):
                    kk = self.c_up % 2
                    self.c_up += 1
                    pa, pb, sg = self.ps_a[kk], self.ps_b[kk], self.sg[kk]
                    for c in range(DC):
                        P.mm(pa[:, 0:tn], self.w13b[s][:, 0, c, :], self.hbf[:, c, t0:t0 + tn], c == 0, c == DC - 1,
                             reads=[("w13b", s, 0), ("hbf", c, ti)], writes=[("ps_a", kk)])
                    for c in range(DC):
                        P.mm(pb[:, 0:tn], self.w13b[s][:, 1, c, :], self.hbf[:, c, t0:t0 + tn], c == 0, c == DC - 1,
                             reads=[("w13b", s, 1), ("hbf", c, ti)], writes=[("ps_b", kk)])
                    P.act(sg[:, 0:tn], pa[:, 0:tn], AF.Silu, reads=[("ps_a", kk)], writes=[("sg", kk)])
                    goff = j * GSTR + t0
                    P.stt("dve", self.gbuf[:, goff:goff + tn], sg[:, 0:tn], 0.5, pb[:, 0:tn], ALU.mult, ALU.mult,
                          reads=[("sg", kk), ("ps_b", kk)], writes=[("gbuf", j, ti)])
            for ti, (t0, tn) in enumerate(TCH):
                for c in range(DC):
                    kk = self.nxt() % 2
                    po = self.ps_o[kk]
                    for j in range(nf):
                        off = wsl * 8192 + j * 1024 + c * 128
                        goff = j * GSTR + t0
                        P.mm(po[:, 0:tn], self.wbuf[:, off:off + 128], self.gbuf[:, goff:goff + tn], j == 0, j == nf - 1,
                             reads=[("wbuf", wsl, j), ("gbuf", j, ti)], writes=[("ps_o", kk)])
                    P.stt("dve", self.h32[:, c, t0:t0 + tn], self.h32[:, c, t0:t0 + tn], float(ALPHA) if gi == 0 else 1.0, po[:, 0:tn], ALU.mult, ALU.add,
                          reads=[("ps_o", kk), ("h32", c, ti)], writes=[("h32", c, ti)])

    selv = None

    def load_o(self, oT_dram, j, t0, tn):
        P = self.P
        gk = [("gbuf", jj, tt_) for jj in range(6) for tt_ in range(5)] + [("gbo", j)]
        if self.selv is None:
            P.dma("sp", self.gbuf[:, j * 512:j * 512 + tn], oT_dram[j * 128:(j + 1) * 128, t0:t0 + tn], writes=gk, group="gbo")
            return
        k = self.nxt() % 2
        stg = self.t1[k][:].bitcast(BF16)
        nh = len(oT_dram)
        srcj = oT_dram[j % nh][(j // nh) * 128:(j // nh) * 128 + 128, :]
        P.dma("sp", stg[:, 0:tn], srcj[:, t0:t0 + tn], reads=[("oall", j % nh)], writes=[("t1", k)], group="gbo")
        P.dma("sp", stg[:, 512:512 + tn], srcj[:, 2048 + t0:2048 + t0 + tn], reads=[("oall", j % nh)], writes=[("t1B", k)], group="gbo")
        P.ts("dve", stg[:, 0:tn], stg[:, 0:tn], self.selv[:, 0:1], None, ALU.mult, reads=[("t1", k), ("selv",)], writes=[("t1", k)])
        P.stt("dve", self.gbuf[:, j * 512:j * 512 + tn], stg[:, 512:512 + tn], self.selv[:, 1:2], stg[:, 0:tn], ALU.mult, ALU.add,
              reads=[("t1", k), ("t1B", k), ("selv",)], writes=gk)

    def outproj(self, oT_dram, w_out, kc):
        P = self.P
        for j in range(kc):
            k = self.nxt() % 2
            P.dma("sp", self.w2s[k][:], w_out[j * 128:(j + 1) * 128, :], writes=[("w2s", k)], group=f"w2s{k}")
            P.copy("act", self.wbuf[:, j * 1024:(j + 1) * 1024], self.w2s[k][:], reads=[("w2s", k)],
                   writes=[("wbuf", 0, jj) for jj in range(6)] + [("wbuf", 1, jj) for jj in range(6)] + [("wbo", j)])
        for ti, (t0, tn) in enumerate(TCH):
            for j in range(kc):
                self.load_o(oT_dram, j, t0, tn)
            for c in range(DC):
                kk = self.nxt() % 2
                po = self.ps_o[kk]
                for j in range(kc):
                    P.mm(po[:, 0:tn], self.wbuf[:, j * 1024 + c * 128:j * 1024 + (c + 1) * 128], self.gbuf[:, j * 512:j * 512 + tn],
                         j == 0, j == kc - 1, reads=[("wbo", j), ("gbo", j)], writes=[("ps_o", kk)])
                P.stt("dve", self.h32[:, c, t0:t0 + tn], self.h32[:, c, t0:t0 + tn], float(ALPHA), po[:, 0:tn], ALU.mult, ALU.add,
                      reads=[("ps_o", kk), ("h32", c, ti)], writes=[("h32", c, ti)])
            P.op("pool", lambda e, a=self.sq[0][0:1, 0:1]: e.memset(a, 0.0),
                 reads=[("gbo", j) for j in range(kc)], writes=[("gbuf", jj, tt_) for jj in range(6) for tt_ in range(5)] + [("sq", 0)])
        P.op("pool", lambda e, a=self.sq[0][0:1, 0:1]: e.memset(a, 0.0),
             reads=[("wbo", j) for j in range(kc)], writes=[("wbuf", 0, jj) for jj in range(6)] + [("wbuf", 1, jj) for jj in range(6)] + [("sq", 0)])

    def glu(self, oT_dram, w_glu, b_glu):
        P = self.P
        P.dma("sp", self.bglu[:], b_glu.rearrange("(c p) -> p c", p=128), writes=[("bglu",)], group="const", slow=True)
        wv = self.wbuf[:].rearrange("p (k f) -> p k f", f=2048)
        for j in range(8):
            for hf in range(2):
                k = self.nxt() % 2
                P.dma("sp", self.w2s[k][:], w_glu[j * 128:(j + 1) * 128, hf * 1024:(hf + 1) * 1024], writes=[("w2s", k)], group=f"w2s{k}")
                P.copy("act", wv[:, j, hf * 1024:(hf + 1) * 1024], self.w2s[k][:], reads=[("w2s", k)],
                       writes=[("wbuf", 0, jj) for jj in range(6)] + [("wbuf", 1, jj) for jj in range(6)] + [("wbo", j, hf)])
        for ti, (t0, tn) in enumerate(TCH):
            for j in range(8):
                self.load_o(oT_dram, j, t0, tn)
            for c in range(DC):
                kk = self.nxt() % 2
                pa, pb, sg, t1 = self.ps_a[kk], self.ps_b[kk], self.sg[kk], self.t1[kk]
                for j in range(8):
                    P.mm(pa[:, 0:tn], wv[:, j, c * 128:(c + 1) * 128], self.gbuf[:, j * 512:j * 512 + tn], j == 0, j == 7,
                         reads=[("wbo", j, 0), ("gbo", j)], writes=[("ps_a", kk)])
                for j in range(8):
                    P.mm(pb[:, 0:tn], wv[:, j, 1024 + c * 128:1024 + (c + 1) * 128], self.gbuf[:, j * 512:j * 512 + tn], j == 0, j == 7,
                         reads=[("wbo", j, 1), ("gbo", j)], writes=[("ps_b", kk)])
                P.act(sg[:, 0:tn], pb[:, 0:tn], AF.Sigmoid, bias=self.bglu[:, 8 + c:9 + c], reads=[("ps_b", kk), ("bglu",)], writes=[("sg", kk)])
                P.stt("dve", t1[:, 0:tn], pa[:, 0:tn], self.bglu[:, c:c + 1], sg[:, 0:tn], ALU.add, ALU.mult,
                      reads=[("ps_a", kk), ("sg", kk), ("bglu",)], writes=[("t1", kk)])
                P.stt("dve", self.h32[:, c, t0:t0 + tn], self.h32[:, c, t0:t0 + tn], float(ALPHA), t1[:, 0:tn], ALU.mult, ALU.add,
                      reads=[("t1", kk), ("h32", c, ti)], writes=[("h32", c, ti)])
            P.op("pool", lambda e, a=self.sq[0][0:1, 0:1]: e.memset(a, 0.0),
                 reads=[("gbo", j) for j in range(8)], writes=[("gbuf", jj, tt_) for jj in range(6) for tt_ in range(5)] + [("sq", 0)])
        P.op("pool", lambda e, a=self.sq[0][0:1, 0:1]: e.memset(a, 0.0),
             reads=[("wbo", j, hf) for j in range(8) for hf in range(2)],
             writes=[("wbuf", 0, jj) for jj in range(6)] + [("wbuf", 1, jj) for jj in range(6)] + [("sq", 0)])

    def store_h(self, out32, outbf):
        P = self.P
        for c in range(DC):
            if out32 is not None:
                P.dma("sp", out32[c * 128:(c + 1) * 128, :], self.h32[:, c, :],
                      reads=[("h32", c, t) for t in range(5)], group="out")
            if outbf is not None:
                dst = outbf[c // 2][(c % 2) * 128:(c % 2) * 128 + 128, :] if isinstance(outbf, list) else outbf[c * 128:(c + 1) * 128, :]
                P.dma("sp", dst, self.hbf[:, c, :],
                      reads=[("hbf", c, t) for t in range(5)], writes=[("xin", c)], group="out")


def build_t0():
    nc = bass.Bass("TRN2", target_bir_lowering=False)
    hT = nc.dram_tensor("hT", [D, NT], F32, kind="ExternalInput").ap()
    w1 = nc.dram_tensor("w1", [D, FF], F32, kind="ExternalInput").ap()
    w3 = nc.dram_tensor("w3", [D, FF], F32, kind="ExternalInput").ap()
    w2 = nc.dram_tensor("w2", [FF, D], F32, kind="ExternalInput").ap()
    lg = nc.dram_tensor("ln_g", [1, D], F32, kind="ExternalInput").ap()
    lb = nc.dram_tensor("ln_b", [1, D], F32, kind="ExternalInput").ap()
    o32 = nc.dram_tensor("o32", [D, NT], F32, kind="ExternalOutput").ap()
    obf = nc.dram_tensor("obf", [D, NT], BF16, kind="ExternalOutput").ap()
    P = Prog(nc)
    T = TPhase(P, nc)
    T.load_ln(lg, lb, 1)
    T.load_h(hT)
    T.ffn(w1, w3, w2)
    T.layer_norm(0)
    T.store_h(o32, obf)
    P.emit()
    return nc


QCH = [(0, 16)] + [(16 + 512 * i, 512) for i in range(8)]
KBL = [(0, 16)] + [(16 + 128 * i, 128) for i in range(32)]
NEG = -30000.0


def load_cast_w(P, dst_bf, src_dram, stg, rows_chunks, cols, tagkey, eng="act", colblk=1024):
    k = 0
    for c in range(rows_chunks):
        for c0 in range(0, cols, colblk):
            cn = min(colblk, cols - c0)
            s = k % len(stg)
            k += 1
            P.dma("sp", stg[s][:, 0:cn], src_dram[c * 128:(c + 1) * 128, c0:c0 + cn],
                  writes=[("stg", id(stg[s]))], group=f"stg{id(stg[s])}")
            P.copy(eng, dst_bf[:, c, c0:c0 + cn], stg[s][:, 0:cn],
                   reads=[("stg", id(stg[s]))], writes=[tagkey])


class Ctx:
    def __init__(self, nc, P, pre, xall=None, oin=None, selv=None):
        self.nc, self.P, self.pre, self.xall, self.oin, self.selv = nc, P, pre, xall, oin, selv

    def x_rows(self, rank, c):
        k, w = c // 2, (c % 2) * 128
        return self.xall[k][rank * 256 + w:rank * 256 + w + 128, :]


class ORows:
    def __init__(self, oT, chunks=None):
        self.oT, self.chunks = oT, chunks

    def rows(self, r0, n):
        if self.chunks is None:
            return self.oT[r0:r0 + n, :]
        j, w = r0 // 128, r0 % 128
        assert w + n <= 128
        return self.chunks[j][w:w + n, :]


def _load_seq(P, ctx, dst, nch, hbT, keyfn):
    for c in range(nch):
        if ctx is None:
            P.dma("sp", dst[:, c, :], hbT[c * 128:(c + 1) * 128, :], writes=keyfn(c, 0) + keyfn(c, 1), group="hb")
        else:
            P.dma("sp", dst[:, c, 0:NT], ctx.x_rows(0, c), reads=[("xall", c // 2)], writes=keyfn(c, 0), group="hb")
            P.dma("sp", dst[:, c, NT:LSEQ], ctx.x_rows(1, c)[:, 16:NT], reads=[("xall", c // 2)], writes=keyfn(c, 1), group="hb")


class AttnCore:
    def __init__(self, P, nmaps, E, s_depth=2):
        self.P = P
        self.nmaps = nmaps
        self.E = E
        self.sd = s_depth
        self.psS = [[P.psum([128, 512], F32, f"psS{m}_{i}") for i in range(s_depth)] for m in range(nmaps)]
        self.po = [P.psum([128, 512], F32, f"po{m}") for m in range(nmaps)]
        self.pn = [P.psum([128, 512], F32, f"pn{m}") for m in range(nmaps)]
        self.PT = [[P.sbuf([128, 512], BF16, f"PT{m}_{i}") for i in range(4)] for m in range(nmaps)]
        self.tmp = [[P.sbuf([128, 256], F32, f"atmp{m}_{i}") for i in range(2)] for m in range(nmaps)]
        self.onesb = P.sbuf([128, 128], BF16, "onesb")
        P.memset("pool", self.onesb[:], 1.0, writes=[("onesb",)])
        self.kS = [0] * nmaps
        self.kP = [0] * nmaps
        self.kT = [0] * nmaps

    def chunk(self, c, kT, qT, kd_base, kd, V, vkeys, sc, tiles, far_bias, keysK, keysQ, keysV, pv_extra=None):
        P = self.P
        E = self.E
        qs, nq = QCH[c]
        if c == 0:
            kbs = [0]
        else:
            kbs = list(range(0, 4 * c + 1))
        qb0 = 4 * (c - 1) + 1
        work = []
        for kb in kbs:
            ks, nk = KBL[kb]
            last = (kb == kbs[-1])
            specials = []
            if c == 0:
                cs = 0
                specials.append((0, 16, tiles["md"][0:16, 0:16]))
                cf = 16
            elif kb == 0:
                cs = 0
                cf = 0
                if c == 1 and tiles.get("mq1") is not None:
                    specials.append((0, 128, tiles["mq1"][0:16, 0:128]))
                    cf = 128
            else:
                i0 = max(0, kb - qb0)
                cs = 128 * i0
                d0 = qb0 + i0 - kb
                cf = cs
                if d0 == 0:
                    w = 128
                    if tiles["prev"] and cs + 256 <= nq:
                        w = 256
                    specials.append((cs, w, tiles["dd"][:, 0:w]))
                    cf = cs + w
                elif d0 == 1 and tiles["prev"]:
                    specials.append((cs, 128, tiles["dd"][:, 128:256]))
                    cf = cs + 128
            for m in range(self.nmaps):
                bS = self.kS[m] % self.sd
                self.kS[m] += 1
                bP = self.kP[m] % 4
                self.kP[m] += 1
                ps = self.psS[m][bS]
                PT = self.PT[m][bP]
                pkey = ("psS", m, bS)
                tkey = ("PT", m, bP)

                def emit_S(kb=kb, m=m, ps=ps, pkey=pkey, nk=nk, cs=cs):
                    P.mm(ps[0:nk, cs:nq], kT(m, kb), qT(m)[:, cs:nq], True, True,
                         reads=keysK(m, kb) + keysQ(m), writes=[pkey])

                def emit_post(kb=kb, m=m, ps=ps, PT=PT, pkey=pkey, tkey=tkey, nk=nk, cs=cs, cf=cf, specials=specials, last=last):
                    for (c0, w, tap) in specials:
                        bT = self.kT[m] % 2
                        self.kT[m] += 1
                        t = self.tmp[m][bT]
                        P.stt("dve", t[0:nk, 0:w], ps[0:nk, c0:c0 + w], float(sc), tap, ALU.mult, ALU.add,
                              reads=[pkey, ("tiles",)], writes=[("atmp", m, bT)])
                        P.act(PT[0:nk, c0:c0 + w], t[0:nk, 0:w], AF.Exp, reads=[("atmp", m, bT)], writes=[tkey])
                    if cf < nq:
                        fb = far_bias[m] if isinstance(far_bias, (list, tuple)) else far_bias
                        if fb is not None:
                            P.act(PT[0:nk, cf:nq], ps[0:nk, cf:nq], AF.Exp, bias=fb[0:nk, :], scale=float(sc),
                                  reads=[pkey, ("tiles",)], writes=[tkey])
                        else:
                            P.act(PT[0:nk, cf:nq], ps[0:nk, cf:nq], AF.Exp, scale=float(sc), reads=[pkey], writes=[tkey])
                    P.mm(self.po[m][0:E, cs:nq], V(kb), PT[0:nk, cs:nq], kb == 0, last,
                         reads=[tkey] + keysV(kb), writes=[("po", m)])
                    P.mm(self.pn[m][0:E, cs:nq], self.onesb[0:nk, 0:E], PT[0:nk, cs:nq], kb == 0, last,
                         reads=[tkey, ("onesb",)], writes=[("pn", m)])
                work.append((emit_S, emit_post))
        LA = self.nmaps * (self.sd - 1)
        for i in range(len(work) + LA):
            if i < len(work):
                work[i][0]()
            if i - LA >= 0:
                work[i - LA][1]()


def build_mla(ctx=None):
    nc = bass.Bass("TRN2", target_bir_lowering=False) if ctx is None else ctx.nc
    pre = "" if ctx is None else ctx.pre
    dt = lambda n, s, d=F32: nc.dram_tensor(pre + n, s, d, kind="ExternalInput").ap()
    hbT = dt("hbT", [D, LSEQ], BF16) if ctx is None else None
    w_in = dt("w_in_ext", [D, 832])
    w_uq = dt("w_uq", [384, 768])
    w_uqs = dt("w_uq_sw", [384, 768])
    w_uk = dt("w_uk", [256, 512])
    w_uv = dt("w_uv", [256, 512])
    qg = dt("qg", [384])
    kvg = dt("kvg", [256])
    ropeC = dt("ropeC", [32, LSEQ])
    ropeS = dt("ropeS", [32, LSEQ])
    mask_dd = dt("mask_dd", [128, 128])
    oT = ORows(nc.dram_tensor("oT", [512, LSEQ], BF16, kind="ExternalOutput").ap()) if ctx is None else ORows(None, ctx.oin)
    P = Prog(nc) if ctx is None else ctx.P
    big = P.sbuf([128, 8, LSEQ], BF16, "big")
    cqn = P.sbuf([128, 3, LSEQ], BF16, "cqn")
    ckvn = P.sbuf([128, 2, LSEQ], BF16, "ckvn")
    Vt = P.sbuf([128, 33, 512], BF16, "Vt")
    winb = P.sbuf([128, 8, 832], BF16, "winb")
    wuqb = P.sbuf([128, 3, 768], BF16, "wuqb")
    wuqsb = P.sbuf([128, 3, 768], BF16, "wuqsb")
    wukb = P.sbuf([128, 2, 512], BF16, "wukb")
    wuvb = P.sbuf([128, 2, 512], BF16, "wuvb")
    stg = [P.sbuf([128, 1024], F32, f"stg{i}") for i in range(1)]
    c32 = [P.sbuf([128, 512], F32, f"c32_{i}") for i in range(5)]
    rs = [P.sbuf([128, 512], F32, f"rs{i}") for i in range(1)]
    tC = P.sbuf([128, 512], F32, "tC")
    tS = P.sbuf([128, 512], F32, "tS")
    r1 = P.sbuf([128, 512], F32, "r1")
    r2 = P.sbuf([128, 512], F32, "r2")
    qTt = [P.sbuf([128, 512], BF16, f"qT{i}") for i in range(2)]
    osb = [P.sbuf([128, 512], BF16, f"osb{i}") for i in range(2)]
    rcp = P.sbuf([128, 512], F32, "rcp")
    onesq = P.sbuf([128, 128], F32, "onesq")
    oneskv = P.sbuf([128, 128], F32, "oneskv")
    gq = P.sbuf([128, 3], F32, "gq")
    gkv = P.sbuf([128, 2], F32, "gkv")
    mdd = P.sbuf([128, 128], F32, "mdd")
    A = AttnCore(P, 1, 64, s_depth=3)
    psx = [P.psum([128, 512], F32, f"psx{i}") for i in range(2)]
    pss = P.psum([128, 512], F32, "pss")
    P.memset("pool", onesq[:], 1.0 / 384, writes=[("onesq",)])
    P.memset("pool", oneskv[:], 1.0 / 256, writes=[("oneskv",)])
    P.dma("sp", gq[:], qg.rearrange("(c p) -> p c", p=128), writes=[("gq",)], group="const", slow=True)
    P.dma("sp", gkv[:], kvg.rearrange("(c p) -> p c", p=128), writes=[("gkv",)], group="const", slow=True)
    P.dma("sp", mdd[:], mask_dd, writes=[("tiles",)], group="const")
    _load_seq(P, ctx, big, 8, hbT, lambda c, part: [("big", c, t) for t in (range(0, 5) if part == 0 else range(5, 9))])
    load_cast_w(P, winb, w_in, stg, 8, 832, ("winb",))
    load_cast_w(P, wuqb, w_uq, stg, 3, 768, ("wuqb",))
    load_cast_w(P, wuqsb, w_uqs, stg, 3, 768, ("wuqsb",))
    load_cast_w(P, wukb, w_uk, stg, 2, 512, ("wukb",))
    load_cast_w(P, wuvb, w_uv, stg, 2, 512, ("wuvb",))
    cnt = [0]

    def nxt():
        cnt[0] += 1
        return cnt[0] - 1

    pxc = [0]

    def nxp():
        pxc[0] += 1
        return pxc[0] - 1

    for ti, (t0, tn) in enumerate(QCH):
        for grp, (f0, nf, ones_t, okey, gt, dst) in enumerate([(0, 3, onesq, ("onesq",), gq, cqn), (3, 2, oneskv, ("oneskv",), gkv, ckvn)]):
            for f in range(nf):
                kx = nxp() % 2
                ps = psx[kx]
                for c in range(8):
                    P.mm(ps[:, 0:tn], winb[:, c, (f0 + f) * 128:(f0 + f + 1) * 128], big[:, c, t0:t0 + tn], c == 0, c == 7,
                         reads=[("winb",), ("big", c, ti)], writes=[("psx", kx)])
                cc = c32[f0 + f]
                P.copy("act", cc[:, 0:tn], ps[:, 0:tn], reads=[("psx", kx)], writes=[("c32", f0 + f)])
                ks = nxt() % 2
                sqt = [r1, r2][ks]
                sqk = [("r1",), ("r2",)][ks]
                P.tt("dve", sqt[:, 0:tn], cc[:, 0:tn], cc[:, 0:tn], ALU.mult, reads=[("c32", f0 + f)], writes=[sqk])
                P.mm(pss[:, 0:tn], ones_t[:], sqt[:, 0:tn], f == 0, f == nf - 1,
                     reads=[sqk, okey], writes=[("pss",)])
            kr = 0
            P.ts("dve", rs[kr][:, 0:tn], pss[:, 0:tn], RMS_EPS, None, ALU.add, reads=[("pss",)], writes=[("rs", kr)])
            P.act(rs[kr][:, 0:tn], rs[kr][:, 0:tn], AF.Sqrt, reads=[("rs", kr)], writes=[("rs", kr)])
            P.op("dve", lambda e, o=rs[kr][:, 0:tn]: e.reciprocal(o, o), reads=[("rs", kr)], writes=[("rs", kr)])
            for f in range(nf):
                cc = c32[f0 + f]
                P.tt("dve", cc[:, 0:tn], cc[:, 0:tn], rs[kr][:, 0:tn], ALU.mult, reads=[("c32", f0 + f), ("rs", kr)], writes=[("c32", f0 + f)])
                P.act(dst[:, f, t0:t0 + tn], cc[:, 0:tn], AF.Identity, scale=gt[:, f:f + 1],
                      reads=[("c32", f0 + f), ("gq",), ("gkv",)], writes=[(id(dst), f, ti)])
        P.dma("sp", tC[64:96, 0:tn], ropeC[:, t0:t0 + tn], writes=[("tC",)], group="tC")
        P.dma("sp", tS[64:96, 0:tn], ropeS[:, t0:t0 + tn], writes=[("tS",)], group="tS")
        ka, kb_ = nxp() % 2, None
        psa = psx[ka]
        for c in range(8):
            P.mm(psa[0:96, 0:tn], winb[:, c, 640:736], big[:, c, t0:t0 + tn], c == 0, c == 7,
                 reads=[("winb",), ("big", c, ti)], writes=[("psx", ka)])
        P.tt("dve", r1[64:96, 0:tn], psa[64:96, 0:tn], tC[64:96, 0:tn], ALU.mult, reads=[("psx", ka), ("tC",)], writes=[("r1",)])
        kb_ = nxp() % 2
        psb = psx[kb_]
        for c in range(8):
            P.mm(psb[0:96, 0:tn], winb[:, c, 736:832], big[:, c, t0:t0 + tn], c == 0, c == 7,
                 reads=[("winb",), ("big", c, ti)], writes=[("psx", kb_)])
        P.tt("dve", r2[64:96, 0:tn], psb[64:96, 0:tn], tS[64:96, 0:tn], ALU.mult, reads=[("psx", kb_), ("tS",)], writes=[("r2",)])
        P.tt("dve", r1[64:96, 0:tn], r1[64:96, 0:tn], r2[64:96, 0:tn], ALU.add, reads=[("r1",), ("r2",)], writes=[("r1",)])
        for h in range(8):
            P.copy("pool" if h % 2 else "act", big[64:96, h, t0:t0 + tn], r1[64:96, 0:tn],
                   reads=[("r1",)] + [("big", cc_, ti) for cc_ in range(8)], writes=[("bigr", h, ti)])
    for h in range(8):
        for ti, (t0, tn) in enumerate(QCH):
            kx = nxp() % 2
            ps = psx[kx]
            for kc in range(2):
                P.mm(ps[0:64, 0:tn], wukb[:, kc, h * 64:(h + 1) * 64], ckvn[:, kc, t0:t0 + tn], kc == 0, kc == 1,
                     reads=[("wukb",), (id(ckvn), kc, ti)], writes=[("psx", kx)])
            P.copy("act" if h % 2 else "dve", big[0:64, h, t0:t0 + tn], ps[0:64, 0:tn],
                   reads=[("psx", kx)] + [("big", cc_, ti) for cc_ in range(8)], writes=[("bign", h, ti)])
    for kb, (ks, nk) in enumerate(KBL):
        kx = nxp() % 2
        ps = psx[kx]
        ti = 0 if kb == 0 else 1 + (kb - 1) // 4
        for kc in range(2):
            P.mm(ps[0:nk, 0:512], ckvn[:, kc, ks:ks + nk], wuvb[:, kc, :], kc == 0, kc == 1,
                 reads=[("wuvb",), (id(ckvn), kc, ti)], writes=[("psx", kx)])
        P.copy("act" if kb % 2 else "dve", Vt[0:nk, kb, :], ps[0:nk, 0:512], reads=[("psx", kx)], writes=[("Vt", kb)])
    sc = (64 + 32) ** -0.5
    tiles = dict(dd=mdd, prev=False, md=mdd, mq1=None)
    for c, (qs, nq) in enumerate(QCH):
        P.dma("sp", tC[64:96, 0:nq], ropeC[:, qs:qs + nq], writes=[("tC",)], group="tC")
        P.dma("sp", tS[64:96, 0:nq], ropeS[:, qs:qs + nq], writes=[("tS",)], group="tS")
        for h in range(8):
            ka = nxp() % 2
            psa = psx[ka]
            for kc in range(3):
                P.mm(psa[0:96, 0:nq], wuqb[:, kc, h * 96:(h + 1) * 96], cqn[:, kc, qs:qs + nq], kc == 0, kc == 2,
                     reads=[("wuqb",), (id(cqn), kc, c)], writes=[("psx", ka)])
            kb_ = nxp() % 2
            psb = psx[kb_]
            for kc in range(3):
                P.mm(psb[0:96, 0:nq], wuqsb[:, kc, h * 96:(h + 1) * 96], cqn[:, kc, qs:qs + nq], kc == 0, kc == 2,
                     reads=[("wuqsb",), (id(cqn), kc, c)], writes=[("psx", kb_)])
            qi = nxt() % 2
            qt = qTt[qi]
            P.copy("act", qt[0:64, 0:nq], psa[0:64, 0:nq], reads=[("psx", ka)], writes=[("qT", qi, 0)])
            P.tt("dve", r1[64:96, 0:nq], psa[64:96, 0:nq], tC[64:96, 0:nq], ALU.mult, reads=[("psx", ka), ("tC",)], writes=[("r1",)])
            P.tt("dve", r2[64:96, 0:nq], psb[64:96, 0:nq], tS[64:96, 0:nq], ALU.mult, reads=[("psx", kb_), ("tS",)], writes=[("r2",)])
            P.tt("dve", qt[64:96, 0:nq], r1[64:96, 0:nq], r2[64:96, 0:nq], ALU.add, reads=[("r1",), ("r2",)], writes=[("qT", qi, 1)])
            A.chunk(c,
                    kT=lambda m, kb, h=h: big[0:96, h, KBL[kb][0]:KBL[kb][0] + KBL[kb][1]],
                    qT=lambda m, qt=qt, nq=nq: qt[0:96, 0:nq],
                    kd_base=0, kd=96,
                    V=lambda kb, h=h: Vt[0:KBL[kb][1], kb, h * 64:(h + 1) * 64], vkeys=None,
                    sc=sc, tiles=tiles, far_bias=None,
                    keysK=lambda m, kb, h=h: [("bign", h, 0 if kb == 0 else 1 + (kb - 1) // 4), ("bigr", h, 0 if kb == 0 else 1 + (kb - 1) // 4)],
                    keysQ=lambda m, qi=qi: [("qT", qi, 0), ("qT", qi, 1)],
                    keysV=lambda kb: [("Vt", kb)])
            P.op("dve", lambda e, o=rcp[0:64, 0:nq], i=A.pn[0][0:64, 0:nq]: e.reciprocal(o, i), reads=[("pn", 0)], writes=[("rcp",)])
            oi = nxt() % 2
            P.tt("dve", osb[oi][0:64, 0:nq], A.po[0][0:64, 0:nq], rcp[0:64, 0:nq], ALU.mult,
                 reads=[("po", 0), ("rcp",)], writes=[("osb", oi)])
            P.dma("sp", oT.rows(h * 64, 64)[:, qs:qs + nq], osb[oi][0:64, 0:nq], reads=[("osb", oi)], writes=[("oin", h // 2, h, c)], group=f"oout{oi}")
    if ctx is None:
        P.emit()
        return nc


def build_diff(lam_init, ctx=None):
    nc = bass.Bass("TRN2", target_bir_lowering=False) if ctx is None else ctx.nc
    pre = "" if ctx is None else ctx.pre
    dt = lambda n, s, d=F32: nc.dram_tensor(pre + n, s, d, kind="ExternalInput").ap()
    hbT = dt("hbT", [D, LSEQ], BF16) if ctx is None else None
    w_q = dt("w_q", [D, 512]); w_k = dt("w_k", [D, 512]); w_v = dt("w_v", [D, 512])
    lam4 = dt("lam4", [4, 64]); subg = dt("subg", [128]); rb = dt("rb", [32 * 4])
    m_dd = dt("m_dd", [32, 128, 256]); m_md = dt("m_md", [32, 16, 16]); m_mq1 = dt("m_mq1", [32, 16, 128])
    n_dd = dt("n_dd", [128, 256]); n_md = dt("n_md", [16, 16])
    oT = ORows(nc.dram_tensor("oT", [512, LSEQ], BF16, kind="ExternalOutput").ap()) if ctx is None else ORows(None, ctx.oin)
    P = Prog(nc) if ctx is None else ctx.P
    big = P.sbuf([128, 8, LSEQ], BF16, "big")
    qTt = P.sbuf([128, 4, LSEQ], BF16, "qTt")
    kTt = P.sbuf([128, 4, LSEQ], BF16, "kTt")
    Vt = P.sbuf([128, 33, 512], BF16, "Vt")
    wb1 = P.sbuf([128, 8, 512], BF16, "wb1")
    wb = [wb1, wb1, wb1]
    stg = [P.sbuf([128, 512], F32, "stg0")]
    Tdd = [P.sbuf([128, 256], F32, f"Tdd{h}") for h in range(4)]
    Tmd = [P.sbuf([16, 16], F32, f"Tmd{h}") for h in range(4)]
    Tmq = [P.sbuf([16, 128], F32, f"Tmq{h}") for h in range(4)]
    mk = P.sbuf([128, 256], F32, "mk")
    mk2 = P.sbuf([16, 16], F32, "mk2")
    mk3 = P.sbuf([16, 128], F32, "mk3")
    rbb = P.sbuf([128, 128], F32, "rbb")
    lamb = P.sbuf([128, 4, 64], F32, "lamb")
    lt = P.sbuf([128, 64], F32, "lt")
    l1 = P.sbuf([128, 1], F32, "l1"); l2 = P.sbuf([128, 1], F32, "l2"); nlam = P.sbuf([128, 1], F32, "nlam")
    gs = P.sbuf([128, 1], F32, "gs")
    ones32 = P.sbuf([128, 128], F32, "ones32")
    u = P.sbuf([128, 512], F32, "u"); t_ = P.sbuf([128, 512], F32, "t_"); rc = P.sbuf([128, 512], F32, "rc")
    osb = [P.sbuf([128, 512], BF16, f"osb{i}") for i in range(2)]
    A = AttnCore(P, 2, 128)
    P.memset("pool", ones32[:], 1.0 / 128, writes=[("ones32",)])
    P.dma("sp", rbb[:], rb.partition_broadcast(128), writes=[("rbb",)], group="const")
    P.dma("sp", lamb[:], lam4.rearrange("a b -> (a b)").partition_broadcast(128), writes=[("lamb",)], group="const")
    P.dma("sp", gs[:], subg.rearrange("(p o) -> p o", o=1), writes=[("gs",)], group="const", slow=True)
    P.ts("dve", gs[:], gs[:], float(1.0 - lam_init), None, ALU.mult, reads=[("gs",)], writes=[("gs",)])
    for i, dst in enumerate([l1, l2]):
        P.tt("dve", lt[:], lamb[:, 2 * i, :], lamb[:, 2 * i + 1, :], ALU.mult, reads=[("lamb",)], writes=[("lt",)])
        P.op("dve", lambda e, o=dst[:], i_=lt[:]: e.reduce_sum(o, i_, AX.X), reads=[("lt",)], writes=[("l", i)])
        P.act(dst[:], dst[:], AF.Exp, reads=[("l", i)], writes=[("l", i)])
    P.tt("dve", nlam[:], l2[:], l1[:], ALU.subtract, reads=[("l", 0), ("l", 1)], writes=[("nlam",)])
    P.ts("dve", nlam[:], nlam[:], float(-lam_init), None, ALU.add, reads=[("nlam",)], writes=[("tiles",)])
    for h in range(4):
        P.dma("sp", Tdd[h][:], n_dd, writes=[("Tdd", h)], group="const")
        P.dma("sp", Tmd[h][:], n_md, writes=[("Tmd", h)], group="const")
        P.memset("pool", Tmq[h][:], 0.0, writes=[("Tmq", h)])
    for b in range(32):
        P.dma("sp", mk[:], m_dd[b], writes=[("mk",)], group="mk")
        P.dma("sp", mk2[:], m_md[b], writes=[("mk2",)], group="mk2")
        P.dma("sp", mk3[:], m_mq1[b], writes=[("mk3",)], group="mk3")
        for h in range(4):
            col = b * 4 + h
            P.stt("dve", Tdd[h][:], mk[:], rbb[:, col:col + 1], Tdd[h][:], ALU.mult, ALU.add,
                  reads=[("mk",), ("rbb",), ("Tdd", h)], writes=[("Tdd", h)])
            P.stt("dve", Tmd[h][:], mk2[:], rbb[0:16, col:col + 1], Tmd[h][:], ALU.mult, ALU.add,
                  reads=[("mk2",), ("rbb",), ("Tmd", h)], writes=[("Tmd", h)])
            P.stt("dve", Tmq[h][:], mk3[:], rbb[0:16, col:col + 1], Tmq[h][:], ALU.mult, ALU.add,
                  reads=[("mk3",), ("rbb",), ("Tmq", h)], writes=[("Tmq", h)])
    for h in range(4):
        P.copy("pool", Tmq[h][:], Tmq[h][:], reads=[("Tdd", h), ("Tmd", h), ("Tmq", h)], writes=[("tiles",), ("Tmq", h)])
    _load_seq(P, ctx, big, 8, hbT, lambda c, part: [("big", c, t) for t in (range(0, 5) if part == 0 else range(5, 9))])
    cnt = [0]

    def nxt():
        cnt[0] += 1
        return cnt[0] - 1

    for wi, dstT in [(0, qTt), (1, kTt)]:
        load_cast_w(P, wb1, [w_q, w_k][wi], stg, 8, 512, ("wb",), colblk=512)
        for h in range(4):
            for ti, (t0, tn) in enumerate(QCH):
                kk = nxt()
                m_, i_ = kk % 2, (kk // 2) % 2
                ps = A.psS[m_][i_]
                for c in range(8):
                    P.mm(ps[:, 0:tn], wb[wi][:, c, h * 128:(h + 1) * 128], big[:, c, t0:t0 + tn], c == 0, c == 7,
                         reads=[("wb",), ("big", c, ti)], writes=[("psS", m_, i_)])
                P.copy("act" if kk % 2 else "dve", dstT[:, h, t0:t0 + tn], ps[:, 0:tn],
                       reads=[("psS", m_, i_)], writes=[(id(dstT), h, ti)])
    load_cast_w(P, wb1, w_v, stg, 8, 512, ("wb",), colblk=512)
    for kb, (ks, nk) in enumerate(KBL):
        kk = nxt()
        m_, i_ = kk % 2, (kk // 2) % 2
        ps = A.psS[m_][i_]
        ti = 0 if kb == 0 else 1 + (kb - 1) // 4
        for c in range(8):
            P.mm(ps[0:nk, 0:512], big[:, c, ks:ks + nk], wb[2][:, c, :], c == 0, c == 7,
                 reads=[("wb",), ("big", c, ti)], writes=[("psS", m_, i_)])
        P.copy("act" if kb % 2 else "dve", Vt[0:nk, kb, :], ps[0:nk, 0:512], reads=[("psS", m_, i_)], writes=[("Vt", kb)])
    sc = 64 ** -0.5
    for c, (qs, nq) in enumerate(QCH):
        for h in range(4):
            tiles = dict(dd=Tdd[h], prev=True, md=Tmd[h], mq1=Tmq[h])
            fb = rbb[:, 31 * 4 + h:31 * 4 + h + 1]
            A.chunk(c,
                    kT=lambda m, kb, h=h: kTt[64 * m:64 * m + 64, h, KBL[kb][0]:KBL[kb][0] + KBL[kb][1]],
                    qT=lambda m, h=h, qs=qs, nq=nq: qTt[64 * m:64 * m + 64, h, qs:qs + nq],
                    kd_base=0, kd=64,
                    V=lambda kb, h=h: Vt[0:KBL[kb][1], kb, h * 128:(h + 1) * 128], vkeys=None,
                    sc=sc, tiles=tiles, far_bias=fb,
                    keysK=lambda m, kb, h=h: [(id(kTt), h, 0 if kb == 0 else 1 + (kb - 1) // 4)],
                    keysQ=lambda m, h=h, c=c: [(id(qTt), h, c)],
                    keysV=lambda kb: [("Vt", kb)])
            P.op("dve", lambda e, o=rc[:, 0:nq], i=A.pn[0][:, 0:nq]: e.reciprocal(o, i), reads=[("pn", 0)], writes=[("rc",)])
            P.tt("dve", u[:, 0:nq], A.po[0][:, 0:nq], rc[:, 0:nq], ALU.mult, reads=[("po", 0), ("rc",)], writes=[("u",)])
            P.op("dve", lambda e, o=rc[:, 0:nq], i=A.pn[1][:, 0:nq]: e.reciprocal(o, i), reads=[("pn", 1), ("rc",)], writes=[("rc",)])
            P.tt("dve", t_[:, 0:nq], A.po[1][:, 0:nq], rc[:, 0:nq], ALU.mult, reads=[("po", 1), ("rc",)], writes=[("t_",)])
            P.stt("dve", u[:, 0:nq], t_[:, 0:nq], nlam[:, 0:1], u[:, 0:nq], ALU.mult, ALU.add,
                  reads=[("t_",), ("u",), ("tiles",)], writes=[("u",)])
            P.act(t_[:, 0:nq], u[:, 0:nq], AF.Square, reads=[("u",)], writes=[("t_",)])
            pst = A.psS[0][0]
            P.mm(pst[:, 0:nq], ones32[:], t_[:, 0:nq], True, True, reads=[("t_",), ("ones32",)], writes=[("psS", 0, 0)])
            P.ts("dve", rc[:, 0:nq], pst[:, 0:nq], RMS_EPS, None, ALU.add, reads=[("psS", 0, 0)], writes=[("rc",)])
            P.act(rc[:, 0:nq], rc[:, 0:nq], AF.Sqrt, reads=[("rc",)], writes=[("rc",)])
            P.op("dve", lambda e, o=rc[:, 0:nq]: e.reciprocal(o, o), reads=[("rc",)], writes=[("rc",)])
            P.tt("dve", u[:, 0:nq], u[:, 0:nq], rc[:, 0:nq], ALU.mult, reads=[("u",), ("rc",)], writes=[("u",)])
            oi = nxt() % 2
            P.act(osb[oi][:, 0:nq], u[:, 0:nq], AF.Identity, scale=gs[:, 0:1], reads=[("u",), ("gs",)], writes=[("osb", oi)])
            P.dma("sp", oT.rows(h * 128, 128)[:, qs:qs + nq], osb[oi][:, 0:nq], reads=[("osb", oi)], writes=[("oin", h, h, c)], group=f"oout{oi}")
    if ctx is None:
        P.emit()
        return nc


def build_s5(ctx=None):
    TWO_PI = 2.0 * math.pi
    nc = bass.Bass("TRN2", target_bir_lowering=False) if ctx is None else ctx.nc
    pre = "" if ctx is None else ctx.pre
    dt = lambda n, s, d=F32: nc.dram_tensor(pre + n, s, d, kind="ExternalInput").ap()
    hbh = dt("hbh", [512, LSEQ], BF16) if ctx is None else None
    pl3 = dt("pl3", [3, 128, 16]); bc3 = dt("bc3", [3, 2048])
    bpad = dt("bpad", [2, 128, 2048]); cpad = dt("cpad", [2, 128, 2048]); dpad = dt("dpad", [128, 512]); iota = dt("iota", [129])
    oT = ORows(nc.dram_tensor("oT", [512, LSEQ], BF16, kind="ExternalOutput").ap()) if ctx is None else ORows(None, ctx.oin)
    P = Prog(nc) if ctx is None else ctx.P
    hb = P.sbuf([128, 4, LSEQ], BF16, "hb")
    W = 2048
    lrB = P.sbuf([128, W], F32, "lrB"); liB = P.sbuf([128, W], F32, "liB"); stB = P.sbuf([128, W], F32, "stB")
    magB = P.sbuf([128, W], F32, "magB"); thB = P.sbuf([128, W], F32, "thB"); cB = P.sbuf([128, W], F32, "cB"); sB = P.sbuf([128, W], F32, "sB")
    crB = P.sbuf([128, W], F32, "crB"); ciB = P.sbuf([128, W], F32, "ciB"); tB = P.sbuf([128, W], F32, "tB")
    bre = P.sbuf([128, W], F32, "bre"); bim = P.sbuf([128, W], F32, "bim")
    BBr = P.sbuf([128, W], BF16, "BBr"); BBi = P.sbuf([128, W], BF16, "BBi")
    Cr = P.sbuf([128, W], BF16, "Cr"); Ci = P.sbuf([128, W], BF16, "Ci"); Dd = P.sbuf([128, 512], BF16, "Dd")
    lrP = P.sbuf([128, 16], F32, "lrP"); liP = P.sbuf([128, 16], F32, "liP"); stP = P.sbuf([128, 16], F32, "stP")
    magP = P.sbuf([128, 16], F32, "magP"); thP = P.sbuf([128, 16], F32, "thP")
    io = P.sbuf([128, 129], F32, "io")
    ang = P.sbuf([128, 129], F32, "ang")
    cosT = P.sbuf([128, 16, 129], F32, "cosT"); sinT = P.sbuf([128, 16, 129], F32, "sinT"); nsinT = P.sbuf([128, 16, 129], F32, "nsinT")
    magT = P.sbuf([128, 16, 128], F32, "magT")
    car = P.sbuf([128, 16], F32, "car"); cai = P.sbuf([128, 16], F32, "cai")
    tmp1 = P.sbuf([128, 4], F32, "tmp1")
    T1 = [P.sbuf([128, 128], F32, f"T1_{i}") for i in range(2)]; T2 = [P.sbuf([128, 128], F32, f"T2_{i}") for i in range(2)]
    T3 = [P.sbuf([128, 128], F32, f"T3_{i}") for i in range(2)]; T4 = [P.sbuf([128, 128], F32, f"T4_{i}") for i in range(2)]
    xr = [P.sbuf([128, 128], F32, f"xr{i}") for i in range(2)]; xi = [P.sbuf([128, 128], F32, f"xi{i}") for i in range(2)]
    wr = [P.sbuf([128, 128], F32, f"wr{i}") for i in range(2)]; wi = [P.sbuf([128, 128], F32, f"wi{i}") for i in range(2)]
    sr = [P.sbuf([128, 128], BF16, f"sr{i}") for i in range(2)]; si = [P.sbuf([128, 128], BF16, f"si{i}") for i in range(2)]
    og = [P.sbuf([128, 128], BF16, f"og{i}") for i in range(2)]
    psr = [P.psum([128, 128], F32, f"psr{i}") for i in range(2)]; psi = [P.psum([128, 128], F32, f"psi{i}") for i in range(2)]
    psy = [P.psum([128, 128], F32, f"psy{i}") for i in range(2)]

    def load_bc(dst, row, key):
        P.dma("sp", dst[:], bc3[row].partition_broadcast(128), writes=[key], group="const")

    load_bc(lrB, 0, ("lrB",)); load_bc(liB, 1, ("liB",)); load_bc(stB, 2, ("stB",))
    P.dma("sp", lrP[:], pl3[0], writes=[("lrP",)], group="const")
    P.dma("sp", liP[:], pl3[1], writes=[("liP",)], group="const")
    P.dma("sp", stP[:], pl3[2], writes=[("stP",)], group="const")
    P.dma("sp", io[:], iota.partition_broadcast(128), writes=[("io",)], group="const")
    if ctx is None:
        for c in range(4):
            P.dma("sp", hb[:, c, :], hbh[c * 128:(c + 1) * 128, :], writes=[("hb", c)], group="hb")
    else:
        hstA = P.sbuf([128, NT], BF16, "hstA"); hstB = P.sbuf([128, NT], BF16, "hstB")
        for c in range(4):
            for part, (rk, c0, d0, n_) in enumerate([(0, 0, 0, NT), (1, 16, NT, NT - 16)]):
                P.dma("sp", hstA[:, 0:n_], ctx.x_rows(rk, c)[:, c0:c0 + n_], writes=[("hstA",)], group="hb")
                P.dma("sp", hstB[:, 0:n_], ctx.x_rows(rk, c + 4)[:, c0:c0 + n_], writes=[("hstB",)], group="hb")
                P.ts("dve", hstA[:, 0:n_], hstA[:, 0:n_], ctx.selv[:, 0:1], None, ALU.mult, reads=[("hstA",), ("selv",)], writes=[("hstA",)])
                P.stt("dve", hb[:, c, d0:d0 + n_], hstB[:, 0:n_], ctx.selv[:, 1:2], hstA[:, 0:n_], ALU.mult, ALU.add,
                      reads=[("hstA",), ("hstB",), ("selv",)], writes=[("hb", c)])

    I32 = mybir.dt.int32
    ki_s = P.sbuf([128, 129], I32, "ki_s")
    km_s = P.sbuf([128, 129], F32, "km_s")
    pP = P.sbuf([128, 16], F32, "pP")

    def sin_of(th, off, prescale, dst, tmp, kt, kd, ktmp, small):
        if small:
            ki, km, kk = ki_s[:], km_s[:], ("sc_tmp",)
        else:
            ki, km, kk = bre[:].bitcast(I32), bim[:], ("bre",)
        P.ts("dve", tmp, th, float(prescale), float(off), ALU.mult, ALU.add, reads=[kt, ktmp], writes=[ktmp])
        P.ts("dve", km, tmp, 1.0 / TWO_PI, None, ALU.mult, reads=[ktmp, kk], writes=[kk])
        P.copy("dve", ki, km, reads=[kk], writes=[kk])
        P.copy("dve", km, ki, reads=[kk], writes=[kk])
        P.stt("dve", tmp, km, -TWO_PI, tmp, ALU.mult, ALU.add, reads=[kk, ktmp], writes=[ktmp])
        P.ts("dve", km, tmp, math.pi, None, ALU.is_gt, reads=[ktmp, kk], writes=[kk])
        P.stt("dve", tmp, km, -TWO_PI, tmp, ALU.mult, ALU.add, reads=[kk, ktmp], writes=[ktmp])
        P.ts("dve", km, tmp, -math.pi, None, ALU.is_lt, reads=[ktmp, kk], writes=[kk])
        P.stt("dve", tmp, km, TWO_PI, tmp, ALU.mult, ALU.add, reads=[kk, ktmp], writes=[ktmp])
        P.act(dst, tmp, AF.Sin, reads=[ktmp], writes=[kd])

    def expm1_horner(x, p, kx, kp):
        P.ts("dve", p, x, 1.0 / 8, 1.0, ALU.mult, ALU.add, reads=[kx, kp], writes=[kp])
        for kdiv in (7, 6, 5, 4, 3, 2):
            P.tt("dve", p, p, x, ALU.mult, reads=[kx, kp], writes=[kp])
            P.ts("dve", p, p, 1.0 / kdiv, 1.0, ALU.mult, ALU.add, reads=[kp], writes=[kp])
        P.tt("dve", x, p, x, ALU.mult, reads=[kx, kp], writes=[kx])

    P.ts("dve", lrP[:], lrP[:], -1e-4, None, ALU.min, reads=[("lrP",)], writes=[("lrP",)])
    P.act(stP[:], stP[:], AF.Exp, reads=[("stP",)], writes=[("stP",)])
    P.tt("dve", magP[:], lrP[:], stP[:], ALU.mult, reads=[("lrP",), ("stP",)], writes=[("magP",)])
    expm1_horner(magP[:], pP[:], ("magP",), ("pP",))
    P.ts("dve", magP[:], magP[:], 1.0, None, ALU.add, reads=[("magP",)], writes=[("magP",)])
    P.tt("dve", thP[:], liP[:], stP[:], ALU.mult, reads=[("liP",), ("stP",)], writes=[("thP",)])
    P.ts("dve", lrB[:], lrB[:], -1e-4, None, ALU.min, reads=[("lrB",)], writes=[("lrB",)])
    P.act(stB[:], stB[:], AF.Exp, reads=[("stB",)], writes=[("stB",)])
    P.tt("dve", magB[:], lrB[:], stB[:], ALU.mult, reads=[("lrB",), ("stB",)], writes=[("magB",)])
    expm1_horner(magB[:], crB[:], ("magB",), ("crB",))
    P.tt("dve", thB[:], liB[:], stB[:], ALU.mult, reads=[("liB",), ("stB",)], writes=[("thB",)])
    sin_of(thB[:], 0.0, 1.0, sB[:], tB[:], ("thB",), ("sB",), ("tB",), False)
    sin_of(thB[:], 0.5 * math.pi, 1.0, cB[:], tB[:], ("thB",), ("cB",), ("tB",), False)
    sin_of(thB[:], 0.0, 0.5, crB[:], tB[:], ("thB",), ("crB",), ("tB",), False)
    P.tt("dve", crB[:], crB[:], crB[:], ALU.mult, reads=[("crB",)], writes=[("crB",)])
    P.ts("dve", crB[:], crB[:], -2.0, None, ALU.mult, reads=[("crB",)], writes=[("crB",)])
    P.tt("dve", cB[:], cB[:], magB[:], ALU.mult, reads=[("cB",), ("magB",)], writes=[("cB",)])
    P.tt("dve", cB[:], cB[:], crB[:], ALU.add, reads=[("cB",), ("crB",)], writes=[("cB",)])
    P.stt("dve", sB[:], magB[:], 1.0, sB[:], ALU.add, ALU.mult, reads=[("sB",), ("magB",)], writes=[("sB",)])
    P.tt("dve", magB[:], lrB[:], lrB[:], ALU.mult, reads=[("lrB",), ("magB",), ("sB",), ("cB",)], writes=[("magB",)])
    P.tt("dve", tB[:], liB[:], liB[:], ALU.mult, reads=[("liB",), ("tB",)], writes=[("tB",)])
    P.tt("dve", magB[:], magB[:], tB[:], ALU.add, reads=[("magB",), ("tB",)], writes=[("magB",)])
    P.op("dve", lambda e, o=magB[:]: e.reciprocal(o, o), reads=[("magB",)], writes=[("magB",)])
    P.tt("dve", crB[:], cB[:], lrB[:], ALU.mult, reads=[("cB",), ("lrB",), ("crB",)], writes=[("crB",)])
    P.tt("dve", tB[:], sB[:], liB[:], ALU.mult, reads=[("sB",), ("liB",), ("tB",)], writes=[("tB",)])
    P.tt("dve", crB[:], crB[:], tB[:], ALU.add, reads=[("crB",), ("tB",)], writes=[("crB",)])
    P.tt("dve", crB[:], crB[:], magB[:], ALU.mult, reads=[("crB",), ("magB",)], writes=[("crB",)])
    P.tt("dve", ciB[:], sB[:], lrB[:], ALU.mult, reads=[("sB",), ("lrB",)], writes=[("ciB",)])
    P.tt("dve", tB[:], cB[:], liB[:], ALU.mult, reads=[("cB",), ("liB",), ("tB",)], writes=[("tB",)])
    P.tt("dve", ciB[:], ciB[:], tB[:], ALU.subtract, reads=[("ciB",), ("tB",)], writes=[("ciB",)])
    P.tt("dve", ciB[:], ciB[:], magB[:], ALU.mult, reads=[("ciB",), ("magB",)], writes=[("ciB",)])
    P.dma("sp", bre[:], bpad[0], writes=[("bre",)], group="const3")
    P.dma("sp", bim[:], bpad[1], reads=[("bre",)], writes=[("bim",)], group="const3")
    P.tt("dve", tB[:], crB[:], bre[:], ALU.mult, reads=[("crB",), ("bre",), ("tB",)], writes=[("tB",)])
    P.tt("dve", thB[:], ciB[:], bim[:], ALU.mult, reads=[("ciB",), ("bim",), ("thB",)], writes=[("thB",)])
    P.tt("dve", BBr[:], tB[:], thB[:], ALU.subtract, reads=[("tB",), ("thB",)], writes=[("BBr",)])
    P.tt("dve", tB[:], crB[:], bim[:], ALU.mult, reads=[("crB",), ("bim",), ("tB",), ("BBr",)], writes=[("tB",)])
    P.tt("dve", thB[:], ciB[:], bre[:], ALU.mult, reads=[("ciB",), ("bre",), ("thB",), ("BBr",)], writes=[("thB",)])
    P.tt("dve", BBi[:], tB[:], thB[:], ALU.add, reads=[("tB",), ("thB",)], writes=[("BBi",)])
    P.dma("sp", bre[:], cpad[0], reads=[("BBr",), ("BBi",)], writes=[("bre",)], group="const2")
    P.dma("sp", bim[:], cpad[1], reads=[("BBr",), ("BBi",)], writes=[("bim",)], group="const2")
    P.copy("pool", Cr[:], bre[:], reads=[("bre",)], writes=[("Cr",)])
    P.copy("pool", Ci[:], bim[:], reads=[("bim",)], writes=[("Ci",)])
    P.dma("sp", lrB[:, 0:512], dpad, reads=[("crB",), ("ciB",)], writes=[("lrB",)], group="const2")
    P.copy("pool", Dd[:], lrB[:, 0:512], reads=[("lrB",)], writes=[("Dd",)])
    for j in range(16):
        P.ts("dve", ang[:], io[:], thP[:, j:j + 1], None, ALU.mult, reads=[("io",), ("thP",), ("ang",)], writes=[("ang",)])
        sin_of(ang[:], 0.0, 1.0, sinT[:, j, :], nsinT[:, j, :], ("ang",), ("sinT", j), ("nsinT", j), True)
        sin_of(ang[:], 0.5 * math.pi, 1.0, cosT[:, j, :], nsinT[:, j, :], ("ang",), ("cosT", j), ("nsinT", j), True)
        P.ts("pool", nsinT[:, j, :], sinT[:, j, :], -1.0, None, ALU.mult, reads=[("sinT", j), ("nsinT", j)], writes=[("nsinT", j)])
        P.memset("pool", magT[:, j, :], 1.0, writes=[("magT", j)])
        P.ts("pool", magT[:, j, :], magT[:, j, :], magP[:, j:j + 1], None, ALU.mult, reads=[("magT", j), ("magP",)], writes=[("magT", j)])
    P.memset("pool", car[:], 0.0, writes=[("car", j) for j in range(16)])
    P.memset("pool", cai[:], 0.0, writes=[("cai", j) for j in range(16)])
    k = 0
    work = []
    for bi, (t0, tn) in enumerate(KBL):
        for j in range(16):
            ch = j // 4
            b2 = k % 2
            k += 1
            c_, s_, ns_ = cosT[:, j, 0:tn], sinT[:, j, 0:tn], nsinT[:, j, 0:tn]
            tk = [("cosT", j), ("sinT", j), ("nsinT", j)]

            def stage_a(bi=bi, t0=t0, tn=tn, j=j, ch=ch, b2=b2, c_=c_, s_=s_, tk=tk):
                P.mm(psr[b2][:, 0:tn], BBr[:, j * 128:(j + 1) * 128], hb[:, ch, t0:t0 + tn], True, True,
                     reads=[("BBr",), ("hb", ch)], writes=[("psr", b2)])
                P.mm(psi[b2][:, 0:tn], BBi[:, j * 128:(j + 1) * 128], hb[:, ch, t0:t0 + tn], True, True,
                     reads=[("BBi",), ("hb", ch)], writes=[("psi", b2)])
                P.tt("dve", T1[b2][:, 0:tn], psr[b2][:, 0:tn], c_, ALU.mult, reads=[("psr", b2)] + tk, writes=[("T1", b2)])
                P.tt("dve", T2[b2][:, 0:tn], psi[b2][:, 0:tn], s_, ALU.mult, reads=[("psi", b2)] + tk, writes=[("T2", b2)])
                P.tt("dve", T3[b2][:, 0:tn], psi[b2][:, 0:tn], c_, ALU.mult, reads=[("psi", b2)] + tk, writes=[("T3", b2)])
                P.tt("dve", T4[b2][:, 0:tn], psr[b2][:, 0:tn], s_, ALU.mult, reads=[("psr", b2)] + tk, writes=[("T4", b2)])
                P.tt("pool", xr[b2][:, 0:tn], T1[b2][:, 0:tn], T2[b2][:, 0:tn], ALU.add, reads=[("T1", b2), ("T2", b2)], writes=[("xr", b2)])
                P.tt("pool", xi[b2][:, 0:tn], T3[b2][:, 0:tn], T4[b2][:, 0:tn], ALU.subtract, reads=[("T3", b2), ("T4", b2)], writes=[("xi", b2)])

            def stage_b(bi=bi, t0=t0, tn=tn, j=j, ch=ch, b2=b2, c_=c_, s_=s_, ns_=ns_, tk=tk):
                P.op("dve", lambda e, o=wr[b2][:, 0:tn], d0=magT[:, j, 0:tn], d1=xr[b2][:, 0:tn], ini=car[:, j:j + 1]:
                     e.tensor_tensor_scan(o, d0, d1, ini, ALU.mult, ALU.add),
                     reads=[("magT", j), ("xr", b2), ("car", j)], writes=[("wr", b2)])
                P.op("dve", lambda e, o=wi[b2][:, 0:tn], d0=magT[:, j, 0:tn], d1=xi[b2][:, 0:tn], ini=cai[:, j:j + 1]:
                     e.tensor_tensor_scan(o, d0, d1, ini, ALU.mult, ALU.add),
                     reads=[("magT", j), ("xi", b2), ("cai", j)], writes=[("wi", b2)])
                er, ei = cosT[:, j, tn:tn + 1], sinT[:, j, tn:tn + 1]
                wl_r, wl_i = wr[b2][:, tn - 1:tn], wi[b2][:, tn - 1:tn]
                P.tt("dve", tmp1[:, 0:1], wl_i, ei, ALU.mult, reads=[("wi", b2)] + tk, writes=[("tmp1", 0)])
                P.stt("dve", car[:, j:j + 1], wl_r, er, tmp1[:, 0:1], ALU.mult, ALU.subtract,
                      reads=[("wr", b2), ("tmp1", 0)] + tk, writes=[("car", j)])
                P.tt("dve", tmp1[:, 1:2], wl_r, ei, ALU.mult, reads=[("wr", b2)] + tk, writes=[("tmp1", 1)])
                P.stt("dve", cai[:, j:j + 1], wl_i, er, tmp1[:, 1:2], ALU.mult, ALU.add,
                      reads=[("wi", b2), ("tmp1", 1)] + tk, writes=[("cai", j)])
                P.tt("pool", T1[b2][:, 0:tn], wr[b2][:, 0:tn], c_, ALU.mult, reads=[("wr", b2)] + tk, writes=[("T1", b2)])
                P.tt("pool", T2[b2][:, 0:tn], wi[b2][:, 0:tn], s_, ALU.mult, reads=[("wi", b2)] + tk, writes=[("T2", b2)])
                P.tt("pool", sr[b2][:, 0:tn], T1[b2][:, 0:tn], T2[b2][:, 0:tn], ALU.subtract, reads=[("T1", b2), ("T2", b2)], writes=[("sr", b2)])
                P.tt("dve", T3[b2][:, 0:tn], wr[b2][:, 0:tn], ns_, ALU.mult, reads=[("wr", b2)] + tk, writes=[("T3", b2)])
                P.tt("pool", T4[b2][:, 0:tn], wi[b2][:, 0:tn], c_, ALU.mult, reads=[("wi", b2)] + tk, writes=[("T4", b2)])
                P.tt("pool", si[b2][:, 0:tn], T3[b2][:, 0:tn], T4[b2][:, 0:tn], ALU.subtract, reads=[("T3", b2), ("T4", b2)], writes=[("si", b2)])
                yb = (bi * 4 + ch) % 2
                if j % 4 == 0:
                    P.mm(psy[yb][:, 0:tn], Dd[:, ch * 128:(ch + 1) * 128], hb[:, ch, t0:t0 + tn], True, False,
                         reads=[("Dd",), ("hb", ch)], writes=[("psy", yb)])
                P.mm(psy[yb][:, 0:tn], Cr[:, j * 128:(j + 1) * 128], sr[b2][:, 0:tn], False, False,
                     reads=[("Cr",), ("sr", b2)], writes=[("psy", yb)])
                P.mm(psy[yb][:, 0:tn], Ci[:, j * 128:(j + 1) * 128], si[b2][:, 0:tn], False, j % 4 == 3,
                     reads=[("Ci",), ("si", b2)], writes=[("psy", yb)])
                if j % 4 == 3:
                    P.act(og[yb][:, 0:tn], psy[yb][:, 0:tn], AF.Gelu, reads=[("psy", yb)], writes=[("og", yb)])
                    P.dma("sp", oT.rows(ch * 128, 128)[:, t0:t0 + tn], og[yb][:, 0:tn], reads=[("og", yb)], writes=[("oin", ch, ch, bi)], group=f"oout{yb}")
            work.append((stage_a, stage_b))
    for i_ in range(len(work) + 1):
        if i_ < len(work):
            work[i_][0]()
        if i_ >= 1:
            work[i_ - 1][1]()
    if ctx is None:
        P.emit()
        return nc


def build_ssd(ctx=None):
    nc = bass.Bass("TRN2", target_bir_lowering=False) if ctx is None else ctx.nc
    pre = "" if ctx is None else ctx.pre
    dt = lambda n, s, d=F32: nc.dram_tensor(pre + n, s, d, kind="ExternalInput").ap()
    hbT = dt("hbT", [D, LSEQ], BF16) if ctx is None else None
    w_z = dt("w_z", [D, 1024]); w_x = dt("w_x", [D, 1024]); w_B = dt("w_B", [D, 512]); w_C = dt("w_C", [D, 512]); w_dt = dt("w_dt", [D, 16])
    convw = dt("convw", [4, 2048]); convb = dt("convb", [2048])
    dtb = dt("dtb", [16]); alog = dt("alog", [16]); dskip = dt("dskip", [16]); ng = dt("ng", [1024])
    ident_d = dt("ident", [128, 128]); negm_d = dt("negm", [128, 128]); sel_d = dt("sel", [16, 2048])
    oT = ORows(nc.dram_tensor("oT", [1024, LSEQ], BF16, kind="ExternalOutput").ap()) if ctx is None else ORows(None, ctx.oin)
    P = Prog(nc) if ctx is None else ctx.P
    hb = P.sbuf([128, 8, LSEQ], BF16, "hb")
    praw = P.sbuf([128, LSEQ + 3], F32, "praw")
    PIECE = 1040
    PCS = [(0, 1040), (1040, 1024), (2064, 1024), (3088, 1024)]
    acc = P.sbuf([128, PIECE], F32, "acc")
    cvo = P.sbuf([128, PIECE], BF16, "cvo")
    BT = P.sbuf([128, LSEQ], BF16, "BT"); CT = P.sbuf([128, LSEQ], BF16, "CT")
    V = P.sbuf([128, 33, 256], BF16, "V")
    acT = P.sbuf([16, LSEQ], F32, "acT")
    DPC = [(0, 528)] + [(528 + 512 * i_, 512) for i_ in range(7)]
    dtp = P.sbuf([16, 528], F32, "dtp"); atp = P.sbuf([16, 528], F32, "atp"); onesr = P.sbuf([16, 528], F32, "onesr")
    dt_k = P.sbuf([128, 33, 16], F32, "dt_k"); nac_k = P.sbuf([128, 33, 16], F32, "nac_k")
    bcS = [P.sbuf([128, 512], F32, f"bcS{i}") for i in range(4)]
    E = [P.sbuf([128, 128], F32, f"E{i}") for i in range(4)]
    PT = [P.sbuf([128, 128], BF16, f"PT{i}") for i in range(4)]
    Ew = [P.sbuf([128, 128], F32, f"Ew{i}") for i in range(4)]
    CTs = [P.sbuf([128, 128], BF16, f"CTs{i}") for i in range(4)]
    Vw = [P.sbuf([128, 64], BF16, f"Vw{i}") for i in range(4)]
    wv = [P.sbuf([128, 2], F32, f"wv{i}") for i in range(4)]
    nbc0 = [P.sbuf([128, 4], F32, f"nbc0_{i}") for i in range(4)]
    Bk = P.sbuf([128, 33, 128], BF16, "Bk")
    S32 = P.sbuf([128, 256], F32, "S32")
    Sbf = P.sbuf([128, 256], BF16, "Sbf")
    cnt_q = [0]
    Dt = P.sbuf([128, 16, 128], BF16, "Dt")
    wst = P.sbuf([128, 8, 128], F32, "wst")
    wtb = P.sbuf([128, 8, 128], BF16, "wtb")
    wzb = P.sbuf([128, 8, 256], BF16, "wzb")
    wdtb = P.sbuf([128, 8, 16], BF16, "wdtb")
    wdts = P.sbuf([128, 8, 16], F32, "wdts")
    sz = P.sbuf([128, 512], F32, "sz"); sq = P.sbuf([128, 512], F32, "sqq"); rstd = P.sbuf([128, 512], F32, "rstd")
    osb = [P.sbuf([128, 512], BF16, f"osb{i}") for i in range(2)]
    ident = P.sbuf([128, 128], F32, "identS"); identb = P.sbuf([128, 128], BF16, "identb"); negm = P.sbuf([128, 128], F32, "negmS")
    sel = P.sbuf([16, 2048], F32, "selS")
    cw = P.sbuf([128, 16, 4], F32, "cw"); cb = P.sbuf([128, 16], F32, "cbias")
    dtb_t = P.sbuf([16, 1], F32, "dtb_t"); A_t = P.sbuf([16, 1], F32, "A_t"); one_t = P.sbuf([16, 1], F32, "one_t")
    dbc = P.sbuf([128, 16], F32, "dbc"); ngt = P.sbuf([64, 16], F32, "ngt"); ones64 = P.sbuf([64, 64], F32, "ones64")
    car = P.sbuf([16, 1], F32, "carS")
    po = [P.psum([128, 512], F32, f"po{i}") for i in range(4)]
    pcb = [P.psum([128, 512], F32, f"pcb{i}") for i in range(2)]
    psx = P.psum([128, 512], F32, "psx")
    pst = P.psum([128, 512], F32, "pst")
    psxb = psx[:].bitcast(BF16)

    P.dma("sp", ident[:], ident_d, writes=[("ident",)], group="const")
    P.dma("sp", negm[:], negm_d, writes=[("negm",)], group="const")
    P.dma("sp", sel[:], sel_d, writes=[("sel",)], group="const")
    for k_ in range(4):
        P.dma("sp", cw[:, :, k_], convw[k_].rearrange("(c p) -> p c", p=128), writes=[("cw",)], group="const", slow=True)
    P.dma("sp", cb[:], convb.rearrange("(c p) -> p c", p=128), writes=[("cbias",)], group="const", slow=True)
    P.dma("sp", dtb_t[:], dtb.rearrange("(p o) -> p o", o=1), writes=[("dtb",)], group="const", slow=True)
    P.dma("sp", A_t[:], alog.rearrange("(p o) -> p o", o=1), writes=[("A",)], group="const", slow=True)
    P.dma("sp", dbc[:], dskip.partition_broadcast(128), writes=[("dbc",)], group="const")
    P.dma("sp", ngt[:], ng.rearrange("(h p) -> p h", p=64), writes=[("ngt",)], group="const", slow=True)
    P.copy("pool", identb[:], ident[:], reads=[("ident",)], writes=[("identb",)])
    P.memset("pool", ones64[:], 1.0 / 256, writes=[("ones64",)])
    P.memset("pool", one_t[:], 1.0, writes=[("one_t",)])
    P.memset("pool", onesr[:], 1.0, writes=[("onesr",)])
    P.memset("pool", praw[:, 0:3], 0.0, writes=[("praw0",)])
    P.memset("pool", car[:], 0.0, writes=[("car",)])
    P.act(A_t[:], A_t[:], AF.Exp, reads=[("A",)], writes=[("A",)])
    P.ts("dve", A_t[:], A_t[:], -1.0, None, ALU.mult, reads=[("A",)], writes=[("A",)])
    for h in range(16):
        P.ts("dve", Dt[:, h, :], ident[:], dbc[:, h:h + 1], None, ALU.mult, reads=[("ident",), ("dbc",)], writes=[("Dt",)])
    _load_seq(P, ctx, hb, 8, hbT, lambda c, part: [("hb", c)])
    HBK = [("hb", c) for c in range(8)]
    P.dma("sp", wdts[:], w_dt.rearrange("(c p) f -> p c f", p=128), writes=[("wdts",)], group="const", slow=True)
    P.copy("pool", wdtb[:], wdts[:], reads=[("wdts",)], writes=[("wdtb",)])
    for pi, (p0, pn) in enumerate(DPC):
        for s0 in range(0, pn, 512):
            sn = min(512, pn - s0)
            for c in range(8):
                P.mm(psx[0:16, 0:sn], wdtb[:, c, :], hb[:, c, p0 + s0:p0 + s0 + sn], c == 0, c == 7,
                     reads=[("wdtb",), ("hb", c)], writes=[("psx",)])
            P.act(dtp[:, s0:s0 + sn], psx[0:16, 0:sn], AF.Exp, bias=dtb_t[:, 0:1], reads=[("psx",), ("dtb",)], writes=[("dtp",)])
        P.act(dtp[:, 0:pn], dtp[:, 0:pn], AF.Ln, bias=one_t[:, 0:1], reads=[("dtp",), ("one_t",)], writes=[("dtp",)])
        P.ts("dve", atp[:, 0:pn], dtp[:, 0:pn], A_t[:, 0:1], None, ALU.mult, reads=[("dtp",), ("A",)], writes=[("atp",)])
        P.op("dve", lambda e, o=acT[:, p0:p0 + pn], d0=onesr[:, 0:pn], d1=atp[:, 0:pn], ini=car[:, 0:1]:
             e.tensor_tensor_scan(o, d0, d1, ini, ALU.mult, ALU.add),
             reads=[("onesr",), ("atp",), ("car",)], writes=[("acT", pi)])
        P.copy("dve", car[:], acT[:, p0 + pn - 1:p0 + pn], reads=[("acT", pi)], writes=[("car",)])
        for kb, (ks, nk) in enumerate(KBL):
            if not (p0 <= ks < p0 + pn):
                continue
            assert ks + nk <= p0 + pn
            P.tr(psx[0:nk, 0:16], dtp[:, ks - p0:ks - p0 + nk], ident[0:16, 0:16], reads=[("dtp",), ("ident",)], writes=[("psx",)])
            P.copy("act", dt_k[0:nk, kb, :], psx[0:nk, 0:16], reads=[("psx",)], writes=[("dt_k", kb)])
            P.tr(psx[0:nk, 0:16], acT[:, ks:ks + nk], ident[0:16, 0:16], reads=[("acT", pi), ("ident",)], writes=[("psx",)])
            P.ts("dve", nac_k[0:nk, kb, :], psx[0:nk, 0:16], -1.0, None, ALU.mult, reads=[("psx",)], writes=[("nac_k", kb)])
    ACK = [("acT", pi) for pi in range(8)]

    def proj_conv(wsrc, col0, f, out_fn):
        P.dma("sp", wst[:], wsrc[:, col0:col0 + 128].rearrange("(c p) f -> p c f", p=128), writes=[("wst",)], group="wst")
        P.copy("act", wtb[:], wst[:], reads=[("wst",)], writes=[("wtb",)])
        for ti, (t0, tn) in enumerate(QCH):
            kq = ti % 2
            for c in range(8):
                P.mm(pcb[kq][:, 0:tn], wtb[:, c, :], hb[:, c, t0:t0 + tn], c == 0, c == 7,
                     reads=[("wtb",), ("hb", c)], writes=[("pcb", kq)])
            P.copy("act", praw[:, 3 + t0:3 + t0 + tn], pcb[kq][:, 0:tn], reads=[("pcb", kq), ("praw0",)], writes=[("praw", ti)])
        PK = [("praw", ti) for ti in range(9)] + [("praw0",)]
        for pi, (p0, pn) in enumerate(PCS):
            P.ts("dve", acc[:, 0:pn], praw[:, p0:p0 + pn], cw[:, f, 0:1], None, ALU.mult, reads=PK + [("cw",)], writes=[("acc",)])
            for k in (1, 2, 3):
                P.stt("dve", acc[:, 0:pn], praw[:, p0 + k:p0 + k + pn], cw[:, f, k:k + 1], acc[:, 0:pn], ALU.mult, ALU.add,
                      reads=PK + [("cw",), ("acc",)], writes=[("acc",)])
            out_fn(pi, p0, pn)

    ncount = [0]

    def nx():
        ncount[0] += 1
        return ncount[0] - 1

    for g in range(4):
        def outB(pi, p0, pn, g=g):
            P.act(BT[:, p0:p0 + pn], acc[:, 0:pn], AF.Silu, bias=cb[:, 8 + g:9 + g], reads=[("acc",), ("cbias",)], writes=[("BT", pi)])
            for kb, (ks, nk) in enumerate(KBL):
                if not (p0 <= ks < p0 + pn):
                    continue
                P.tr(psxb[0:nk, 0:128], BT[:, ks:ks + nk], identb[:], reads=[("BT", pi), ("identb",)], writes=[("psx",)])
                P.copy("act" if kb % 2 else "dve", Bk[0:nk, kb, :], psxb[0:nk, 0:128], reads=[("psx",)], writes=[("Bk", kb)])

        def outC(pi, p0, pn, g=g):
            P.act(CT[:, p0:p0 + pn], acc[:, 0:pn], AF.Silu, bias=cb[:, 12 + g:13 + g], reads=[("acc",), ("cbias",)], writes=[("CT", pi)])
        proj_conv(w_B, g * 128, 8 + g, outB)
        proj_conv(w_C, g * 128, 12 + g, outC)
        for xc in range(2):
            f = 2 * g + xc

            def outX(pi, p0, pn, f=f, xc=xc):
                P.act(cvo[:, 0:pn], acc[:, 0:pn], AF.Silu, bias=cb[:, f:f + 1], reads=[("acc",), ("cbias",)], writes=[("cvo",)])
                for kb, (ks, nk) in enumerate(KBL):
                    if not (p0 <= ks < p0 + pn):
                        continue
                    P.tr(psxb[0:nk, 0:128], cvo[:, ks - p0:ks - p0 + nk], identb[:], reads=[("cvo",), ("identb",)], writes=[("psx",)])
                    P.copy("act" if kb % 2 else "dve", V[0:nk, kb, xc * 128:(xc + 1) * 128], psxb[0:nk, 0:128],
                           reads=[("psx",)], writes=[("V", kb)])
            proj_conv(w_x, f * 128, f, outX)
        for hf in range(2):
            P.dma("sp", wst[:], w_z[:, g * 256 + hf * 128:g * 256 + (hf + 1) * 128].rearrange("(c p) f -> p c f", p=128),
                  writes=[("wst",)], group="wst")
            P.copy("act", wzb[:, :, hf * 128:(hf + 1) * 128], wst[:], reads=[("wst",)], writes=[("wzb",)])
        BK = [("BT", pi) for pi in range(4)]
        CK = [("CT", pi) for pi in range(4)]
        P.memset("pool", S32[:], 0.0, writes=[("S32", hl_) for hl_ in range(4)])
        P.memset("pool", Sbf[:], 0.0, writes=[("Sbf", hl_) for hl_ in range(4)])
        for c, (qs, nq) in enumerate(QCH):
            qb0 = 4 * (c - 1) + 1
            blocks = [(0, 0, 16)] if c == 0 else [(qb0 + i_, 128 * i_, 128) for i_ in range(4)]
            for hl in range(4):
                h = 4 * g + hl
                P.mm(psx[:, 0:nq], sel[:, h * 128:(h + 1) * 128], acT[:, qs:qs + nq], True, True,
                     reads=[("sel",)] + ACK, writes=[("psx",)])
                P.copy("act", bcS[hl][:, 0:nq], psx[:, 0:nq], reads=[("psx",)], writes=[("bcS", hl)])
                if c > 0:
                    P.mm(pst[:, 0:512], sel[:, h * 128:(h + 1) * 128], acT[:, qs - 1:qs - 1 + 512], True, True,
                         reads=[("sel",)] + ACK, writes=[("pst",)])
                    for i_ in range(4):
                        P.ts("dve", nbc0[hl][:, i_:i_ + 1], pst[:, 128 * i_:128 * i_ + 1], -1.0, None, ALU.mult,
                             reads=[("pst",)], writes=[("nbc0", hl)])
                else:
                    P.memset("pool", nbc0[hl][:], 0.0, writes=[("nbc0", hl)])
            for i_, (kb, col0, nb) in enumerate(blocks):
                ks, nk = KBL[kb]
                kq = cnt_q[0] % 2
                cnt_q[0] += 1
                P.mm(pcb[kq][0:nk, 0:nb], BT[:, ks:ks + nk], CT[:, ks:ks + nb], True, True,
                     reads=BK + CK, writes=[("pcb", kq)])
                HL = range(4)
                hs_ = [4 * g + hl for hl in HL]
                for hl in HL:
                    P.tt("dve", E[hl][0:nk, 0:nb], bcS[hl][0:nk, col0:col0 + nb], negm[0:nk, 0:nb], ALU.add,
                         reads=[("bcS", hl), ("negm",)], writes=[("E", hl)])
                for hl in HL:
                    P.act(E[hl][0:nk, 0:nb], E[hl][0:nk, 0:nb], AF.Exp, bias=nac_k[0:nk, kb, hs_[hl]:hs_[hl] + 1],
                          reads=[("E", hl), ("nac_k", kb)], writes=[("E", hl)])
                if kb > 0:
                    for hl in HL:
                        P.act(Ew[hl][:, 0:nb], bcS[hl][:, col0:col0 + nb], AF.Exp, bias=nbc0[hl][:, i_:i_ + 1],
                              reads=[("bcS", hl), ("nbc0", hl)], writes=[("Ew", hl)])
                if kb < 32:
                    for hl in HL:
                        al = bcS[hl][:, col0 + nb - 1:col0 + nb]
                        P.act(wv[hl][0:nk, 0:1], nac_k[0:nk, kb, hs_[hl]:hs_[hl] + 1], AF.Exp, bias=al[0:nk, :],
                              reads=[("bcS", hl), ("nac_k", kb)], writes=[("wv", hl, 0)])
                        P.act(wv[hl][:, 1:2], al, AF.Exp, bias=nbc0[hl][:, i_:i_ + 1],
                              reads=[("bcS", hl), ("nbc0", hl)], writes=[("wv", hl, 1)])
                for hl in HL:
                    P.stt("dve", PT[hl][0:nk, 0:nb], pcb[kq][0:nk, 0:nb], dt_k[0:nk, kb, hs_[hl]:hs_[hl] + 1], E[hl][0:nk, 0:nb], ALU.mult, ALU.mult,
                          reads=[("pcb", kq), ("dt_k", kb), ("E", hl)], writes=[("PT", hl)])
                if kb > 0:
                    for hl in HL:
                        P.tt("dve", CTs[hl][:, 0:nb], CT[:, ks:ks + nb], Ew[hl][:, 0:nb], ALU.mult,
                             reads=CK + [("Ew", hl)], writes=[("CTs", hl)])
                if kb < 32:
                    for hl in HL:
                        P.tt("dve", wv[hl][0:nk, 0:1], wv[hl][0:nk, 0:1], dt_k[0:nk, kb, hs_[hl]:hs_[hl] + 1], ALU.mult,
                             reads=[("wv", hl, 0), ("dt_k", kb)], writes=[("wv", hl, 0)])
                    for hl in HL:
                        P.ts("dve", Vw[hl][0:nk, :], V[0:nk, kb, hl * 64:(hl + 1) * 64], wv[hl][0:nk, 0:1], None, ALU.mult,
                             reads=[("V", kb), ("wv", hl, 0)], writes=[("Vw", hl)])
                for hl in HL:
                    P.mm(po[hl][0:64, col0:col0 + nb], V[0:nk, kb, hl * 64:(hl + 1) * 64], PT[hl][0:nk, 0:nb], True, False,
                         reads=[("PT", hl), ("V", kb)], writes=[("po", hl)])
                    P.mm(po[hl][0:64, col0:col0 + nb], V[0:nk, kb, hl * 64:(hl + 1) * 64], Dt[0:nk, hs_[hl], 0:nb], False, kb == 0,
                         reads=[("Dt",), ("V", kb)], writes=[("po", hl)])
                    if kb > 0:
                        P.mm(po[hl][0:64, col0:col0 + nb], Sbf[:, hl * 64:(hl + 1) * 64], CTs[hl][:, 0:nb], False, True,
                             reads=[("Sbf", hl), ("CTs", hl)], writes=[("po", hl)])
                if kb < 32:
                    for hl in HL:
                        P.mm(pst[:, hl * 64:(hl + 1) * 64], Bk[0:nk, kb, :], Vw[hl][0:nk, :], True, True,
                             reads=[("Bk", kb), ("Vw", hl)], writes=[("pst",)])
                    for hl in HL:
                        P.stt("dve", S32[:, hl * 64:(hl + 1) * 64], S32[:, hl * 64:(hl + 1) * 64], wv[hl][:, 1:2], pst[:, hl * 64:(hl + 1) * 64],
                              ALU.mult, ALU.add, reads=[("S32", hl), ("wv", hl, 1), ("pst",)], writes=[("S32", hl)])
                    for hl in HL:
                        P.copy("act", Sbf[:, hl * 64:(hl + 1) * 64], S32[:, hl * 64:(hl + 1) * 64], reads=[("S32", hl)], writes=[("Sbf", hl)])
            for hl in range(4):
                h = 4 * g + hl
                for cc in range(8):
                    P.mm(psx[0:64, 0:nq], wzb[:, cc, hl * 64:(hl + 1) * 64], hb[:, cc, qs:qs + nq], cc == 0, cc == 7,
                         reads=[("wzb",), ("hb", cc)], writes=[("psx",)])
                P.act(sz[0:64, 0:nq], psx[0:64, 0:nq], AF.Silu, reads=[("psx",)], writes=[("sz",)])
                P.tt("dve", bcS[hl][0:64, 0:nq], po[hl][0:64, 0:nq], sz[0:64, 0:nq], ALU.mult,
                     reads=[("po", hl), ("sz",), ("bcS", hl)], writes=[("bcS", hl)])
                P.act(sq[0:64, 0:nq], bcS[hl][0:64, 0:nq], AF.Square, reads=[("bcS", hl)], writes=[("sqq",)])
                P.mm(pst[0:64, 0:nq], ones64[:], sq[0:64, 0:nq], hl == 0, hl == 3, reads=[("sqq",), ("ones64",)], writes=[("pst",)])
            P.ts("dve", rstd[0:64, 0:nq], pst[0:64, 0:nq], RMS_EPS, None, ALU.add, reads=[("pst",)], writes=[("rstd",)])
            P.act(rstd[0:64, 0:nq], rstd[0:64, 0:nq], AF.Sqrt, reads=[("rstd",)], writes=[("rstd",)])
            P.op("dve", lambda e, o=rstd[0:64, 0:nq]: e.reciprocal(o, o), reads=[("rstd",)], writes=[("rstd",)])
            for hl in range(4):
                h = 4 * g + hl
                oi = nx() % 2
                P.tt("dve", bcS[hl][0:64, 0:nq], bcS[hl][0:64, 0:nq], rstd[0:64, 0:nq], ALU.mult,
                     reads=[("bcS", hl), ("rstd",)], writes=[("bcS", hl)])
                P.act(osb[oi][0:64, 0:nq], bcS[hl][0:64, 0:nq], AF.Identity, scale=ngt[:, h:h + 1],
                      reads=[("bcS", hl), ("ngt",)], writes=[("osb", oi)])
                P.dma("sp", oT.rows(h * 64, 64)[:, qs:qs + nq], osb[oi][0:64, 0:nq], reads=[("osb", oi)], writes=[("oin", h // 2, h, c)], group=f"oout{oi}")
    if ctx is None:
        P.emit()
        return nc


def ssd_inputs(d, hbT, hh):
    w = d['ssd_w_in'][0]
    DI = 2048
    sl = slice(hh * 1024, (hh + 1) * 1024)
    w_z = w[:, 0:DI][:, sl]
    w_x = w[:, DI:2 * DI][:, sl]
    w_B = w[:, 2 * DI:2 * DI + 1024][:, hh * 512:(hh + 1) * 512]
    w_C = w[:, 2 * DI + 1024:2 * DI + 2048][:, hh * 512:(hh + 1) * 512]
    w_dt = w[:, 2 * DI + 2048:][:, hh * 16:(hh + 1) * 16]
    cw = d['ssd_conv_w'][0]; cbv = d['ssd_conv_b'][0]
    convw = np.concatenate([cw[:, 0:DI][:, sl], cw[:, DI:DI + 1024][:, hh * 512:(hh + 1) * 512], cw[:, DI + 1024:][:, hh * 512:(hh + 1) * 512]], 1)
    convb = np.concatenate([cbv[0:DI][sl], cbv[DI:DI + 1024][hh * 512:(hh + 1) * 512], cbv[DI + 1024:][hh * 512:(hh + 1) * 512]])
    kk = np.arange(128)[:, None]; qq = np.arange(128)[None, :]
    negm = np.where(kk <= qq, 0.0, NEG).astype(np.float32)
    sel = np.zeros((16, 16, 128), np.float32)
    for h in range(16):
        sel[h, h, :] = 1.0
    c = np.ascontiguousarray
    return dict(hbT=hbT, w_z=c(w_z), w_x=c(w_x), w_B=c(w_B), w_C=c(w_C), w_dt=c(w_dt), convw=c(convw), convb=c(convb),
                dtb=c(d['ssd_dt_bias'][0][hh * 16:(hh + 1) * 16]), alog=c(d['ssd_a_log'][0][hh * 16:(hh + 1) * 16]),
                dskip=c(d['ssd_d'][0][hh * 16:(hh + 1) * 16]), ng=c(d['ssd_norm_g'][0][sl]),
                ident=np.eye(128, dtype=np.float32), negm=negm, sel=sel.reshape(16, 2048))


def build_t(prev, n_ffn, want_bf):
    nc = bass.Bass("TRN2", target_bir_lowering=False)
    dt = lambda n, s, d=F32: nc.dram_tensor(n, s, d, kind="ExternalInput").ap()
    hT = dt("hT", [D, NT])
    nln = n_ffn + (1 if prev else 0)
    lg = dt("ln_g", [nln, D]); lb = dt("ln_b", [nln, D])
    ws = [(dt(f"w1_{i}", [D, FF]), dt(f"w3_{i}", [D, FF]), dt(f"w2_{i}", [FF, D])) for i in range(n_ffn)]
    if prev in ("lin8", "lin16"):
        kc = 8 if prev == "lin8" else 16
        oT = dt("oT", [kc * 128, NT], BF16); w_out = dt("w_out", [kc * 128, D])
    elif prev == "glu":
        oT = dt("oT", [D, NT], BF16); w_glu = dt("w_glu", [D, 2 * D]); b_glu = dt("b_glu", [2 * D])
    o32 = nc.dram_tensor("o32", [D, NT], F32, kind="ExternalOutput").ap()
    obf = nc.dram_tensor("obf", [D, NT], BF16, kind="ExternalOutput").ap() if want_bf else None
    P = Prog(nc)
    T = TPhase(P, nc)
    T.load_ln(lg, lb, nln)
    T.load_h(hT)
    s = 0
    if prev in ("lin8", "lin16"):
        T.outproj(oT, w_out, kc); T.layer_norm(s); s += 1
    elif prev == "glu":
        T.glu(oT, w_glu, b_glu); T.layer_norm(s); s += 1
    for i in range(n_ffn):
        T.ffn(*ws[i]); T.layer_norm(s); s += 1
    T.store_h(o32, obf)
    P.emit()
    return nc


def s5_inputs(d, hbT, hh):
    G0 = hh * 32
    lam_re = d['s5_lam_re'][0][G0:G0 + 32]; lam_im = d['s5_lam_im'][0][G0:G0 + 32]
    lstep = np.repeat(d['s5_log_step'][0][G0:G0 + 32][:, None], 64, 1)

    def pl(a):
        return np.ascontiguousarray(a.reshape(16, 2, 64).transpose(1, 2, 0).reshape(128, 16))
    pl3 = np.stack([pl(lam_re), pl(lam_im), pl(lstep)]).astype(np.float32)
    bc3 = np.stack([lam_re.reshape(-1), lam_im.reshape(-1), lstep.reshape(-1)]).astype(np.float32)
    bpad = np.zeros((2, 128, 16, 128), np.float32)
    cpad = np.zeros((2, 128, 16, 128), np.float32)
    for k, (bn, cn) in enumerate([('s5_b_re', 's5_c_re'), ('s5_b_im', 's5_c_im')]):
        B = d[bn][0][G0:G0 + 32]
        C = d[cn][0][G0:G0 + 32]
        for g in range(32):
            j, gl, g8 = g // 2, g % 2, g % 8
            bpad[k, g8 * 16:(g8 + 1) * 16, j, gl * 64:(gl + 1) * 64] = B[g].T
            cpad[k, gl * 64:(gl + 1) * 64, j, g8 * 16:(g8 + 1) * 16] = C[g].T
    dv = d['s5_d'][0][G0:G0 + 32].reshape(4, 128)
    dpad = np.zeros((128, 4, 128), np.float32)
    for c in range(4):
        dpad[np.arange(128), c, np.arange(128)] = dv[c]
    return dict(hbh=None if hbT is None else np.ascontiguousarray(hbT[hh * 512:(hh + 1) * 512]), pl3=pl3, bc3=bc3,
                bpad=bpad.reshape(2, 128, 2048), cpad=cpad.reshape(2, 128, 2048), dpad=dpad.reshape(128, 512),
                iota=np.arange(129, dtype=np.float32))


_DBG = None


def _run(nc, in_maps):
    res = run_bass_kernel_spmd(nc, in_maps, core_ids=list(range(NCORES)))
    if _DBG is not None:
        _DBG.append(res.results)
    return res.results


def _seq_from_shards(shards):
    return [np.ascontiguousarray(np.concatenate([shards[2 * b], shards[2 * b + 1][:, 16:]], axis=1)) for b in range(4)]


def _shards_from_seq(seqs_halves):
    out = []
    for r in range(NCORES):
        b, half = r // 2, r % 2
        full = np.concatenate([seqs_halves[2 * b], seqs_halves[2 * b + 1]], axis=0)
        if half == 0:
            out.append(np.ascontiguousarray(full[:, 0:NT]))
        else:
            z = np.zeros((full.shape[0], 16), full.dtype)
            out.append(np.ascontiguousarray(np.concatenate([z, full[:, NT:]], axis=1)))
    return out


def _bucket_table(n):
    dist = np.arange(n)
    max_exact = 16
    df = np.maximum(dist, max_exact).astype(np.float32)
    large = max_exact + (np.log(df / np.float32(max_exact)) / np.float32(math.log(128 / max_exact)) * np.float32(32 - max_exact)).astype(np.int32)
    large = np.minimum(large, 31)
    return np.where(dist < max_exact, dist, large)


def diff_consts():
    tab = _bucket_table(400)
    ki = np.arange(128)[:, None]; qi = np.arange(256)[None, :]
    dist = qi - ki
    bk = tab[np.maximum(dist, 0)]
    m_dd = np.stack([((bk == b) & (dist >= 0)) for b in range(32)]).astype(np.float32)
    n_dd = np.where(dist >= 0, 0.0, NEG).astype(np.float32)
    k2 = np.arange(16)[:, None]; q2 = np.arange(16)[None, :]
    d2 = q2 - k2
    b2 = tab[np.maximum(d2, 0)]
    m_md = np.stack([((b2 == b) & (d2 >= 0)) for b in range(32)]).astype(np.float32)
    n_md = np.where(d2 >= 0, 0.0, NEG).astype(np.float32)
    q3 = np.arange(128)[None, :]
    d3 = (16 + q3) - k2
    b3 = tab[d3]
    m_mq1 = np.stack([(b3 == b) for b in range(32)]).astype(np.float32)
    return dict(m_dd=m_dd, n_dd=n_dd, m_md=m_md, n_md=n_md, m_mq1=m_mq1)


def diff_inputs(d, hbT, hh, consts):
    w = d['diff_w_qkv'][0]
    sl = slice(hh * 512, (hh + 1) * 512)
    lam4 = np.stack([d['diff_lam_q1'][0], d['diff_lam_k1'][0], d['diff_lam_q2'][0], d['diff_lam_k2'][0]])
    rb = np.ascontiguousarray(d['rel_bias'][:, hh * 4:(hh + 1) * 4]).reshape(-1)
    c = np.ascontiguousarray
    r = dict(hbT=hbT, w_q=c(w[:, 0:1024][:, sl]), w_k=c(w[:, 1024:2048][:, sl]), w_v=c(w[:, 2048:3072][:, sl]),
             lam4=c(lam4), subg=d['diff_subln_g'][0], rb=rb)
    r.update(consts)
    return r


def mla_inputs(d, hbT, hh):
    w_in = d['mla_w_in'][0]
    kr = w_in[:, 640:672]
    krs = np.concatenate([kr[:, 16:], kr[:, :16]], 1)
    z = np.zeros((1024, 64), np.float32)
    w_in_ext = np.concatenate([w_in[:, :640], z, kr, z, krs], 1)
    wq = d['mla_w_uq'][0].reshape(384, 16, 96)[:, hh * 8:(hh + 1) * 8]
    wqs = np.concatenate([wq[:, :, :64], wq[:, :, 80:96], wq[:, :, 64:80]], 2)
    wkv = d['mla_w_ukv'][0].reshape(256, 16, 128)[:, hh * 8:(hh + 1) * 8]
    pos = np.arange(LSEQ, dtype=np.float32)
    inv = (np.float32(10000.0) ** (-np.arange(0, 32, 2, dtype=np.float32) / np.float32(32))).astype(np.float32)
    ang = pos[None, :] * inv[:, None]
    cos = np.cos(ang).astype(np.float32); sin = np.sin(ang).astype(np.float32)
    ropeC = np.concatenate([cos, cos], 0); ropeS = np.concatenate([-sin, sin], 0)
    kk = np.arange(128)[:, None]; qq = np.arange(128)[None, :]
    mask = np.where(kk <= qq, 0.0, NEG).astype(np.float32)
    c = np.ascontiguousarray
    return dict(hbT=hbT, w_in_ext=c(w_in_ext), w_uq=c(wq.reshape(384, 768)), w_uq_sw=c(wqs.reshape(384, 768)),
                w_uk=c(wkv[:, :, :64].reshape(256, 512)), w_uv=c(wkv[:, :, 64:].reshape(256, 512)),
                qg=d['mla_q_norm_g'][0], kvg=d['mla_kv_norm_g'][0], ropeC=c(ropeC), ropeS=c(ropeS), mask_dd=mask)


MIX = ["ssd", "diff", "s5", "mla"]
MIX_FH = {"ssd": 1024, "diff": 512, "s5": 512, "mla": 512}
MIX_PREV = {"ssd": ("lin16", 16), "diff": ("lin8", 8), "s5": ("glu", 8), "mla": ("lin8", 8)}


def build_fused(nstage=9):
    nc = bass.Bass("TRN2", target_bir_lowering=False)
    ein = lambda n, s, d=F32: nc.dram_tensor(n, s, d, kind="ExternalInput").ap()
    hT = ein("hT", [D, NT])
    selv_d = ein("selv", [128, 2])
    out32 = nc.dram_tensor("out32", [D, NT], F32, kind="ExternalOutput").ap()
    hres = nc.dram_tensor("hres", [D, NT], F32).ap()
    xin = [[nc.dram_tensor(f"xin{i}_{k}", [256, NT], BF16).ap() for k in range(4)] for i in range(4)]
    xall = [[nc.dram_tensor(f"xall{i}_{k}", [512, NT], BF16).ap() for k in range(4)] for i in range(4)]
    oin = [[nc.dram_tensor(f"oin{i}_{j}", [128, LSEQ], BF16).ap() for j in range(MIX_FH[MIX[i]] // 128)] for i in range(4)]
    oall = [[nc.dram_tensor(f"oall{i}_{j}", [256, LSEQ], BF16).ap() for j in range(MIX_FH[MIX[i]] // 128)] for i in range(4)]
    P = Prog(nc, arena=True)

    def t_phase(ti):
        pre = f"t{ti}_"
        last = (ti == 4)
        n_ffn = 1 if (ti == 0 or last) else 2
        nln = n_ffn + (0 if ti == 0 else 1)
        lg = ein(pre + "ln_g", [nln, D]); lb = ein(pre + "ln_b", [nln, D])
        ws = [(ein(pre + f"w1_{k}", [D, FF]), ein(pre + f"w3_{k}", [D, FF]), ein(pre + f"w2_{k}", [FF, D])) for k in range(n_ffn)]
        T = TPhase(P, nc)
        T.load_ln(lg, lb, nln)
        s = 0
        if ti == 0:
            T.load_h(hT)
        else:
            T.load_h(hres)
            sv = P.sbuf([128, 2], F32, "selv")
            P.dma("sp", sv[:], selv_d, writes=[("selv",)], group="const")
            T.selv = sv
            prev, kc = MIX_PREV[MIX[ti - 1]]
            if prev == "glu":
                w_glu = ein(pre + "w_glu", [D, 2 * D]); b_glu = ein(pre + "b_glu", [2 * D])
                T.glu(oall[ti - 1], w_glu, b_glu)
            else:
                w_out = ein(pre + "w_out", [kc * 128, D])
                T.outproj(oall[ti - 1], w_out, kc)
            T.layer_norm(s); s += 1
        for k in range(n_ffn):
            T.ffn(*ws[k]); T.layer_norm(s); s += 1
        if last or (2 * ti + 1 >= nstage):
            T.store_h(out32, None)
        else:
            T.store_h(hres, xin[ti])
            for k in range(4):
                P.cc_allgather(xin[ti][k], xall[ti][k], reads=[("xin", 2 * k), ("xin", 2 * k + 1)], writes=[("xall", k)])

    t_phase(0)
    for i in range(4):
        if 2 * i + 1 >= nstage:
            break
        P.phase_begin()
        pre = f"m{i}_"
        sv = P.sbuf([128, 2], F32, "selv")
        P.dma("sp", sv[:], selv_d, writes=[("selv",)], group="const")
        ctx = Ctx(nc, P, pre, xall=xall[i], oin=oin[i], selv=sv)
        if MIX[i] == "ssd":
            build_ssd(ctx)
        elif MIX[i] == "diff":
            build_diff(0.8 - 0.6 * math.exp(-0.3 * i), ctx)
        elif MIX[i] == "s5":
            build_s5(ctx)
        else:
            build_mla(ctx)
        for j in range(len(oin[i])):
            okeys = [k for k in P.last_w if isinstance(k, tuple) and len(k) > 1 and k[0] == "oin" and k[1] == j]
            P.cc_allgather(oin[i][j], oall[i][j], reads=okeys, writes=[("oall", j)])
        P.phase_begin()
        t_phase(i + 1)
    P.emit()
    return nc


def fused_inputs(d, r, hs, consts, nstage=9):
    c = np.ascontiguousarray
    b, half = r // 2, r % 2
    m = dict(hT=hs[r], selv=np.tile(np.array([[1.0, 0.0]] if half == 0 else [[0.0, 1.0]], np.float32), (128, 1)))

    def ffnw(pre, i, j, k):
        m[pre + f"w1_{k}"] = c(d['ffn_w1'][i, j]); m[pre + f"w3_{k}"] = c(d['ffn_w3'][i, j]); m[pre + f"w2_{k}"] = c(d['ffn_w2'][i, j])

    m["t0_ln_g"] = c(d['ln_g'][0, 0:1]); m["t0_ln_b"] = c(d['ln_b'][0, 0:1])
    ffnw("t0_", 0, 0, 0)
    for i in range(DEPTH):
        if 2 * i + 1 >= nstage:
            break
        pre = f"m{i}_"
        if i == 0:
            mi = ssd_inputs(d, None, half)
        elif i == 1:
            mi = diff_inputs(d, None, half, consts)
        elif i == 2:
            mi = s5_inputs(d, None, half)
        else:
            mi = mla_inputs(d, None, half)
        for k_, v in mi.items():
            if k_ in ("hbT", "hbh"):
                continue
            m[pre + k_] = v
        tp = f"t{i + 1}_"
        last = (i == DEPTH - 1)
        lng = [d['ln_g'][i, 1], d['ln_g'][i, 2]] + ([] if last else [d['ln_g'][i + 1, 0]])
        lnb = [d['ln_b'][i, 1], d['ln_b'][i, 2]] + ([] if last else [d['ln_b'][i + 1, 0]])
        m[tp + "ln_g"] = c(np.stack(lng)); m[tp + "ln_b"] = c(np.stack(lnb))
        ffnw(tp, i, 1, 0)
        if not last:
            ffnw(tp, i + 1, 0, 1)
        if i == 0:
            m[tp + "w_out"] = c(d['ssd_w_out'][0])
        elif i == 1:
            m[tp + "w_out"] = c(d['diff_w_out'][0])
        elif i == 2:
            m[tp + "w_glu"] = c(d['s5_w_glu'][0]); m[tp + "b_glu"] = c(d['s5_b_glu'][0])
        else:
            m[tp + "w_out"] = c(d['mla_w_out'][0])
    return m


NSTAGE = 9


def kernel(**inputs):
    d = {k: np.asarray(v) for k, v in inputs.items()}
    x, meta = d['x'], d['meta']
    c = np.ascontiguousarray
    hs = []
    for r in range(NCORES):
        b, half = r // 2, r % 2
        if half == 0:
            t = np.concatenate([meta, x[b, :2048]], 0)
        else:
            t = np.concatenate([np.zeros_like(meta), x[b, 2048:]], 0)
        hs.append(c(t.T))
    consts = diff_consts()
    nc = build_fused(NSTAGE)
    ims = [fused_inputs(d, r, hs, consts, NSTAGE) for r in range(NCORES)]
    res = _run(nc, ims)
    if _DBG is not None:
        return None
    out = np.empty((4, 4096, D), np.float32)
    for r in range(NCORES):
        b, half = r // 2, r % 2
        out[b, half * 2048:(half + 1) * 2048, :] = res[r]['out32'][:, 16:].T
    return out
```

```python
import math
from contextlib import ExitStack
import numpy as np
import ml_dtypes
import concourse.bass as bass
import concourse.mybir as mybir
from concourse.bass_utils import run_bass_kernel_spmd

F32 = mybir.dt.float32
BF16 = mybir.dt.bfloat16
AF = mybir.ActivationFunctionType
ALU = mybir.AluOpType
AX = mybir.AxisListType

D = 1024
DC = 8
FF = 2816
FC = 22
DEPTH = 4
ALPHA = (2.0 * DEPTH) ** 0.25
LN_EPS = 1e-5
RMS_EPS = 1e-6
NCORES = 8
DEBUG_OPS = False
NT = 2064
LSEQ = 4112
TCH = [(0, 512), (512, 512), (1024, 512), (1536, 512), (2048, 16)]


class Prog:
    ENGS = ("sp", "pe", "act", "dve", "pool")

    ARENA_F32 = 53000

    def __init__(self, nc, arena=False):
        self.nc = nc
        self.ops = []
        self.last_w = {}
        self.readers = {}
        self.es = ExitStack()
        self.ncnt = 0
        self.barriers = []
        self.arena = None
        if arena:
            self.arena = self.es.enter_context(nc.sbuf_tensor("arena", [128, self.ARENA_F32], F32))
            self.banks = [self.es.enter_context(nc.psum_tensor(f"bank{i}", [128, 512], F32)) for i in range(8)]
            self.aoff = 0
            self.nbank = 0

    def phase_begin(self):
        if DEBUG_OPS:
            print("phase end: arena bytes", self.aoff, "banks", self.nbank, "ops", len(self.ops))
        self.barriers.append(len(self.ops))
        self.last_w = {}
        self.readers = {}
        self.aoff = 0
        self.nbank = 0

    def sbuf(self, shape, dt, name=None):
        self.ncnt += 1
        if self.arena is None:
            return self.es.enter_context(self.nc.sbuf_tensor(name or f"sb{self.ncnt}", list(shape), dt))
        esz = 2 if dt == BF16 else 4
        n = 1
        for s in shape[1:]:
            n *= s
        nbytes = ((n * esz + 31) // 32) * 32
        w0 = self.aoff // 4
        self.aoff += nbytes
        assert self.aoff <= self.ARENA_F32 * 4, f"arena overflow allocating {name} {shape}: {self.aoff}"
        v = self.arena[0:shape[0], w0:w0 + nbytes // 4]
        if dt != F32:
            v = v.bitcast(dt)
        v = v[:, 0:n]
        if len(shape) == 3:
            v = v.rearrange("p (a b) -> p a b", b=shape[2])
        elif len(shape) == 4:
            v = v.rearrange("p (a b c) -> p a b c", b=shape[2], c=shape[3])
        return v

    def psum(self, shape, dt, name=None):
        self.ncnt += 1
        if self.arena is None:
            return self.es.enter_context(self.nc.psum_tensor(name or f"ps{self.ncnt}", list(shape), dt))
        assert dt == F32 and self.nbank < 8
        b = self.banks[self.nbank]
        self.nbank += 1
        return b[0:shape[0], 0:shape[1]]

    def cc_allgather(self, in_ap, out_ap, reads=(), writes=()):
        rg = [[0, 1], [2, 3], [4, 5], [6, 7]]
        return self.op("pool", lambda e: e.collective_compute("AllGather", ALU.bypass, replica_groups=rg,
                                                              ins=[in_ap.opt()], outs=[out_ap.opt()]),
                       reads, writes, group="__cc__")

    def op(self, eng, fn, reads=(), writes=(), group=None):
        i = len(self.ops)
        deps = set()
        for k in reads:
            w = self.last_w.get(k)
            if w is not None:
                deps.add(w)
        for k in writes:
            w = self.last_w.get(k)
            if w is not None:
                deps.add(w)
            for r in self.readers.get(k, ()):
                deps.add(r)
        for k in reads:
            self.readers.setdefault(k, []).append(i)
        for k in writes:
            self.last_w[k] = i
            self.readers[k] = []
        deps.discard(i)
        self.ops.append(dict(eng=eng, fn=fn, deps=sorted(deps), group=group))
        if DEBUG_OPS:
            import traceback
            self.ops[-1]["where"] = "".join(traceback.format_stack(limit=5)[:-1])
        return i

    def dma(self, eng, out, in_, reads=(), writes=(), group=None, slow=False):
        assert group is not None
        if slow:
            return self.op(eng, lambda e: e.dma_start(out=out, in_=in_, allow_slow_non_contiguous=True), reads, writes, group)
        return self.op(eng, lambda e: e.dma_start(out=out, in_=in_), reads, writes, group)


    def mm(self, out, lhsT, rhs, start, stop, reads=(), writes=()):
        return self.op("pe", lambda e: e.matmul(out, lhsT, rhs, start=start, stop=stop), reads, writes)

    def tr(self, out, in_, ident, reads=(), writes=()):
        return self.op("pe", lambda e: e.transpose(out, in_, ident), reads, writes)

    def act(self, out, in_, func, bias=None, scale=None, reads=(), writes=(), accum_out=None):
        kw = {}
        if bias is not None:
            kw["bias"] = bias
        if scale is not None:
            kw["scale"] = scale
        if accum_out is not None:
            kw["accum_out"] = accum_out
        return self.op("act", lambda e: e.activation(out, in_, func, **kw), reads, writes)

    def tt(self, eng, out, in0, in1, op, reads=(), writes=()):
        return self.op(eng, lambda e: e.tensor_tensor(out, in0, in1, op), reads, writes)

    def ts(self, eng, out, in0, s1, s2, op0, op1=None, reads=(), writes=(), accum_out=None):
        def f(e):
            kw = {}
            if accum_out is not None:
                kw["accum_out"] = accum_out
            if op1 is None:
                return e.tensor_scalar(out, in0, s1, None, op0, **kw)
            return e.tensor_scalar(out, in0, s1, s2, op0, op1, **kw)
        return self.op(eng, f, reads, writes)

    def stt(self, eng, out, in0, scalar, in1, op0, op1, reads=(), writes=()):
        return self.op(eng, lambda e: e.scalar_tensor_tensor(out, in0, scalar, in1, op0, op1), reads, writes)

    def copy(self, eng, out, in_, reads=(), writes=()):
        if eng == "act":
            return self.op("act", lambda e: e.copy(out, in_), reads, writes)
        return self.op(eng, lambda e: e.tensor_copy(out, in_), reads, writes)

    def memset(self, eng, ap, val, writes=()):
        return self.op(eng, lambda e: e.memset(ap, val), (), writes)

    NDS = 40

    def emit(self):
        nc = self.nc
        ops = self.ops
        needed = [False] * len(ops)

        def skip(p, o):
            return (p["group"] is None and o["group"] is None and p["eng"] == "pe" and o["eng"] == "pe")

        for o in ops:
            for d in o["deps"]:
                if not skip(ops[d], o):
                    needed[d] = True
        for b in self.barriers:
            for e in self.ENGS:
                for i in range(b - 1, -1, -1):
                    if ops[i]["eng"] == e and ops[i]["group"] is None:
                        needed[i] = True
                        break
        ecount = {e: 0 for e in self.ENGS}
        ndma = 0
        ncc = 0
        for i, o in enumerate(ops):
            o["idx"] = i
            if o["group"] == "__cc__":
                ncc += 1
                o["sem"] = ("cc",)
                o["val"] = ncc
            elif o["group"] is not None:
                o["sem"] = ("d", ndma % self.NDS)
                o["val"] = 16 * (ndma // self.NDS + 1)
                o["dn"] = ndma
                ndma += 1
            else:
                o["sem"] = ("e", o["eng"])
                if needed[i]:
                    ecount[o["eng"]] += 1
                    o["val"] = ecount[o["eng"]]
                else:
                    o["val"] = None
        sems = {}
        for e in self.ENGS:
            sems[("e", e)] = self.es.enter_context(nc.semaphore(f"sem_{e}"))
        for k in range(min(self.NDS, max(ndma, 1))):
            sems[("d", k)] = self.es.enter_context(nc.semaphore(f"semd_{k}"))
        sems[("cc",)] = self.es.enter_context(nc.semaphore("sem_cc"))
        bvals = []
        for b in self.barriers:
            vals = {}
            for o in ops[:b]:
                if o["val"] is not None:
                    vals[o["sem"]] = max(vals.get(o["sem"], 0), o["val"])
            bvals.append(vals)
        per = {e: [o for o in ops if o["eng"] == e] for e in self.ENGS}
        final = {}
        for o in ops:
            if o["group"] is not None:
                final[o["sem"]] = o["val"]
        final.pop(("cc",), None) if ncc == 0 else None

        def run(engobj, ename):
            waited = {}
            nb = 0
            for o in per[ename]:
                while nb < len(self.barriers) and o["idx"] >= self.barriers[nb]:
                    for key, val in bvals[nb].items():
                        if key == ("e", "pe") and ename == "pe":
                            continue
                        if waited.get(key, 0) < val:
                            engobj.wait_ge(sems[key], val)
                            waited[key] = val
                    nb += 1
                for d in o["deps"]:
                    p = ops[d]
                    if skip(p, o):
                        continue
                    key, val = p["sem"], p["val"]
                    if waited.get(key, 0) < val:
                        engobj.wait_ge(sems[key], val)
                        waited[key] = val
                if o["group"] is not None and o["group"] != "__cc__" and o["dn"] >= self.NDS:
                    key, val = o["sem"], o["val"] - 16
                    if waited.get(key, 0) < val:
                        engobj.wait_ge(sems[key], val)
                        waited[key] = val
                try:
                    ins = o["fn"](engobj)
                except Exception:
                    print("FAILED OP", o.get("idx"), o["eng"], o.get("where", ""))
                    raise
                if o["group"] == "__cc__":
                    ins.then_inc(sems[o["sem"]])
                elif o["group"] is not None:
                    ins.then_inc(sems[o["sem"]], 16)
                elif o["val"] is not None:
                    ins.then_inc(sems[o["sem"]], 1)
            if ename == "sp":
                for key, v in final.items():
                    if waited.get(key, 0) < v:
                        engobj.wait_ge(sems[key], v)

        with nc.Block() as block:
            @block.sync
            def _(e):
                run(e, "sp")

            @block.tensor
            def _(e):
                run(e, "pe")

            @block.scalar
            def _(e):
                run(e, "act")

            @block.vector
            def _(e):
                run(e, "dve")

            @block.gpsimd
            def _(e):
                run(e, "pool")
        self.es.close()


def bcast_rows(ap, n):
    return ap.partition_broadcast(n)


class TPhase:
    def __init__(self, P, nc):
        self.P = P
        self.nc = nc
        self.h32 = P.sbuf([128, DC, NT], F32, "h32")
        self.hbf = P.sbuf([128, DC, NT], BF16, "hbf")
        self.wbuf = P.sbuf([128, 16384], BF16, "wbuf")
        self.gbuf = P.sbuf([128, 12384], BF16, "gbuf")
        self.w13s = [P.sbuf([128, 2, DC, 128], F32, f"w13s{i}") for i in range(2)]
        self.w13b = [P.sbuf([128, 2, DC, 128], BF16, f"w13b{i}") for i in range(2)]
        self.w2s = [P.sbuf([128, 1024], F32, f"w2s{i}") for i in range(2)]
        self.sq = [P.sbuf([128, 512], F32, f"sq{i}") for i in range(2)]
        self.t1 = [P.sbuf([128, 512], F32, f"t1_{i}") for i in range(2)]
        self.sg = [P.sbuf([128, 512], F32, f"sg{i}") for i in range(2)]
        self.st_m = P.sbuf([128, 512], F32, "st_m")
        self.st_r = P.sbuf([128, 512], F32, "st_r")
        self.st_v = P.sbuf([128, 512], F32, "st_v")
        self.ones = P.sbuf([128, 128], F32, "ones")
        self.lng = P.sbuf([128, 3 * DC], F32, "lng")
        self.lnb = P.sbuf([128, 3 * DC], F32, "lnb")
        self.bglu = P.sbuf([128, 16], F32, "bglu")
        self.ps_a = [P.psum([128, 512], F32, f"ps_a{i}") for i in range(2)]
        self.ps_b = [P.psum([128, 512], F32, f"ps_b{i}") for i in range(2)]
        self.ps_o = [P.psum([128, 512], F32, f"ps_o{i}") for i in range(2)]
        self.ps_m = P.psum([128, 512], F32, "ps_m")
        self.ps_q = P.psum([128, 512], F32, "ps_q")
        self.cnt = 0
        self.c_w13 = 0
        self.c_up = 0
        P.memset("pool", self.ones[:], 1.0 / D, writes=[("ones",)])

    def nxt(self):
        self.cnt += 1
        return self.cnt - 1

    def load_h(self, hT_dram):
        P = self.P
        for c in range(DC):
            P.dma("sp", self.h32[:, c, :], hT_dram[c * 128:(c + 1) * 128, :],
                  writes=[("h32", c, t) for t in range(5)], group=f"hload{c % 4}")
            for ti, (t0, tn) in enumerate(TCH):
                eng = "act" if (ti % 2 == 0) else "dve"
                P.copy(eng, self.hbf[:, c, t0:t0 + tn], self.h32[:, c, t0:t0 + tn],
                       reads=[("h32", c, ti)], writes=[("hbf", c, ti)])

    def load_ln(self, g_dram, b_dram, nsets):
        P = self.P
        for s in range(nsets):
            P.dma("sp", self.lng[:, s * DC:(s + 1) * DC], g_dram[s, :].rearrange("(c p) -> p c", p=128),
                  writes=[("lng",)], group="const", slow=True)
            P.dma("sp", self.lnb[:, s * DC:(s + 1) * DC], b_dram[s, :].rearrange("(c p) -> p c", p=128),
                  writes=[("lnb",)], group="const", slow=True)

    def scale_h(self, alpha):
        P = self.P
        for c in range(DC):
            for ti, (t0, tn) in enumerate(TCH):
                P.ts("pool", self.h32[:, c, t0:t0 + tn], self.h32[:, c, t0:t0 + tn], float(alpha), None, ALU.mult,
                     reads=[("h32", c, ti)], writes=[("h32", c, ti)])

    def layer_norm(self, lnset):
        P = self.P
        for ti, (t0, tn) in enumerate(TCH):
            for c in range(DC):
                k = self.nxt() % 2
                sq = self.sq[k]
                P.act(sq[:, 0:tn], self.h32[:, c, t0:t0 + tn], AF.Square,
                      reads=[("h32", c, ti)], writes=[("sq", k)])
                P.mm(self.ps_m[:, 0:tn], self.ones[:], self.h32[:, c, t0:t0 + tn], c == 0, c == DC - 1,
                     reads=[("h32", c, ti), ("ones",)], writes=[("ps_m",)])
                P.mm(self.ps_q[:, 0:tn], self.ones[:], sq[:, 0:tn], c == 0, c == DC - 1,
                     reads=[("sq", k), ("ones",)], writes=[("ps_q",)])
            P.copy("act", self.st_m[:, 0:tn], self.ps_m[:, 0:tn], reads=[("ps_m",)], writes=[("st_m",)])
            P.tt("dve", self.st_v[:, 0:tn], self.st_m[:, 0:tn], self.st_m[:, 0:tn], ALU.mult,
                 reads=[("st_m",)], writes=[("st_v",)])
            P.tt("dve", self.st_v[:, 0:tn], self.ps_q[:, 0:tn], self.st_v[:, 0:tn], ALU.subtract,
                 reads=[("ps_q",), ("st_v",)], writes=[("st_v",)])
            P.ts("dve", self.st_v[:, 0:tn], self.st_v[:, 0:tn], LN_EPS, None, ALU.add,
                 reads=[("st_v",)], writes=[("st_v",)])
            P.act(self.st_v[:, 0:tn], self.st_v[:, 0:tn], AF.Sqrt, reads=[("st_v",)], writes=[("st_v",)])
            P.op("dve", lambda e, o=self.st_r[:, 0:tn], i=self.st_v[:, 0:tn]: e.reciprocal(o, i),
                 reads=[("st_v",)], writes=[("st_r",)])
            for c in range(DC):
                k = self.nxt() % 2
                t1 = self.t1[k]
                P.tt("dve", t1[:, 0:tn], self.h32[:, c, t0:t0 + tn], self.st_m[:, 0:tn], ALU.subtract,
                     reads=[("h32", c, ti), ("st_m",)], writes=[("t1", k)])
                P.tt("dve", t1[:, 0:tn], t1[:, 0:tn], self.st_r[:, 0:tn], ALU.mult,
                     reads=[("t1", k), ("st_r",)], writes=[("t1", k)])
                col = lnset * DC + c
                P.act(self.h32[:, c, t0:t0 + tn], t1[:, 0:tn], AF.Identity,
                      bias=self.lnb[:, col:col + 1], scale=self.lng[:, col:col + 1],
                      reads=[("t1", k), ("lng",), ("lnb",)], writes=[("h32", c, ti)])
                P.act(self.hbf[:, c, t0:t0 + tn], t1[:, 0:tn], AF.Identity,
                      bias=self.lnb[:, col:col + 1], scale=self.lng[:, col:col + 1],
                      reads=[("t1", k), ("lng",), ("lnb",)], writes=[("hbf", c, ti)])

    def ffn(self, w1, w3, w2):
        P = self.P
        groups = [(0, 6), (6, 6), (12, 5), (17, 5)]
        GSTR = 2064
        for gi, (f0, nf) in enumerate(groups):
            wsl = gi % 2
            for j in range(nf):
                fc = f0 + j
                k = self.nxt() % 2
                P.dma("sp", self.w2s[k][:], w2[fc * 128:(fc + 1) * 128, :],
                      writes=[("w2s", k)], group=f"w2s{k}")
                off = wsl * 8192 + j * 1024
                P.copy("act", self.wbuf[:, off:off + 1024], self.w2s[k][:],
                       reads=[("w2s", k)], writes=[("wbuf", wsl, j)])
            for j in range(nf):
                fc = f0 + j
                s = self.c_w13 % 2
                self.c_w13 += 1
                P.dma("sp", self.w13s[s][:, 0, :, :], w1[:, fc * 128:(fc + 1) * 128].rearrange("(c p) f -> p c f", p=128),
                      writes=[("w13s", s, 0)], group=f"w13s{s}a")
                P.dma("sp", self.w13s[s][:, 1, :, :], w3[:, fc * 128:(fc + 1) * 128].rearrange("(c p) f -> p c f", p=128),
                      writes=[("w13s", s, 1)], group=f"w13s{s}b")
                P.copy("act", self.w13b[s][:, 0, :, :], self.w13s[s][:, 0, :, :],
                       reads=[("w13s", s, 0)], writes=[("w13b", s, 0)])
                P.copy("pool", self.w13b[s][:, 1, :, :], self.w13s[s][:, 1, :, :],
                       reads=[("w13s", s, 1)], writes=[("w13b", s, 1)])
                for ti, (t0, tn) in enumerate(TCH):
                    kk = self.c_up % 2
                    self.c_up += 1
                    pa, pb, sg = self.ps_a[kk], self.ps_b[kk], self.sg[kk]
                    for c in range(DC):
                        P.mm(pa[:, 0:tn], self.w13b[s][:, 0, c, :], self.hbf[:, c, t0:t0 + tn], c == 0, c == DC - 1,
                             reads=[("w13b", s, 0), ("hbf", c, ti)], writes=[("ps_a", kk)])
                    for c in range(DC):
                        P.mm(pb[:, 0:tn], self.w13b[s][:, 1, c, :], self.hbf[:, c, t0:t0 + tn], c == 0, c == DC - 1,
                             reads=[("w13b", s, 1), ("hbf", c, ti)], writes=[("ps_b", kk)])
                    P.act(sg[:, 0:tn], pa[:, 0:tn], AF.Silu, reads=[("ps_a", kk)], writes=[("sg", kk)])
                    goff = j * GSTR + t0
                    P.stt("dve", self.gbuf[:, goff:goff + tn], sg[:, 0:tn], 0.5, pb[:, 0:tn], ALU.mult, ALU.mult,
                          reads=[("sg", kk), ("ps_b", kk)], writes=[("gbuf", j, ti)])
            for ti, (t0, tn) in enumerate(TCH):
                for c in range(DC):
                    kk = self.nxt() % 2
                    po = self.ps_o[kk]
                    for j in range(nf):
                        off = wsl * 8192 + j * 1024 + c * 128
                        goff = j * GSTR + t0
                        P.mm(po[:, 0:tn], self.wbuf[:, off:off + 128], self.gbuf[:, goff:goff + tn], j == 0, j == nf - 1,
                             reads=[("wbuf", wsl, j), ("gbuf", j, ti)], writes=[("ps_o", kk)])
                    P.stt("dve", self.h32[:, c, t0:t0 + tn], self.h32[:, c, t0:t0 + tn], float(ALPHA) if gi == 0 else 1.0, po[:, 0:tn], ALU.mult, ALU.add,
                          reads=[("ps_o", kk), ("h32", c, ti)], writes=[("h32", c, ti)])

    selv = None

    def load_o(self, oT_dram, j, t0, tn):
        P = self.P
        gk = [("gbuf", jj, tt_) for jj in range(6) for tt_ in range(5)] + [("gbo", j)]
        if self.selv is None:
            P.dma("sp", self.gbuf[:, j * 512:j * 512 + tn], oT_dram[j * 128:(j + 1) * 128, t0:t0 + tn], writes=gk, group="gbo")
            return
        k = self.nxt() % 2
        stg = self.t1[k][:].bitcast(BF16)
        nh = len(oT_dram)
        srcj = oT_dram[j % nh][(j // nh) * 128:(j // nh) * 128 + 128, :]
        P.dma("sp", stg[:, 0:tn], srcj[:, t0:t0 + tn], reads=[("oall", j % nh)], writes=[("t1", k)], group="gbo")
        P.dma("sp", stg[:, 512:512 + tn], srcj[:, 2048 + t0:2048 + t0 + tn], reads=[("oall", j % nh)], writes=[("t1B", k)], group="gbo")
        P.ts("dve", stg[:, 0:tn], stg[:, 0:tn], self.selv[:, 0:1], None, ALU.mult, reads=[("t1", k), ("selv",)], writes=[("t1", k)])
        P.stt("dve", self.gbuf[:, j * 512:j * 512 + tn], stg[:, 512:512 + tn], self.selv[:, 1:2], stg[:, 0:tn], ALU.mult, ALU.add,
              reads=[("t1", k), ("t1B", k), ("selv",)], writes=gk)

    def outproj(self, oT_dram, w_out, kc):
        P = self.P
        for j in range(kc):
            k = self.nxt() % 2
            P.dma("sp", self.w2s[k][:], w_out[j * 128:(j + 1) * 128, :], writes=[("w2s", k)], group=f"w2s{k}")
            P.copy("act", self.wbuf[:, j * 1024:(j + 1) * 1024], self.w2s[k][:], reads=[("w2s", k)],
                   writes=[("wbuf", 0, jj) for jj in range(6)] + [("wbuf", 1, jj) for jj in range(6)] + [("wbo", j)])
        for ti, (t0, tn) in enumerate(TCH):
            for j in range(kc):
                self.load_o(oT_dram, j, t0, tn)
            for c in range(DC):
                kk = self.nxt() % 2
                po = self.ps_o[kk]
                for j in range(kc):
                    P.mm(po[:, 0:tn], self.wbuf[:, j * 1024 + c * 128:j * 1024 + (c + 1) * 128], self.gbuf[:, j * 512:j * 512 + tn],
                         j == 0, j == kc - 1, reads=[("wbo", j), ("gbo", j)], writes=[("ps_o", kk)])
                P.stt("dve", self.h32[:, c, t0:t0 + tn], self.h32[:, c, t0:t0 + tn], float(ALPHA), po[:, 0:tn], ALU.mult, ALU.add,
                      reads=[("ps_o", kk), ("h32", c, ti)], writes=[("h32", c, ti)])
            P.op("pool", lambda e, a=self.sq[0][0:1, 0:1]: e.memset(a, 0.0),
                 reads=[("gbo", j) for j in range(kc)], writes=[("gbuf", jj, tt_) for jj in range(6) for tt_ in range(5)] + [("sq", 0)])
        P.op("pool", lambda e, a=self.sq[0][0:1, 0:1]: e.memset(a, 0.0),
             reads=[("wbo", j) for j in range(kc)], writes=[("wbuf", 0, jj) for jj in range(6)] + [("wbuf", 1, jj) for jj in range(6)] + [("sq", 0)])

    def glu(self, oT_dram, w_glu, b_glu):
        P = self.P
        P.dma("sp", self.bglu[:], b_glu.rearrange("(c p) -> p c", p=128), writes=[("bglu",)], group="const", slow=True)
        wv = self.wbuf[:].rearrange("p (k f) -> p k f", f=2048)
        for j in range(8):
            for hf in range(2):
                k = self.nxt() % 2
                P.dma("sp", self.w2s[k][:], w_glu[j * 128:(j + 1) * 128, hf * 1024:(hf + 1) * 1024], writes=[("w2s", k)], group=f"w2s{k}")
                P.copy("act", wv[:, j, hf * 1024:(hf + 1) * 1024], self.w2s[k][:], reads=[("w2s", k)],
                       writes=[("wbuf", 0, jj) for jj in range(6)] + [("wbuf", 1, jj) for jj in range(6)] + [("wbo", j, hf)])
        for ti, (t0, tn) in enumerate(TCH):
            for j in range(8):
                self.load_o(oT_dram, j, t0, tn)
            for c in range(DC):
                kk = self.nxt() % 2
                pa, pb, sg, t1 = self.ps_a[kk], self.ps_b[kk], self.sg[kk], self.t1[kk]
                for j in range(8):
                    P.mm(pa[:, 0:tn], wv[:, j, c * 128:(c + 1) * 128], self.gbuf[:, j * 512:j * 512 + tn], j == 0, j == 7,
                         reads=[("wbo", j, 0), ("gbo", j)], writes=[("ps_a", kk)])
                for j in range(8):
                    P.mm(pb[:, 0:tn], wv[:, j, 1024 + c * 128:1024 + (c + 1) * 128], self.gbuf[:, j * 512:j * 512 + tn], j == 0, j == 7,
                         reads=[("wbo", j, 1), ("gbo", j)], writes=[("ps_b", kk)])
                P.act(sg[:, 0:tn], pb[:, 0:tn], AF.Sigmoid, bias=self.bglu[:, 8 + c:9 + c], reads=[("ps_b", kk), ("bglu",)], writes=[("sg", kk)])
                P.stt("dve", t1[:, 0:tn], pa[:, 0:tn], self.bglu[:, c:c + 1], sg[:, 0:tn], ALU.add, ALU.mult,
                      reads=[("ps_a", kk), ("sg", kk), ("bglu",)], writes=[("t1", kk)])
                P.stt("dve", self.h32[:, c, t0:t0 + tn], self.h32[:, c, t0:t0 + tn], float(ALPHA), t1[:, 0:tn], ALU.mult, ALU.add,
                      reads=[("t1", kk), ("h32", c, ti)], writes=[("h32", c, ti)])
            P.op("pool", lambda e, a=self.sq[0][0:1, 0:1]: e.memset(a, 0.0),
                 reads=[("gbo", j) for j in range(8)], writes=[("gbuf", jj, tt_) for jj in range(6) for tt_ in range(5)] + [("sq", 0)])
        P.op("pool", lambda e, a=self.sq[0][0:1, 0:1]: e.memset(a, 0.0),
             reads=[("wbo", j, hf) for j in range(8) for hf in range(2)],
             writes=[("wbuf", 0, jj) for jj in range(6)] + [("wbuf", 1, jj) for jj in range(6)] + [("sq", 0)])

    def store_h(self, out32, outbf):
        P = self.P
        for c in range(DC):
            if out32 is not None:
                P.dma("sp", out32[c * 128:(c + 1) * 128, :], self.h32[:, c, :],
                      reads=[("h32", c, t) for t in range(5)], group="out")
            if outbf is not None:
                dst = outbf[c // 2][(c % 2) * 128:(c % 2) * 128 + 128, :] if isinstance(outbf, list) else outbf[c * 128:(c + 1) * 128, :]
                P.dma("sp", dst, self.hbf[:, c, :],
                      reads=[("hbf", c, t) for t in range(5)], writes=[("xin", c)], group="out")


def build_t0():
    nc = bass.Bass("TRN2", target_bir_lowering=False)
    hT = nc.dram_tensor("hT", [D, NT], F32, kind="ExternalInput").ap()
    w1 = nc.dram_tensor("w1", [D, FF], F32, kind="ExternalInput").ap()
    w3 = nc.dram_tensor("w3", [D, FF], F32, kind="ExternalInput").ap()
    w2 = nc.dram_tensor("w2", [FF, D], F32, kind="ExternalInput").ap()
    lg = nc.dram_tensor("ln_g", [1, D], F32, kind="ExternalInput").ap()
    lb = nc.dram_tensor("ln_b", [1, D], F32, kind="ExternalInput").ap()
    o32 = nc.dram_tensor("o32", [D, NT], F32, kind="ExternalOutput").ap()
    obf = nc.dram_tensor("obf", [D, NT], BF16, kind="ExternalOutput").ap()
    P = Prog(nc)
    T = TPhase(P, nc)
    T.load_ln(lg, lb, 1)
    T.load_h(hT)
    T.ffn(w1, w3, w2)
    T.layer_norm(0)
    T.store_h(o32, obf)
    P.emit()
    return nc


QCH = [(0, 16)] + [(16 + 512 * i, 512) for i in range(8)]
KBL = [(0, 16)] + [(16 + 128 * i, 128) for i in range(32)]
NEG = -30000.0


def load_cast_w(P, dst_bf, src_dram, stg, rows_chunks, cols, tagkey, eng="act", colblk=1024):
    k = 0
    for c in range(rows_chunks):
        for c0 in range(0, cols, colblk):
            cn = min(colblk, cols - c0)
            s = k % len(stg)
            k += 1
            P.dma("sp", stg[s][:, 0:cn], src_dram[c * 128:(c + 1) * 128, c0:c0 + cn],
                  writes=[("stg", id(stg[s]))], group=f"stg{id(stg[s])}")
            P.copy(eng, dst_bf[:, c, c0:c0 + cn], stg[s][:, 0:cn],
                   reads=[("stg", id(stg[s]))], writes=[tagkey])


class Ctx:
    def __init__(self, nc, P, pre, xall=None, oin=None, selv=None):
        self.nc, self.P, self.pre, self.xall, self.oin, self.selv = nc, P, pre, xall, oin, selv

    def x_rows(self, rank, c):
        k, w = c // 2, (c % 2) * 128
        return self.xall[k][rank * 256 + w:rank * 256 + w + 128, :]


class ORows:
    def __init__(self, oT, chunks=None):
        self.oT, self.chunks = oT, chunks

    def rows(self, r0, n):
        if self.chunks is None:
            return self.oT[r0:r0 + n, :]
        j, w = r0 // 128, r0 % 128
        assert w + n <= 128
        return self.chunks[j][w:w + n, :]


def _load_seq(P, ctx, dst, nch, hbT, keyfn):
    for c in range(nch):
        if ctx is None:
            P.dma("sp", dst[:, c, :], hbT[c * 128:(c + 1) * 128, :], writes=keyfn(c, 0) + keyfn(c, 1), group="hb")
        else:
            P.dma("sp", dst[:, c, 0:NT], ctx.x_rows(0, c), reads=[("xall", c // 2)], writes=keyfn(c, 0), group="hb")
            P.dma("sp", dst[:, c, NT:LSEQ], ctx.x_rows(1, c)[:, 16:NT], reads=[("xall", c // 2)], writes=keyfn(c, 1), group="hb")


class AttnCore:
    def __init__(self, P, nmaps, E, s_depth=2):
        self.P = P
        self.nmaps = nmaps
        self.E = E
        self.sd = s_depth
        self.psS = [[P.psum([128, 512], F32, f"psS{m}_{i}") for i in range(s_depth)] for m in range(nmaps)]
        self.po = [P.psum([128, 512], F32, f"po{m}") for m in range(nmaps)]
        self.pn = [P.psum([128, 512], F32, f"pn{m}") for m in range(nmaps)]
        self.PT = [[P.sbuf([128, 512], BF16, f"PT{m}_{i}") for i in range(4)] for m in range(nmaps)]
        self.tmp = [[P.sbuf([128, 256], F32, f"atmp{m}_{i}") for i in range(2)] for m in range(nmaps)]
        self.onesb = P.sbuf([128, 128], BF16, "onesb")
        P.memset("pool", self.onesb[:], 1.0, writes=[("onesb",)])
        self.kS = [0] * nmaps
        self.kP = [0] * nmaps
        self.kT = [0] * nmaps

    def chunk(self, c, kT, qT, kd_base, kd, V, vkeys, sc, tiles, far_bias, keysK, keysQ, keysV, pv_extra=None):
        P = self.P
        E = self.E
        qs, nq = QCH[c]
        if c == 0:
            kbs = [0]
        else:
            kbs = list(range(0, 4 * c + 1))
        qb0 = 4 * (c - 1) + 1
        work = []
        for kb in kbs:
            ks, nk = KBL[kb]
            last = (kb == kbs[-1])
            specials = []
            if c == 0:
                cs = 0
                specials.append((0, 16, tiles["md"][0:16, 0:16]))
                cf = 16
            elif kb == 0:
                cs = 0
                cf = 0
                if c == 1 and tiles.get("mq1") is not None:
                    specials.append((0, 128, tiles["mq1"][0:16, 0:128]))
                    cf = 128
            else:
                i0 = max(0, kb - qb0)
                cs = 128 * i0
                d0 = qb0 + i0 - kb
                cf = cs
                if d0 == 0:
                    w = 128
                    if tiles["prev"] and cs + 256 <= nq:
                        w = 256
                    specials.append((cs, w, tiles["dd"][:, 0:w]))
                    cf = cs + w
                elif d0 == 1 and tiles["prev"]:
                    specials.append((cs, 128, tiles["dd"][:, 128:256]))
                    cf = cs + 128
            for m in range(self.nmaps):
                bS = self.kS[m] % self.sd
                self.kS[m] += 1
                bP = self.kP[m] % 4
                self.kP[m] += 1
                ps = self.psS[m][bS]
                PT = self.PT[m][bP]
                pkey = ("psS", m, bS)
                tkey = ("PT", m, bP)

                def emit_S(kb=kb, m=m, ps=ps, pkey=pkey, nk=nk, cs=cs):
                    P.mm(ps[0:nk, cs:nq], kT(m, kb), qT(m)[:, cs:nq], True, True,
                         reads=keysK(m, kb) + keysQ(m), writes=[pkey])

                def emit_post(kb=kb, m=m, ps=ps, PT=PT, pkey=pkey, tkey=tkey, nk=nk, cs=cs, cf=cf, specials=specials, last=last):
                    for (c0, w, tap) in specials:
                        bT = self.kT[m] % 2
                        self.kT[m] += 1
                        t = self.tmp[m][bT]
                        P.stt("dve", t[0:nk, 0:w], ps[0:nk, c0:c0 + w], float(sc), tap, ALU.mult, ALU.add,
                              reads=[pkey, ("tiles",)], writes=[("atmp", m, bT)])
                        P.act(PT[0:nk, c0:c0 + w], t[0:nk, 0:w], AF.Exp, reads=[("atmp", m, bT)], writes=[tkey])
                    if cf < nq:
                        fb = far_bias[m] if isinstance(far_bias, (list, tuple)) else far_bias
                        if fb is not None:
                            P.act(PT[0:nk, cf:nq], ps[0:nk, cf:nq], AF.Exp, bias=fb[0:nk, :], scale=float(sc),
                                  reads=[pkey, ("tiles",)], writes=[tkey])
                        else:
                            P.act(PT[0:nk, cf:nq], ps[0:nk, cf:nq], AF.Exp, scale=float(sc), reads=[pkey], writes=[tkey])
                    P.mm(self.po[m][0:E, cs:nq], V(kb), PT[0:nk, cs:nq], kb == 0, last,
                         reads=[tkey] + keysV(kb), writes=[("po", m)])
                    P.mm(self.pn[m][0:E, cs:nq], self.onesb[0:nk, 0:E], PT[0:nk, cs:nq], kb == 0, last,
                         reads=[tkey, ("onesb",)], writes=[("pn", m)])
                work.append((emit_S, emit_post))
        LA = self.nmaps * (self.sd - 1)
        for i in range(len(work) + LA):
            if i < len(work):
                work[i][0]()
            if i - LA >= 0:
                work[i - LA][1]()


def build_mla(ctx=None):
    nc = bass.Bass("TRN2", target_bir_lowering=False) if ctx is None else ctx.nc
    pre = "" if ctx is None else ctx.pre
    dt = lambda n, s, d=F32: nc.dram_tensor(pre + n, s, d, kind="ExternalInput").ap()
    hbT = dt("hbT", [D, LSEQ], BF16) if ctx is None else None
    w_in = dt("w_in_ext", [D, 832])
    w_uq = dt("w_uq", [384, 768])
    w_uqs = dt("w_uq_sw", [384, 768])
    w_uk = dt("w_uk", [256, 512])
    w_uv = dt("w_uv", [256, 512])
    qg = dt("qg", [384])
    kvg = dt("kvg", [256])
    ropeC = dt("ropeC", [32, LSEQ])
    ropeS = dt("ropeS", [32, LSEQ])
    mask_dd = dt("mask_dd", [128, 128])
    oT = ORows(nc.dram_tensor("oT", [512, LSEQ], BF16, kind="ExternalOutput").ap()) if ctx is None else ORows(None, ctx.oin)
    P = Prog(nc) if ctx is None else ctx.P
    big = P.sbuf([128, 8, LSEQ], BF16, "big")
    cqn = P.sbuf([128, 3, LSEQ], BF16, "cqn")
    ckvn = P.sbuf([128, 2, LSEQ], BF16, "ckvn")
    Vt = P.sbuf([128, 33, 512], BF16, "Vt")
    winb = P.sbuf([128, 8, 832], BF16, "winb")
    wuqb = P.sbuf([128, 3, 768], BF16, "wuqb")
    wuqsb = P.sbuf([128, 3, 768], BF16, "wuqsb")
    wukb = P.sbuf([128, 2, 512], BF16, "wukb")
    wuvb = P.sbuf([128, 2, 512], BF16, "wuvb")
    stg = [P.sbuf([128, 1024], F32, f"stg{i}") for i in range(1)]
    c32 = [P.sbuf([128, 512], F32, f"c32_{i}") for i in range(5)]
    rs = [P.sbuf([128, 512], F32, f"rs{i}") for i in range(1)]
    tC = P.sbuf([128, 512], F32, "tC")
    tS = P.sbuf([128, 512], F32, "tS")
    r1 = P.sbuf([128, 512], F32, "r1")
    r2 = P.sbuf([128, 512], F32, "r2")
    qTt = [P.sbuf([128, 512], BF16, f"qT{i}") for i in range(2)]
    osb = [P.sbuf([128, 512], BF16, f"osb{i}") for i in range(2)]
    rcp = P.sbuf([128, 512], F32, "rcp")
    onesq = P.sbuf([128, 128], F32, "onesq")
    oneskv = P.sbuf([128, 128], F32, "oneskv")
    gq = P.sbuf([128, 3], F32, "gq")
    gkv = P.sbuf([128, 2], F32, "gkv")
    mdd = P.sbuf([128, 128], F32, "mdd")
    A = AttnCore(P, 1, 64, s_depth=3)
    psx = [P.psum([128, 512], F32, f"psx{i}") for i in range(2)]
    pss = P.psum([128, 512], F32, "pss")
    P.memset("pool", onesq[:], 1.0 / 384, writes=[("onesq",)])
    P.memset("pool", oneskv[:], 1.0 / 256, writes=[("oneskv",)])
    P.dma("sp", gq[:], qg.rearrange("(c p) -> p c", p=128), writes=[("gq",)], group="const", slow=True)
    P.dma("sp", gkv[:], kvg.rearrange("(c p) -> p c", p=128), writes=[("gkv",)], group="const", slow=True)
    P.dma("sp", mdd[:], mask_dd, writes=[("tiles",)], group="const")
    _load_seq(P, ctx, big, 8, hbT, lambda c, part: [("big", c, t) for t in (range(0, 5) if part == 0 else range(5, 9))])
    load_cast_w(P, winb, w_in, stg, 8, 832, ("winb",))
    load_cast_w(P, wuqb, w_uq, stg, 3, 768, ("wuqb",))
    load_cast_w(P, wuqsb, w_uqs, stg, 3, 768, ("wuqsb",))
    load_cast_w(P, wukb, w_uk, stg, 2, 512, ("wukb",))
    load_cast_w(P, wuvb, w_uv, stg, 2, 512, ("wuvb",))
    cnt = [0]

    def nxt():
        cnt[0] += 1
        return cnt[0] - 1

    pxc = [0]

    def nxp():
        pxc[0] += 1
        return pxc[0] - 1

    for ti, (t0, tn) in enumerate(QCH):
        for grp, (f0, nf, ones_t, okey, gt, dst) in enumerate([(0, 3, onesq, ("onesq",), gq, cqn), (3, 2, oneskv, ("oneskv",), gkv, ckvn)]):
            for f in range(nf):
                kx = nxp() % 2
                ps = psx[kx]
                for c in range(8):
                    P.mm(ps[:, 0:tn], winb[:, c, (f0 + f) * 128:(f0 + f + 1) * 128], big[:, c, t0:t0 + tn], c == 0, c == 7,
                         reads=[("winb",), ("big", c, ti)], writes=[("psx", kx)])
                cc = c32[f0 + f]
                P.copy("act", cc[:, 0:tn], ps[:, 0:tn], reads=[("psx", kx)], writes=[("c32", f0 + f)])
                ks = nxt() % 2
                sqt = [r1, r2][ks]
                sqk = [("r1",), ("r2",)][ks]
                P.tt("dve", sqt[:, 0:tn], cc[:, 0:tn], cc[:, 0:tn], ALU.mult, reads=[("c32", f0 + f)], writes=[sqk])
                P.mm(pss[:, 0:tn], ones_t[:], sqt[:, 0:tn], f == 0, f == nf - 1,
                     reads=[sqk, okey], writes=[("pss",)])
            kr = 0
            P.ts("dve", rs[kr][:, 0:tn], pss[:, 0:tn], RMS_EPS, None, ALU.add, reads=[("pss",)], writes=[("rs", kr)])
            P.act(rs[kr][:, 0:tn], rs[kr][:, 0:tn], AF.Sqrt, reads=[("rs", kr)], writes=[("rs", kr)])
            P.op("dve", lambda e, o=rs[kr][:, 0:tn]: e.reciprocal(o, o), reads=[("rs", kr)], writes=[("rs", kr)])
            for f in range(nf):
                cc = c32[f0 + f]
                P.tt("dve", cc[:, 0:tn], cc[:, 0:tn], rs[kr][:, 0:tn], ALU.mult, reads=[("c32", f0 + f), ("rs", kr)], writes=[("c32", f0 + f)])
                P.act(dst[:, f, t0:t0 + tn], cc[:, 0:tn], AF.Identity, scale=gt[:, f:f + 1],
                      reads=[("c32", f0 + f), ("gq",), ("gkv",)], writes=[(id(dst), f, ti)])
        P.dma("sp", tC[64:96, 0:tn], ropeC[:, t0:t0 + tn], writes=[("tC",)], group="tC")
        P.dma("sp", tS[64:96, 0:tn], ropeS[:, t0:t0 + tn], writes=[("tS",)], group="tS")
        ka, kb_ = nxp() % 2, None
        psa = psx[ka]
        for c in range(8):
            P.mm(psa[0:96, 0:tn], winb[:, c, 640:736], big[:, c, t0:t0 + tn], c == 0, c == 7,
                 reads=[("winb",), ("big", c, ti)], writes=[("psx", ka)])
        P.tt("dve", r1[64:96, 0:tn], psa[64:96, 0:tn], tC[64:96, 0:tn], ALU.mult, reads=[("psx", ka), ("tC",)], writes=[("r1",)])
        kb_ = nxp() % 2
        psb = psx[kb_]
        for c in range(8):
            P.mm(psb[0:96, 0:tn], winb[:, c, 736:832], big[:, c, t0:t0 + tn], c == 0, c == 7,
                 reads=[("winb",), ("big", c, ti)], writes=[("psx", kb_)])
        P.tt("dve", r2[64:96, 0:tn], psb[64:96, 0:tn], tS[64:96, 0:tn], ALU.mult, reads=[("psx", kb_), ("tS",)], writes=[("r2",)])
        P.tt("dve", r1[64:96, 0:tn], r1[64:96, 0:tn], r2[64:96, 0:tn], ALU.add, reads=[("r1",), ("r2",)], writes=[("r1",)])
        for h in range(8):
            P.copy("pool" if h % 2 else "act", big[64:96, h, t0:t0 + tn], r1[64:96, 0:tn],
                   reads=[("r1",)] + [("big", cc_, ti) for cc_ in range(8)], writes=[("bigr", h, ti)])
    for h in range(8):
        for ti, (t0, tn) in enumerate(QCH):
            kx = nxp() % 2
            ps = psx[kx]
            for kc in range(2):
                P.mm(ps[0:64, 0:tn], wukb[:, kc, h * 64:(h + 1) * 64], ckvn[:, kc, t0:t0 + tn], kc == 0, kc == 1,
                     reads=[("wukb",), (id(ckvn), kc, ti)], writes=[("psx", kx)])
            P.copy("act" if h % 2 else "dve", big[0:64, h, t0:t0 + tn], ps[0:64, 0:tn],
                   reads=[("psx", kx)] + [("big", cc_, ti) for cc_ in range(8)], writes=[("bign", h, ti)])
    for kb, (ks, nk) in enumerate(KBL):
        kx = nxp() % 2
        ps = psx[kx]
        ti = 0 if kb == 0 else 1 + (kb - 1) // 4
        for kc in range(2):
            P.mm(ps[0:nk, 0:512], ckvn[:, kc, ks:ks + nk], wuvb[:, kc, :], kc == 0, kc == 1,
                 reads=[("wuvb",), (id(ckvn), kc, ti)], writes=[("psx", kx)])
        P.copy("act" if kb % 2 else "dve", Vt[0:nk, kb, :], ps[0:nk, 0:512], reads=[("psx", kx)], writes=[("Vt", kb)])
    sc = (64 + 32) ** -0.5
    tiles = dict(dd=mdd, prev=False, md=mdd, mq1=None)
    for c, (qs, nq) in enumerate(QCH):
        P.dma("sp", tC[64:96, 0:nq], ropeC[:, qs:qs + nq], writes=[("tC",)], group="tC")
        P.dma("sp", tS[64:96, 0:nq], ropeS[:, qs:qs + nq], writes=[("tS",)], group="tS")
        for h in range(8):
            ka = nxp() % 2
            psa = psx[ka]
            for kc in range(3):
                P.mm(psa[0:96, 0:nq], wuqb[:, kc, h * 96:(h + 1) * 96], cqn[:, kc, qs:qs + nq], kc == 0, kc == 2,
                     reads=[("wuqb",), (id(cqn), kc, c)], writes=[("psx", ka)])
            kb_ = nxp() % 2
            psb = psx[kb_]
            for kc in range(3):
                P.mm(psb[0:96, 0:nq], wuqsb[:, kc, h * 96:(h + 1) * 96], cqn[:, kc, qs:qs + nq], kc == 0, kc == 2,
                     reads=[("wuqsb",), (id(cqn), kc, c)], writes=[("psx", kb_)])
            qi = nxt() % 2
            qt = qTt[qi]
            P.copy("act", qt[0:64, 0:nq], psa[0:64, 0:nq], reads=[("psx", ka)], writes=[("qT", qi, 0)])
            P.tt("dve", r1[64:96, 0:nq], psa[64:96, 0:nq], tC[64:96, 0:nq], ALU.mult, reads=[("psx", ka), ("tC",)], writes=[("r1",)])
            P.tt("dve", r2[64:96, 0:nq], psb[64:96, 0:nq], tS[64:96, 0:nq], ALU.mult, reads=[("psx", kb_), ("tS",)], writes=[("r2",)])
            P.tt("dve", qt[64:96, 0:nq], r1[64:96, 0:nq], r2[64:96, 0:nq], ALU.add, reads=[("r1",), ("r2",)], writes=[("qT", qi, 1)])
            A.chunk(c,
                    kT=lambda m, kb, h=h: big[0:96, h, KBL[kb][0]:KBL[kb][0] + KBL[kb][1]],
                    qT=lambda m, qt=qt, nq=nq: qt[0:96, 0:nq],
                    kd_base=0, kd=96,
                    V=lambda kb, h=h: Vt[0:KBL[kb][1], kb, h * 64:(h + 1) * 64], vkeys=None,
                    sc=sc, tiles=tiles, far_bias=None,
                    keysK=lambda m, kb, h=h: [("bign", h, 0 if kb == 0 else 1 + (kb - 1) // 4), ("bigr", h, 0 if kb == 0 else 1 + (kb - 1) // 4)],
                    keysQ=lambda m, qi=qi: [("qT", qi, 0), ("qT", qi, 1)],
                    keysV=lambda kb: [("Vt", kb)])
            P.op("dve", lambda e, o=rcp[0:64, 0:nq], i=A.pn[0][0:64, 0:nq]: e.reciprocal(o, i), reads=[("pn", 0)], writes=[("rcp",)])
            oi = nxt() % 2
            P.tt("dve", osb[oi][0:64, 0:nq], A.po[0][0:64, 0:nq], rcp[0:64, 0:nq], ALU.mult,
                 reads=[("po", 0), ("rcp",)], writes=[("osb", oi)])
            P.dma("sp", oT.rows(h * 64, 64)[:, qs:qs + nq], osb[oi][0:64, 0:nq], reads=[("osb", oi)], writes=[("oin", h // 2, h, c)], group=f"oout{oi}")
    if ctx is None:
        P.emit()
        return nc


def build_diff(lam_init, ctx=None):
    nc = bass.Bass("TRN2", target_bir_lowering=False) if ctx is None else ctx.nc
    pre = "" if ctx is None else ctx.pre
    dt = lambda n, s, d=F32: nc.dram_tensor(pre + n, s, d, kind="ExternalInput").ap()
    hbT = dt("hbT", [D, LSEQ], BF16) if ctx is None else None
    w_q = dt("w_q", [D, 512]); w_k = dt("w_k", [D, 512]); w_v = dt("w_v", [D, 512])
    lam4 = dt("lam4", [4, 64]); subg = dt("subg", [128]); rb = dt("rb", [32 * 4])
    m_dd = dt("m_dd", [32, 128, 256]); m_md = dt("m_md", [32, 16, 16]); m_mq1 = dt("m_mq1", [32, 16, 128])
    n_dd = dt("n_dd", [128, 256]); n_md = dt("n_md", [16, 16])
    oT = ORows(nc.dram_tensor("oT", [512, LSEQ], BF16, kind="ExternalOutput").ap()) if ctx is None else ORows(None, ctx.oin)
    P = Prog(nc) if ctx is None else ctx.P
    big = P.sbuf([128, 8, LSEQ], BF16, "big")
    qTt = P.sbuf([128, 4, LSEQ], BF16, "qTt")
    kTt = P.sbuf([128, 4, LSEQ], BF16, "kTt")
    Vt = P.sbuf([128, 33, 512], BF16, "Vt")
    wb1 = P.sbuf([128, 8, 512], BF16, "wb1")
    wb = [wb1, wb1, wb1]
    stg = [P.sbuf([128, 512], F32, "stg0")]
    Tdd = [P.sbuf([128, 256], F32, f"Tdd{h}") for h in range(4)]
    Tmd = [P.sbuf([16, 16], F32, f"Tmd{h}") for h in range(4)]
    Tmq = [P.sbuf([16, 128], F32, f"Tmq{h}") for h in range(4)]
    mk = P.sbuf([128, 256], F32, "mk")
    mk2 = P.sbuf([16, 16], F32, "mk2")
    mk3 = P.sbuf([16, 128], F32, "mk3")
    rbb = P.sbuf([128, 128], F32, "rbb")
    lamb = P.sbuf([128, 4, 64], F32, "lamb")
    lt = P.sbuf([128, 64], F32, "lt")
    l1 = P.sbuf([128, 1], F32, "l1"); l2 = P.sbuf([128, 1], F32, "l2"); nlam = P.sbuf([128, 1], F32, "nlam")
    gs = P.sbuf([128, 1], F32, "gs")
    ones32 = P.sbuf([128, 128], F32, "ones32")
    u = P.sbuf([128, 512], F32, "u"); t_ = P.sbuf([128, 512], F32, "t_"); rc = P.sbuf([128, 512], F32, "rc")
    osb = [P.sbuf([128, 512], BF16, f"osb{i}") for i in range(2)]
    A = AttnCore(P, 2, 128)
    P.memset("pool", ones32[:], 1.0 / 128, writes=[("ones32",)])
    P.dma("sp", rbb[:], rb.partition_broadcast(128), writes=[("rbb",)], group="const")
    P.dma("sp", lamb[:], lam4.rearrange("a b -> (a b)").partition_broadcast(128), writes=[("lamb",)], group="const")
    P.dma("sp", gs[:], subg.rearrange("(p o) -> p o", o=1), writes=[("gs",)], group="const", slow=True)
    P.ts("dve", gs[:], gs[:], float(1.0 - lam_init), None, ALU.mult, reads=[("gs",)], writes=[("gs",)])
    for i, dst in enumerate([l1, l2]):
        P.tt("dve", lt[:], lamb[:, 2 * i, :], lamb[:, 2 * i + 1, :], ALU.mult, reads=[("lamb",)], writes=[("lt",)])
        P.op("dve", lambda e, o=dst[:], i_=lt[:]: e.reduce_sum(o, i_, AX.X), reads=[("lt",)], writes=[("l", i)])
        P.act(dst[:], dst[:], AF.Exp, reads=[("l", i)], writes=[("l", i)])
    P.tt("dve", nlam[:], l2[:], l1[:], ALU.subtract, reads=[("l", 0), ("l", 1)], writes=[("nlam",)])
    P.ts("dve", nlam[:], nlam[:], float(-lam_init), None, ALU.add, reads=[("nlam",)], writes=[("tiles",)])
    for h in range(4):
        P.dma("sp", Tdd[h][:], n_dd, writes=[("Tdd", h)], group="const")
        P.dma("sp", Tmd[h][:], n_md, writes=[("Tmd", h)], group="const")
        P.memset("pool", Tmq[h][:], 0.0, writes=[("Tmq", h)])
    for b in range(32):
        P.dma("sp", mk[:], m_dd[b], writes=[("mk",)], group="mk")
        P.dma("sp", mk2[:], m_md[b], writes=[("mk2",)], group="mk2")
        P.dma("sp", mk3[:], m_mq1[b], writes=[("mk3",)], group="mk3")
        for h in range(4):
            col = b * 4 + h
            P.stt("dve", Tdd[h][:], mk[:], rbb[:, col:col + 1], Tdd[h][:], ALU.mult, ALU.add,
                  reads=[("mk",), ("rbb",), ("Tdd", h)], writes=[("Tdd", h)])
            P.stt("dve", Tmd[h][:], mk2[:], rbb[0:16, col:col + 1], Tmd[h][:], ALU.mult, ALU.add,
                  reads=[("mk2",), ("rbb",), ("Tmd", h)], writes=[("Tmd", h)])
            P.stt("dve", Tmq[h][:], mk3[:], rbb[0:16, col:col + 1], Tmq[h][:], ALU.mult, ALU.add,
                  reads=[("mk3",), ("rbb",), ("Tmq", h)], writes=[("Tmq", h)])
    for h in range(4):
        P.copy("pool", Tmq[h][:], Tmq[h][:], reads=[("Tdd", h), ("Tmd", h), ("Tmq", h)], writes=[("tiles",), ("Tmq", h)])
    _load_seq(P, ctx, big, 8, hbT, lambda c, part: [("big", c, t) for t in (range(0, 5) if part == 0 else range(5, 9))])
    cnt = [0]

    def nxt():
        cnt[0] += 1
        return cnt[0] - 1

    for wi, dstT in [(0, qTt), (1, kTt)]:
        load_cast_w(P, wb1, [w_q, w_k][wi], stg, 8, 512, ("wb",), colblk=512)
        for h in range(4):
            for ti, (t0, tn) in enumerate(QCH):
                kk = nxt()
                m_, i_ = kk % 2, (kk // 2) % 2
                ps = A.psS[m_][i_]
                for c in range(8):
                    P.mm(ps[:, 0:tn], wb[wi][:, c, h * 128:(h + 1) * 128], big[:, c, t0:t0 + tn], c == 0, c == 7,
                         reads=[("wb",), ("big", c, ti)], writes=[("psS", m_, i_)])
                P.copy("act" if kk % 2 else "dve", dstT[:, h, t0:t0 + tn], ps[:, 0:tn],
                       reads=[("psS", m_, i_)], writes=[(id(dstT), h, ti)])
    load_cast_w(P, wb1, w_v, stg, 8, 512, ("wb",), colblk=512)
    for kb, (ks, nk) in enumerate(KBL):
        kk = nxt()
        m_, i_ = kk % 2, (kk // 2) % 2
        ps = A.psS[m_][i_]
        ti = 0 if kb == 0 else 1 + (kb - 1) // 4
        for c in range(8):
            P.mm(ps[0:nk, 0:512], big[:, c, ks:ks + nk], wb[2][:, c, :], c == 0, c == 7,
                 reads=[("wb",), ("big", c, ti)], writes=[("psS", m_, i_)])
        P.copy("act" if kb % 2 else "dve", Vt[0:nk, kb, :], ps[0:nk, 0:512], reads=[("psS", m_, i_)], writes=[("Vt", kb)])
    sc = 64 ** -0.5
    for c, (qs, nq) in enumerate(QCH):
        for h in range(4):
            tiles = dict(dd=Tdd[h], prev=True, md=Tmd[h], mq1=Tmq[h])
            fb = rbb[:, 31 * 4 + h:31 * 4 + h + 1]
            A.chunk(c,
                    kT=lambda m, kb, h=h: kTt[64 * m:64 * m + 64, h, KBL[kb][0]:KBL[kb][0] + KBL[kb][1]],
                    qT=lambda m, h=h, qs=qs, nq=nq: qTt[64 * m:64 * m + 64, h, qs:qs + nq],
                    kd_base=0, kd=64,
                    V=lambda kb, h=h: Vt[0:KBL[kb][1], kb, h * 128:(h + 1) * 128], vkeys=None,
                    sc=sc, tiles=tiles, far_bias=fb,
                    keysK=lambda m, kb, h=h: [(id(kTt), h, 0 if kb == 0 else 1 + (kb - 1) // 4)],
                    keysQ=lambda m, h=h, c=c: [(id(qTt), h, c)],
                    keysV=lambda kb: [("Vt", kb)])
            P.op("dve", lambda e, o=rc[:, 0:nq], i=A.pn[0][:, 0:nq]: e.reciprocal(o, i), reads=[("pn", 0)], writes=[("rc",)])
            P.tt("dve", u[:, 0:nq], A.po[0][:, 0:nq], rc[:, 0:nq], ALU.mult, reads=[("po", 0), ("rc",)], writes=[("u",)])
            P.op("dve", lambda e, o=rc[:, 0:nq], i=A.pn[1][:, 0:nq]: e.reciprocal(o, i), reads=[("pn", 1), ("rc",)], writes=[("rc",)])
            P.tt("dve", t_[:, 0:nq], A.po[1][:, 0:nq], rc[:, 0:nq], ALU.mult, reads=[("po", 1), ("rc",)], writes=[("t_",)])
            P.stt("dve", u[:, 0:nq], t_[:, 0:nq], nlam[:, 0:1], u[:, 0:nq], ALU.mult, ALU.add,
                  reads=[("t_",), ("u",), ("tiles",)], writes=[("u",)])
            P.act(t_[:, 0:nq], u[:, 0:nq], AF.Square, reads=[("u",)], writes=[("t_",)])
            pst = A.psS[0][0]
            P.mm(pst[:, 0:nq], ones32[:], t_[:, 0:nq], True, True, reads=[("t_",), ("ones32",)], writes=[("psS", 0, 0)])
            P.ts("dve", rc[:, 0:nq], pst[:, 0:nq], RMS_EPS, None, ALU.add, reads=[("psS", 0, 0)], writes=[("rc",)])
            P.act(rc[:, 0:nq], rc[:, 0:nq], AF.Sqrt, reads=[("rc",)], writes=[("rc",)])
            P.op("dve", lambda e, o=rc[:, 0:nq]: e.reciprocal(o, o), reads=[("rc",)], writes=[("rc",)])
            P.tt("dve", u[:, 0:nq], u[:, 0:nq], rc[:, 0:nq], ALU.mult, reads=[("u",), ("rc",)], writes=[("u",)])
            oi = nxt() % 2
            P.act(osb[oi][:, 0:nq], u[:, 0:nq], AF.Identity, scale=gs[:, 0:1], reads=[("u",), ("gs",)], writes=[("osb", oi)])
            P.dma("sp", oT.rows(h * 128, 128)[:, qs:qs + nq], osb[oi][:, 0:nq], reads=[("osb", oi)], writes=[("oin", h, h, c)], group=f"oout{oi}")
    if ctx is None:
        P.emit()
        return nc


def build_s5(ctx=None):
    TWO_PI = 2.0 * math.pi
    nc = bass.Bass("TRN2", target_bir_lowering=False) if ctx is None else ctx.nc
    pre = "" if ctx is None else ctx.pre
    dt = lambda n, s, d=F32: nc.dram_tensor(pre + n, s, d, kind="ExternalInput").ap()
    hbh = dt("hbh", [512, LSEQ], BF16) if ctx is None else None
    pl3 = dt("pl3", [3, 128, 16]); bc3 = dt("bc3", [3, 2048])
    bpad = dt("bpad", [2, 128, 2048]); cpad = dt("cpad", [2, 128, 2048]); dpad = dt("dpad", [128, 512]); iota = dt("iota", [129])
    oT = ORows(nc.dram_tensor("oT", [512, LSEQ], BF16, kind="ExternalOutput").ap()) if ctx is None else ORows(None, ctx.oin)
    P = Prog(nc) if ctx is None else ctx.P
    hb = P.sbuf([128, 4, LSEQ], BF16, "hb")
    W = 2048
    lrB = P.sbuf([128, W], F32, "lrB"); liB = P.sbuf([128, W], F32, "liB"); stB = P.sbuf([128, W], F32, "stB")
    magB = P.sbuf([128, W], F32, "magB"); thB = P.sbuf([128, W], F32, "thB"); cB = P.sbuf([128, W], F32, "cB"); sB = P.sbuf([128, W], F32, "sB")
    crB = P.sbuf([128, W], F32, "crB"); ciB = P.sbuf([128, W], F32, "ciB"); tB = P.sbuf([128, W], F32, "tB")
    bre = P.sbuf([128, W], F32, "bre"); bim = P.sbuf([128, W], F32, "bim")
    BBr = P.sbuf([128, W], BF16, "BBr"); BBi = P.sbuf([128, W], BF16, "BBi")
    Cr = P.sbuf([128, W], BF16, "Cr"); Ci = P.sbuf([128, W], BF16, "Ci"); Dd = P.sbuf([128, 512], BF16, "Dd")
    lrP = P.sbuf([128, 16], F32, "lrP"); liP = P.sbuf([128, 16], F32, "liP"); stP = P.sbuf([128, 16], F32, "stP")
    magP = P.sbuf([128, 16], F32, "magP"); thP = P.sbuf([128, 16], F32, "thP")
    io = P.sbuf([128, 129], F32, "io")
    ang = P.sbuf([128, 129], F32, "ang")
    cosT = P.sbuf([128, 16, 129], F32, "cosT"); sinT = P.sbuf([128, 16, 129], F32, "sinT"); nsinT = P.sbuf([128, 16, 129], F32, "nsinT")
    magT = P.sbuf([128, 16, 128], F32, "magT")
    car = P.sbuf([128, 16], F32, "car"); cai = P.sbuf([128, 16], F32, "cai")
    tmp1 = P.sbuf([128, 4], F32, "tmp1")
    T1 = [P.sbuf([128, 128], F32, f"T1_{i}") for i in range(2)]; T2 = [P.sbuf([128, 128], F32, f"T2_{i}") for i in range(2)]
    T3 = [P.sbuf([128, 128], F32, f"T3_{i}") for i in range(2)]; T4 = [P.sbuf([128, 128], F32, f"T4_{i}") for i in range(2)]
    xr = [P.sbuf([128, 128], F32, f"xr{i}") for i in range(2)]; xi = [P.sbuf([128, 128], F32, f"xi{i}") for i in range(2)]
    wr = [P.sbuf([128, 128], F32, f"wr{i}") for i in range(2)]; wi = [P.sbuf([128, 128], F32, f"wi{i}") for i in range(2)]
    sr = [P.sbuf([128, 128], BF16, f"sr{i}") for i in range(2)]; si = [P.sbuf([128, 128], BF16, f"si{i}") for i in range(2)]
    og = [P.sbuf([128, 128], BF16, f"og{i}") for i in range(2)]
    psr = [P.psum([128, 128], F32, f"psr{i}") for i in range(2)]; psi = [P.psum([128, 128], F32, f"psi{i}") for i in range(2)]
    psy = [P.psum([128, 128], F32, f"psy{i}") for i in range(2)]

    def load_bc(dst, row, key):
        P.dma("sp", dst[:], bc3[row].partition_broadcast(128), writes=[key], group="const")

    load_bc(lrB, 0, ("lrB",)); load_bc(liB, 1, ("liB",)); load_bc(stB, 2, ("stB",))
    P.dma("sp", lrP[:], pl3[0], writes=[("lrP",)], group="const")
    P.dma("sp", liP[:], pl3[1], writes=[("liP",)], group="const")
    P.dma("sp", stP[:], pl3[2], writes=[("stP",)], group="const")
    P.dma("sp", io[:], iota.partition_broadcast(128), writes=[("io",)], group="const")
    if ctx is None:
        for c in range(4):
            P.dma("sp", hb[:, c, :], hbh[c * 128:(c + 1) * 128, :], writes=[("hb", c)], group="hb")
    else:
        hstA = P.sbuf([128, NT], BF16, "hstA"); hstB = P.sbuf([128, NT], BF16, "hstB")
        for c in range(4):
            for part, (rk, c0, d0, n_) in enumerate([(0, 0, 0, NT), (1, 16, NT, NT - 16)]):
                P.dma("sp", hstA[:, 0:n_], ctx.x_rows(rk, c)[:, c0:c0 + n_], writes=[("hstA",)], group="hb")
                P.dma("sp", hstB[:, 0:n_], ctx.x_rows(rk, c + 4)[:, c0:c0 + n_], writes=[("hstB",)], group="hb")
                P.ts("dve", hstA[:, 0:n_], hstA[:, 0:n_], ctx.selv[:, 0:1], None, ALU.mult, reads=[("hstA",), ("selv",)], writes=[("hstA",)])
                P.stt("dve", hb[:, c, d0:d0 + n_], hstB[:, 0:n_], ctx.selv[:, 1:2], hstA[:, 0:n_], ALU.mult, ALU.add,
                      reads=[("hstA",), ("hstB",), ("selv",)], writes=[("hb", c)])

    I32 = mybir.dt.int32
    ki_s = P.sbuf([128, 129], I32, "ki_s")
    km_s = P.sbuf([128, 129], F32, "km_s")
    pP = P.sbuf([128, 16], F32, "pP")

    def sin_of(th, off, prescale, dst, tmp, kt, kd, ktmp, small):
        if small:
            ki, km, kk = ki_s[:], km_s[:], ("sc_tmp",)
        else:
            ki, km, kk = bre[:].bitcast(I32), bim[:], ("bre",)
        P.ts("dve", tmp, th, float(prescale), float(off), ALU.mult, ALU.add, reads=[kt, ktmp], writes=[ktmp])
        P.ts("dve", km, tmp, 1.0 / TWO_PI, None, ALU.mult, reads=[ktmp, kk], writes=[kk])
        P.copy("dve", ki, km, reads=[kk], writes=[kk])
        P.copy("dve", km, ki, reads=[kk], writes=[kk])
        P.stt("dve", tmp, km, -TWO_PI, tmp, ALU.mult, ALU.add, reads=[kk, ktmp], writes=[ktmp])
        P.ts("dve", km, tmp, math.pi, None, ALU.is_gt, reads=[ktmp, kk], writes=[kk])
        P.stt("dve", tmp, km, -TWO_PI, tmp, ALU.mult, ALU.add, reads=[kk, ktmp], writes=[ktmp])
        P.ts("dve", km, tmp, -math.pi, None, ALU.is_lt, reads=[ktmp, kk], writes=[kk])
        P.stt("dve", tmp, km, TWO_PI, tmp, ALU.mult, ALU.add, reads=[kk, ktmp], writes=[ktmp])
        P.act(dst, tmp, AF.Sin, reads=[ktmp], writes=[kd])

    def expm1_horner(x, p, kx, kp):
        P.ts("dve", p, x, 1.0 / 8, 1.0, ALU.mult, ALU.add, reads=[kx, kp], writes=[kp])
        for kdiv in (7, 6, 5, 4, 3, 2):
            P.tt("dve", p, p, x, ALU.mult, reads=[kx, kp], writes=[kp])
            P.ts("dve", p, p, 1.0 / kdiv, 1.0, ALU.mult, ALU.add, reads=[kp], writes=[kp])
        P.tt("dve", x, p, x, ALU.mult, reads=[kx, kp], writes=[kx])

    P.ts("dve", lrP[:], lrP[:], -1e-4, None, ALU.min, reads=[("lrP",)], writes=[("lrP",)])
    P.act(stP[:], stP[:], AF.Exp, reads=[("stP",)], writes=[("stP",)])
    P.tt("dve", magP[:], lrP[:], stP[:], ALU.mult, reads=[("lrP",), ("stP",)], writes=[("magP",)])
    expm1_horner(magP[:], pP[:], ("magP",), ("pP",))
    P.ts("dve", magP[:], magP[:], 1.0, None, ALU.add, reads=[("magP",)], writes=[("magP",)])
    P.tt("dve", thP[:], liP[:], stP[:], ALU.mult, reads=[("liP",), ("stP",)], writes=[("thP",)])
    P.ts("dve", lrB[:], lrB[:], -1e-4, None, ALU.min, reads=[("lrB",)], writes=[("lrB",)])
    P.act(stB[:], stB[:], AF.Exp, reads=[("stB",)], writes=[("stB",)])
    P.tt("dve", magB[:], lrB[:], stB[:], ALU.mult, reads=[("lrB",), ("stB",)], writes=[("magB",)])
    expm1_horner(magB[:], crB[:], ("magB",), ("crB",))
    P.tt("dve", thB[:], liB[:], stB[:], ALU.mult, reads=[("liB",), ("stB",)], writes=[("thB",)])
    sin_of(thB[:], 0.0, 1.0, sB[:], tB[:], ("thB",), ("sB",), ("tB",), False)
    sin_of(thB[:], 0.5 * math.pi, 1.0, cB[:], tB[:], ("thB",), ("cB",), ("tB",), False)
    sin_of(thB[:], 0.0, 0.5, crB[:], tB[:], ("thB",), ("crB",), ("tB",), False)
    P.tt("dve", crB[:], crB[:], crB[:], ALU.mult, reads=[("crB",)], writes=[("crB",)])
    P.ts("dve", crB[:], crB[:], -2.0, None, ALU.mult, reads=[("crB",)], writes=[("crB",)])
    P.tt("dve", cB[:], cB[:], magB[:], ALU.mult, reads=[("cB",), ("magB",)], writes=[("cB",)])
    P.tt("dve", cB[:], cB[:], crB[:], ALU.add, reads=[("cB",), ("crB",)], writes=[("cB",)])
    P.stt("dve", sB[:], magB[:], 1.0, sB[:], ALU.add, ALU.mult, reads=[("sB",), ("magB",)], writes=[("sB",)])
    P.tt("dve", magB[:], lrB[:], lrB[:], ALU.mult, reads=[("lrB",), ("magB",), ("sB",), ("cB",)], writes=[("magB",)])
    P.tt("dve", tB[:], liB[:], liB[:], ALU.mult, reads=[("liB",), ("tB",)], writes=[("tB",)])
    P.tt("dve", magB[:], magB[:], tB[:], ALU.add, reads=[("magB",), ("tB",)], writes=[("magB",)])
    P.op("dve", lambda e, o=magB[:]: e.reciprocal(o, o), reads=[("magB",)], writes=[("magB",)])
    P.tt("dve", crB[:], cB[:], lrB[:], ALU.mult, reads=[("cB",), ("lrB",), ("crB",)], writes=[("crB",)])
    P.tt("dve", tB[:], sB[:], liB[:], ALU.mult, reads=[("sB",), ("liB",), ("tB",)], writes=[("tB",)])
    P.tt("dve", crB[:], crB[:], tB[:], ALU.add, reads=[("crB",), ("tB",)], writes=[("crB",)])
    P.tt("dve", crB[:], crB[:], magB[:], ALU.mult, reads=[("crB",), ("magB",)], writes=[("crB",)])
    P.tt("dve", ciB[:], sB[:], lrB[:], ALU.mult, reads=[("sB",), ("lrB",)], writes=[("ciB",)])
    P.tt("dve", tB[:], cB[:], liB[:], ALU.mult, reads=[("cB",), ("liB",), ("tB",)], writes=[("tB",)])
    P.tt("dve", ciB[:], ciB[:], tB[:], ALU.subtract, reads=[("ciB",), ("tB",)], writes=[("ciB",)])
    P.tt("dve", ciB[:], ciB[:], magB[:], ALU.mult, reads=[("ciB",), ("magB",)], writes=[("ciB",)])
    P.dma("sp", bre[:], bpad[0], writes=[("bre",)], group="const3")
    P.dma("sp", bim[:], bpad[1], reads=[("bre",)], writes=[("bim",)], group="const3")
    P.tt("dve", tB[:], crB[:], bre[:], ALU.mult, reads=[("crB",), ("bre",), ("tB",)], writes=[("tB",)])
    P.tt("dve", thB[:], ciB[:], bim[:], ALU.mult, reads=[("ciB",), ("bim",), ("thB",)], writes=[("thB",)])
    P.tt("dve", BBr[:], tB[:], thB[:], ALU.subtract, reads=[("tB",), ("thB",)], writes=[("BBr",)])
    P.tt("dve", tB[:], crB[:], bim[:], ALU.mult, reads=[("crB",), ("bim",), ("tB",), ("BBr",)], writes=[("tB",)])
    P.tt("dve", thB[:], ciB[:], bre[:], ALU.mult, reads=[("ciB",), ("bre",), ("thB",), ("BBr",)], writes=[("thB",)])
    P.tt("dve", BBi[:], tB[:], thB[:], ALU.add, reads=[("tB",), ("thB",)], writes=[("BBi",)])
    P.dma("sp", bre[:], cpad[0], reads=[("BBr",), ("BBi",)], writes=[("bre",)], group="const2")
    P.dma("sp", bim[:], cpad[1], reads=[("BBr",), ("BBi",)], writes=[("bim",)], group="const2")
    P.copy("pool", Cr[:], bre[:], reads=[("bre",)], writes=[("Cr",)])
    P.copy("pool", Ci[:], bim[:], reads=[("bim",)], writes=[("Ci",)])
    P.dma("sp", lrB[:, 0:512], dpad, reads=[("crB",), ("ciB",)], writes=[("lrB",)], group="const2")
    P.copy("pool", Dd[:], lrB[:, 0:512], reads=[("lrB",)], writes=[("Dd",)])
    for j in range(16):
        P.ts("dve", ang[:], io[:], thP[:, j:j + 1], None, ALU.mult, reads=[("io",), ("thP",), ("ang",)], writes=[("ang",)])
        sin_of(ang[:], 0.0, 1.0, sinT[:, j, :], nsinT[:, j, :], ("ang",), ("sinT", j), ("nsinT", j), True)
        sin_of(ang[:], 0.5 * math.pi, 1.0, cosT[:, j, :], nsinT[:, j, :], ("ang",), ("cosT", j), ("nsinT", j), True)
        P.ts("pool", nsinT[:, j, :], sinT[:, j, :], -1.0, None, ALU.mult, reads=[("sinT", j), ("nsinT", j)], writes=[("nsinT", j)])
        P.memset("pool", magT[:, j, :], 1.0, writes=[("magT", j)])
        P.ts("pool", magT[:, j, :], magT[:, j, :], magP[:, j:j + 1], None, ALU.mult, reads=[("magT", j), ("magP",)], writes=[("magT", j)])
    P.memset("pool", car[:], 0.0, writes=[("car", j) for j in range(16)])
    P.memset("pool", cai[:], 0.0, writes=[("cai", j) for j in range(16)])
    k = 0
    work = []
    for bi, (t0, tn) in enumerate(KBL):
        for j in range(16):
            ch = j // 4
            b2 = k % 2
            k += 1
            c_, s_, ns_ = cosT[:, j, 0:tn], sinT[:, j, 0:tn], nsinT[:, j, 0:tn]
            tk = [("cosT", j), ("sinT", j), ("nsinT", j)]

            def stage_a(bi=bi, t0=t0, tn=tn, j=j, ch=ch, b2=b2, c_=c_, s_=s_, tk=tk):
                P.mm(psr[b2][:, 0:tn], BBr[:, j * 128:(j + 1) * 128], hb[:, ch, t0:t0 + tn], True, True,
                     reads=[("BBr",), ("hb", ch)], writes=[("psr", b2)])
                P.mm(psi[b2][:, 0:tn], BBi[:, j * 128:(j + 1) * 128], hb[:, ch, t0:t0 + tn], True, True,
                     reads=[("BBi",), ("hb", ch)], writes=[("psi", b2)])
                P.tt("dve", T1[b2][:, 0:tn], psr[b2][:, 0:tn], c_, ALU.mult, reads=[("psr", b2)] + tk, writes=[("T1", b2)])
                P.tt("dve", T2[b2][:, 0:tn], psi[b2][:, 0:tn], s_, ALU.mult, reads=[("psi", b2)] + tk, writes=[("T2", b2)])
                P.tt("dve", T3[b2][:, 0:tn], psi[b2][:, 0:tn], c_, ALU.mult, reads=[("psi", b2)] + tk, writes=[("T3", b2)])
                P.tt("dve", T4[b2][:, 0:tn], psr[b2][:, 0:tn], s_, ALU.mult, reads=[("psr", b2)] + tk, writes=[("T4", b2)])
                P.tt("pool", xr[b2][:, 0:tn], T1[b2][:, 0:tn], T2[b2][:, 0:tn], ALU.add, reads=[("T1", b2), ("T2", b2)], writes=[("xr", b2)])
                P.tt("pool", xi[b2][:, 0:tn], T3[b2][:, 0:tn], T4[b2][:, 0:tn], ALU.subtract, reads=[("T3", b2), ("T4", b2)], writes=[("xi", b2)])

            def stage_b(bi=bi, t0=t0, tn=tn, j=j, ch=ch, b2=b2, c_=c_, s_=s_, ns_=ns_, tk=tk):
                P.op("dve", lambda e, o=wr[b2][:, 0:tn], d0=magT[:, j, 0:tn], d1=xr[b2][:, 0:tn], ini=car[:, j:j + 1]:
                     e.tensor_tensor_scan(o, d0, d1, ini, ALU.mult, ALU.add),
                     reads=[("magT", j), ("xr", b2), ("car", j)], writes=[("wr", b2)])
                P.op("dve", lambda e, o=wi[b2][:, 0:tn], d0=magT[:, j, 0:tn], d1=xi[b2][:, 0:tn], ini=cai[:, j:j + 1]:
                     e.tensor_tensor_scan(o, d0, d1, ini, ALU.mult, ALU.add),
                     reads=[("magT", j), ("xi", b2), ("cai", j)], writes=[("wi", b2)])
                er, ei = cosT[:, j, tn:tn + 1], sinT[:, j, tn:tn + 1]
                wl_r, wl_i = wr[b2][:, tn - 1:tn], wi[b2][:, tn - 1:tn]
                P.tt("dve", tmp1[:, 0:1], wl_i, ei, ALU.mult, reads=[("wi", b2)] + tk, writes=[("tmp1", 0)])
                P.stt("dve", car[:, j:j + 1], wl_r, er, tmp1[:, 0:1], ALU.mult, ALU.subtract,
                      reads=[("wr", b2), ("tmp1", 0)] + tk, writes=[("car", j)])
                P.tt("dve", tmp1[:, 1:2], wl_r, ei, ALU.mult, reads=[("wr", b2)] + tk, writes=[("tmp1", 1)])
                P.stt("dve", cai[:, j:j + 1], wl_i, er, tmp1[:, 1:2], ALU.mult, ALU.add,
                      reads=[("wi", b2), ("tmp1", 1)] + tk, writes=[("cai", j)])
                P.tt("pool", T1[b2][:, 0:tn], wr[b2][:, 0:tn], c_, ALU.mult, reads=[("wr", b2)] + tk, writes=[("T1", b2)])
                P.tt("pool", T2[b2][:, 0:tn], wi[b2][:, 0:tn], s_, ALU.mult, reads=[("wi", b2)] + tk, writes=[("T2", b2)])
                P.tt("pool", sr[b2][:, 0:tn], T1[b2][:, 0:tn], T2[b2][:, 0:tn], ALU.subtract, reads=[("T1", b2), ("T2", b2)], writes=[("sr", b2)])
                P.tt("dve", T3[b2][:, 0:tn], wr[b2][:, 0:tn], ns_, ALU.mult, reads=[("wr", b2)] + tk, writes=[("T3", b2)])
                P.tt("pool", T4[b2][:, 0:tn], wi[b2][:, 0:tn], c_, ALU.mult, reads=[("wi", b2)] + tk, writes=[("T4", b2)])
                P.tt("pool", si[b2][:, 0:tn], T3[b2][:, 0:tn], T4[b2][:, 0:tn], ALU.subtract, reads=[("T3", b2), ("T4", b2)], writes=[("si", b2)])
                yb = (bi * 4 + ch) % 2
                if j % 4 == 0:
                    P.mm(psy[yb][:, 0:tn], Dd[:, ch * 128:(ch + 1) * 128], hb[:, ch, t0:t0 + tn], True, False,
                         reads=[("Dd",), ("hb", ch)], writes=[("psy", yb)])
                P.mm(psy[yb][:, 0:tn], Cr[:, j * 128:(j + 1) * 128], sr[b2][:, 0:tn], False, False,
                     reads=[("Cr",), ("sr", b2)], writes=[("psy", yb)])
                P.mm(psy[yb][:, 0:tn], Ci[:, j * 128:(j + 1) * 128], si[b2][:, 0:tn], False, j % 4 == 3,
                     reads=[("Ci",), ("si", b2)], writes=[("psy", yb)])
                if j % 4 == 3:
                    P.act(og[yb][:, 0:tn], psy[yb][:, 0:tn], AF.Gelu, reads=[("psy", yb)], writes=[("og", yb)])
                    P.dma("sp", oT.rows(ch * 128, 128)[:, t0:t0 + tn], og[yb][:, 0:tn], reads=[("og", yb)], writes=[("oin", ch, ch, bi)], group=f"oout{yb}")
            work.append((stage_a, stage_b))
    for i_ in range(len(work) + 1):
        if i_ < len(work):
            work[i_][0]()
        if i_ >= 1:
            work[i_ - 1][1]()
    if ctx is None:
        P.emit()
        return nc


def build_ssd(ctx=None):
    nc = bass.Bass("TRN2", target_bir_lowering=False) if ctx is None else ctx.nc
    pre = "" if ctx is None else ctx.pre
    dt = lambda n, s, d=F32: nc.dram_tensor(pre + n, s, d, kind="ExternalInput").ap()
    hbT = dt("hbT", [D, LSEQ], BF16) if ctx is None else None
    w_z = dt("w_z", [D, 1024]); w_x = dt("w_x", [D, 1024]); w_B = dt("w_B", [D, 512]); w_C = dt("w_C", [D, 512]); w_dt = dt("w_dt", [D, 16])
    convw = dt("convw", [4, 2048]); convb = dt("convb", [2048])
    dtb = dt("dtb", [16]); alog = dt("alog", [16]); dskip = dt("dskip", [16]); ng = dt("ng", [1024])
    ident_d = dt("ident", [128, 128]); negm_d = dt("negm", [128, 128]); sel_d = dt("sel", [16, 2048])
    oT = ORows(nc.dram_tensor("oT", [1024, LSEQ], BF16, kind="ExternalOutput").ap()) if ctx is None else ORows(None, ctx.oin)
    P = Prog(nc) if ctx is None else ctx.P
    hb = P.sbuf([128, 8, LSEQ], BF16, "hb")
    praw = P.sbuf([128, LSEQ + 3], F32, "praw")
    PIECE = 1040
    PCS = [(0, 1040), (1040, 1024), (2064, 1024), (3088, 1024)]
    acc = P.sbuf([128, PIECE], F32, "acc")
    cvo = P.sbuf([128, PIECE], BF16, "cvo")
    BT = P.sbuf([128, LSEQ], BF16, "BT"); CT = P.sbuf([128, LSEQ], BF16, "CT")
    V = P.sbuf([128, 33, 256], BF16, "V")
    acT = P.sbuf([16, LSEQ], F32, "acT")
    DPC = [(0, 528)] + [(528 + 512 * i_, 512) for i_ in range(7)]
    dtp = P.sbuf([16, 528], F32, "dtp"); atp = P.sbuf([16, 528], F32, "atp"); onesr = P.sbuf([16, 528], F32, "onesr")
    dt_k = P.sbuf([128, 33, 16], F32, "dt_k"); nac_k = P.sbuf([128, 33, 16], F32, "nac_k")
    bcS = [P.sbuf([128, 512], F32, f"bcS{i}") for i in range(4)]
    E = [P.sbuf([128, 128], F32, f"E{i}") for i in range(4)]
    PT = [P.sbuf([128, 128], BF16, f"PT{i}") for i in range(4)]
    Ew = [P.sbuf([128, 128], F32, f"Ew{i}") for i in range(4)]
    CTs = [P.sbuf([128, 128], BF16, f"CTs{i}") for i in range(4)]
    Vw = [P.sbuf([128, 64], BF16, f"Vw{i}") for i in range(4)]
    wv = [P.sbuf([128, 2], F32, f"wv{i}") for i in range(4)]
    nbc0 = [P.sbuf([128, 4], F32, f"nbc0_{i}") for i in range(4)]
    Bk = P.sbuf([128, 33, 128], BF16, "Bk")
    S32 = P.sbuf([128, 256], F32, "S32")
    Sbf = P.sbuf([128, 256], BF16, "Sbf")
    cnt_q = [0]
    Dt = P.sbuf([128, 16, 128], BF16, "Dt")
    wst = P.sbuf([128, 8, 128], F32, "wst")
    wtb = P.sbuf([128, 8, 128], BF16, "wtb")
    wzb = P.sbuf([128, 8, 256], BF16, "wzb")
    wdtb = P.sbuf([128, 8, 16], BF16, "wdtb")
    wdts = P.sbuf([128, 8, 16], F32, "wdts")
    sz = P.sbuf([128, 512], F32, "sz"); sq = P.sbuf([128, 512], F32, "sqq"); rstd = P.sbuf([128, 512], F32, "rstd")
    szB = P.sbuf([128, 512], F32, "szB"); szC = P.sbuf([128, 512], F32, "szC")
    osb = [P.sbuf([128, 512], BF16, f"osb{i}") for i in range(2)]
    ident = P.sbuf([128, 128], F32, "identS"); identb = P.sbuf([128, 128], BF16, "identb"); negm = P.sbuf([128, 128], F32, "negmS")
    sel = P.sbuf([16, 2048], F32, "selS")
    cw = P.sbuf([128, 16, 4], F32, "cw"); cb = P.sbuf([128, 16], F32, "cbias")
    dtb_t = P.sbuf([16, 1], F32, "dtb_t"); A_t = P.sbuf([16, 1], F32, "A_t"); one_t = P.sbuf([16, 1], F32, "one_t")
    dbc = P.sbuf([128, 16], F32, "dbc"); ngt = P.sbuf([64, 16], F32, "ngt"); ones64 = P.sbuf([64, 64], F32, "ones64")
    car = P.sbuf([16, 1], F32, "carS")
    po = [P.psum([128, 512], F32, f"po{i}") for i in range(4)]
    pcb = [P.psum([128, 512], F32, f"pcb{i}") for i in range(2)]
    psx = P.psum([128, 512], F32, "psx")
    pst = P.psum([128, 512], F32, "pst")
    psxb = psx[:].bitcast(BF16)

    P.dma("sp", ident[:], ident_d, writes=[("ident",)], group="const")
    P.dma("sp", negm[:], negm_d, writes=[("negm",)], group="const")
    P.dma("sp", sel[:], sel_d, writes=[("sel",)], group="const")
    for k_ in range(4):
        P.dma("sp", cw[:, :, k_], convw[k_].rearrange("(c p) -> p c", p=128), writes=[("cw",)], group="const", slow=True)
    P.dma("sp", cb[:], convb.rearrange("(c p) -> p c", p=128), writes=[("cbias",)], group="const", slow=True)
    P.dma("sp", dtb_t[:], dtb.rearrange("(p o) -> p o", o=1), writes=[("dtb",)], group="const", slow=True)
    P.dma("sp", A_t[:], alog.rearrange("(p o) -> p o", o=1), writes=[("A",)], group="const", slow=True)
    P.dma("sp", dbc[:], dskip.partition_broadcast(128), writes=[("dbc",)], group="const")
    P.dma("sp", ngt[:], ng.rearrange("(h p) -> p h", p=64), writes=[("ngt",)], group="const", slow=True)
    P.copy("pool", identb[:], ident[:], reads=[("ident",)], writes=[("identb",)])
    P.memset("pool", ones64[:], 1.0 / 256, writes=[("ones64",)])
    P.memset("pool", one_t[:], 1.0, writes=[("one_t",)])
    P.memset("pool", onesr[:], 1.0, writes=[("onesr",)])
    P.memset("pool", praw[:, 0:3], 0.0, writes=[("praw0",)])
    P.memset("pool", car[:], 0.0, writes=[("car",)])
    P.act(A_t[:], A_t[:], AF.Exp, reads=[("A",)], writes=[("A",)])
    P.ts("dve", A_t[:], A_t[:], -1.0, None, ALU.mult, reads=[("A",)], writes=[("A",)])
    for h in range(16):
        P.ts("dve", Dt[:, h, :], ident[:], dbc[:, h:h + 1], None, ALU.mult, reads=[("ident",), ("dbc",)], writes=[("Dt",)])
    _load_seq(P, ctx, hb, 8, hbT, lambda c, part: [("hb", c)])
    HBK = [("hb", c) for c in range(8)]
    P.dma("sp", wdts[:], w_dt.rearrange("(c p) f -> p c f", p=128), writes=[("wdts",)], group="const", slow=True)
    P.copy("pool", wdtb[:], wdts[:], reads=[("wdts",)], writes=[("wdtb",)])
    for pi, (p0, pn) in enumerate(DPC):
        for s0 in range(0, pn, 512):
            sn = min(512, pn - s0)
            for c in range(8):
                P.mm(psx[0:16, 0:sn], wdtb[:, c, :], hb[:, c, p0 + s0:p0 + s0 + sn], c == 0, c == 7,
                     reads=[("wdtb",), ("hb", c)], writes=[("psx",)])
            P.act(dtp[:, s0:s0 + sn], psx[0:16, 0:sn], AF.Exp, bias=dtb_t[:, 0:1], reads=[("psx",), ("dtb",)], writes=[("dtp",)])
        P.act(dtp[:, 0:pn], dtp[:, 0:pn], AF.Ln, bias=one_t[:, 0:1], reads=[("dtp",), ("one_t",)], writes=[("dtp",)])
        P.ts("dve", atp[:, 0:pn], dtp[:, 0:pn], A_t[:, 0:1], None, ALU.mult, reads=[("dtp",), ("A",)], writes=[("atp",)])
        P.op("dve", lambda e, o=acT[:, p0:p0 + pn], d0=onesr[:, 0:pn], d1=atp[:, 0:pn], ini=car[:, 0:1]:
             e.tensor_tensor_scan(o, d0, d1, ini, ALU.mult, ALU.add),
             reads=[("onesr",), ("atp",), ("car",)], writes=[("acT", pi)])
        P.copy("dve", car[:], acT[:, p0 + pn - 1:p0 + pn], reads=[("acT", pi)], writes=[("car",)])
        for kb, (ks, nk) in enumerate(KBL):
            if not (p0 <= ks < p0 + pn):
                continue
            assert ks + nk <= p0 + pn
            P.tr(psx[0:nk, 0:16], dtp[:, ks - p0:ks - p0 + nk], ident[0:16, 0:16], reads=[("dtp",), ("ident",)], writes=[("psx",)])
            P.copy("act", dt_k[0:nk, kb, :], psx[0:nk, 0:16], reads=[("psx",)], writes=[("dt_k", kb)])
            P.tr(psx[0:nk, 0:16], acT[:, ks:ks + nk], ident[0:16, 0:16], reads=[("acT", pi), ("ident",)], writes=[("psx",)])
            P.ts("dve", nac_k[0:nk, kb, :], psx[0:nk, 0:16], -1.0, None, ALU.mult, reads=[("psx",)], writes=[("nac_k", kb)])
    ACK = [("acT", pi) for pi in range(8)]

    def proj_conv(wsrc, col0, f, out_fn):
        P.dma("sp", wst[:], wsrc[:, col0:col0 + 128].rearrange("(c p) f -> p c f", p=128), writes=[("wst",)], group="wst")
        P.copy("act", wtb[:], wst[:], reads=[("wst",)], writes=[("wtb",)])
        for ti, (t0, tn) in enumerate(QCH):
            kq = ti % 2
            for c in range(8):
                P.mm(pcb[kq][:, 0:tn], wtb[:, c, :], hb[:, c, t0:t0 + tn], c == 0, c == 7,
                     reads=[("wtb",), ("hb", c)], writes=[("pcb", kq)])
            P.copy("act", praw[:, 3 + t0:3 + t0 + tn], pcb[kq][:, 0:tn], reads=[("pcb", kq), ("praw0",)], writes=[("praw", ti)])
        PK = [("praw", ti) for ti in range(9)] + [("praw0",)]
        for pi, (p0, pn) in enumerate(PCS):
            P.ts("dve", acc[:, 0:pn], praw[:, p0:p0 + pn], cw[:, f, 0:1], None, ALU.mult, reads=PK + [("cw",)], writes=[("acc",)])
            for k in (1, 2, 3):
                P.stt("dve", acc[:, 0:pn], praw[:, p0 + k:p0 + k + pn], cw[:, f, k:k + 1], acc[:, 0:pn], ALU.mult, ALU.add,
                      reads=PK + [("cw",), ("acc",)], writes=[("acc",)])
            out_fn(pi, p0, pn)

    ncount = [0]

    def nx():
        ncount[0] += 1
        return ncount[0] - 1

    for g in range(4):
        def outB(pi, p0, pn, g=g):
            P.act(BT[:, p0:p0 + pn], acc[:, 0:pn], AF.Silu, bias=cb[:, 8 + g:9 + g], reads=[("acc",), ("cbias",)], writes=[("BT", pi)])
            for kb, (ks, nk) in enumerate(KBL):
                if not (p0 <= ks < p0 + pn):
                    continue
                P.tr(psxb[0:nk, 0:128], BT[:, ks:ks + nk], identb[:], reads=[("BT", pi), ("identb",)], writes=[("psx",)])
                P.copy("act" if kb % 2 else "dve", Bk[0:nk, kb, :], psxb[0:nk, 0:128], reads=[("psx",)], writes=[("Bk", kb)])

        def outC(pi, p0, pn, g=g):
            P.act(CT[:, p0:p0 + pn], acc[:, 0:pn], AF.Silu, bias=cb[:, 12 + g:13 + g], reads=[("acc",), ("cbias",)], writes=[("CT", pi)])
        proj_conv(w_B, g * 128, 8 + g, outB)
        proj_conv(w_C, g * 128, 12 + g, outC)
        for xc in range(2):
            f = 2 * g + xc

            def outX(pi, p0, pn, f=f, xc=xc):
                P.act(cvo[:, 0:pn], acc[:, 0:pn], AF.Silu, bias=cb[:, f:f + 1], reads=[("acc",), ("cbias",)], writes=[("cvo",)])
                for kb, (ks, nk) in enumerate(KBL):
                    if not (p0 <= ks < p0 + pn):
                        continue
                    P.tr(psxb[0:nk, 0:128], cvo[:, ks - p0:ks - p0 + nk], identb[:], reads=[("cvo",), ("identb",)], writes=[("psx",)])
                    P.copy("act" if kb % 2 else "dve", V[0:nk, kb, xc * 128:(xc + 1) * 128], psxb[0:nk, 0:128],
                           reads=[("psx",)], writes=[("V", kb)])
            proj_conv(w_x, f * 128, f, outX)
        for hf in range(2):
            P.dma("sp", wst[:], w_z[:, g * 256 + hf * 128:g * 256 + (hf + 1) * 128].rearrange("(c p) f -> p c f", p=128),
                  writes=[("wst",)], group="wst")
            P.copy("act", wzb[:, :, hf * 128:(hf + 1) * 128], wst[:], reads=[("wst",)], writes=[("wzb",)])
        BK = [("BT", pi) for pi in range(4)]
        CK = [("CT", pi) for pi in range(4)]
        P.memset("pool", S32[:], 0.0, writes=[("S32", hl_) for hl_ in range(4)])
        P.memset("pool", Sbf[:], 0.0, writes=[("Sbf", hl_) for hl_ in range(4)])
        for c, (qs, nq) in enumerate(QCH):
            qb0 = 4 * (c - 1) + 1
            blocks = [(0, 0, 16)] if c == 0 else [(qb0 + i_, 128 * i_, 128) for i_ in range(4)]
            for hl in range(4):
                h = 4 * g + hl
                P.mm(psx[:, 0:nq], sel[:, h * 128:(h + 1) * 128], acT[:, qs:qs + nq], True, True,
                     reads=[("sel",)] + ACK, writes=[("psx",)])
                P.copy("act", bcS[hl][:, 0:nq], psx[:, 0:nq], reads=[("psx",)], writes=[("bcS", hl)])
                if c > 0:
                    P.mm(pst[:, 0:512], sel[:, h * 128:(h + 1) * 128], acT[:, qs - 1:qs - 1 + 512], True, True,
                         reads=[("sel",)] + ACK, writes=[("pst",)])
                    for i_ in range(4):
                        P.ts("dve", nbc0[hl][:, i_:i_ + 1], pst[:, 128 * i_:128 * i_ + 1], -1.0, None, ALU.mult,
                             reads=[("pst",)], writes=[("nbc0", hl)])
                else:
                    P.memset("pool", nbc0[hl][:], 0.0, writes=[("nbc0", hl)])
            szs = [(sz, ("sz",)), (szB, ("szB",)), (szC, ("szC",)), (cvo[:].bitcast(F32)[:, 0:512], ("cvo",))]
            for hl in range(4):
                for cc in range(8):
                    P.mm(psx[0:64, 0:nq], wzb[:, cc, hl * 64:(hl + 1) * 64], hb[:, cc, qs:qs + nq], cc == 0, cc == 7,
                         reads=[("wzb",), ("hb", cc)], writes=[("psx",)])
                P.act(szs[hl][0][0:64, 0:nq], psx[0:64, 0:nq], AF.Silu, reads=[("psx",)], writes=[szs[hl][1]])
            for i_, (kb, col0, nb) in enumerate(blocks):
                ks, nk = KBL[kb]
                kq = cnt_q[0] % 2
                cnt_q[0] += 1
                P.mm(pcb[kq][0:nk, 0:nb], BT[:, ks:ks + nk], CT[:, ks:ks + nb], True, True,
                     reads=BK + CK, writes=[("pcb", kq)])
                HL = range(4)
                hs_ = [4 * g + hl for hl in HL]
                for hl in HL:
                    P.tt("dve", E[hl][0:nk, 0:nb], bcS[hl][0:nk, col0:col0 + nb], negm[0:nk, 0:nb], ALU.add,
                         reads=[("bcS", hl), ("negm",)], writes=[("E", hl)])
                for hl in HL:
                    P.act(E[hl][0:nk, 0:nb], E[hl][0:nk, 0:nb], AF.Exp, bias=nac_k[0:nk, kb, hs_[hl]:hs_[hl] + 1],
                          reads=[("E", hl), ("nac_k", kb)], writes=[("E", hl)])
                if kb > 0:
                    for hl in HL:
                        P.act(Ew[hl][:, 0:nb], bcS[hl][:, col0:col0 + nb], AF.Exp, bias=nbc0[hl][:, i_:i_ + 1],
                              reads=[("bcS", hl), ("nbc0", hl)], writes=[("Ew", hl)])
                if kb < 32:
                    for hl in HL:
                        al = bcS[hl][:, col0 + nb - 1:col0 + nb]
                        P.act(wv[hl][0:nk, 0:1], nac_k[0:nk, kb, hs_[hl]:hs_[hl] + 1], AF.Exp, bias=al[0:nk, :],
                              reads=[("bcS", hl), ("nac_k", kb)], writes=[("wv", hl, 0)])
                        P.act(wv[hl][:, 1:2], al, AF.Exp, bias=nbc0[hl][:, i_:i_ + 1],
                              reads=[("bcS", hl), ("nbc0", hl)], writes=[("wv", hl, 1)])
                for hl in HL:
                    P.stt("dve", PT[hl][0:nk, 0:nb], pcb[kq][0:nk, 0:nb], dt_k[0:nk, kb, hs_[hl]:hs_[hl] + 1], E[hl][0:nk, 0:nb], ALU.mult, ALU.mult,
                          reads=[("pcb", kq), ("dt_k", kb), ("E", hl)], writes=[("PT", hl)])
                if kb > 0:
                    for hl in HL:
                        P.tt("dve", CTs[hl][:, 0:nb], CT[:, ks:ks + nb], Ew[hl][:, 0:nb], ALU.mult,
                             reads=CK + [("Ew", hl)], writes=[("CTs", hl)])
                if kb < 32:
                    for hl in HL:
                        P.tt("dve", wv[hl][0:nk, 0:1], wv[hl][0:nk, 0:1], dt_k[0:nk, kb, hs_[hl]:hs_[hl] + 1], ALU.mult,
                             reads=[("wv", hl, 0), ("dt_k", kb)], writes=[("wv", hl, 0)])
                    for hl in HL:
                        P.ts("dve", Vw[hl][0:nk, :], V[0:nk, kb, hl * 64:(hl + 1) * 64], wv[hl][0:nk, 0:1], None, ALU.mult,
                             reads=[("V", kb), ("wv", hl, 0)], writes=[("Vw", hl)])
                for hl in HL:
                    P.mm(po[hl][0:64, col0:col0 + nb], V[0:nk, kb, hl * 64:(hl + 1) * 64], PT[hl][0:nk, 0:nb], True, False,
                         reads=[("PT", hl), ("V", kb)], writes=[("po", hl)])
                    P.mm(po[hl][0:64, col0:col0 + nb], V[0:nk, kb, hl * 64:(hl + 1) * 64], Dt[0:nk, hs_[hl], 0:nb], False, kb == 0,
                         reads=[("Dt",), ("V", kb)], writes=[("po", hl)])
                    if kb > 0:
                        P.mm(po[hl][0:64, col0:col0 + nb], Sbf[:, hl * 64:(hl + 1) * 64], CTs[hl][:, 0:nb], False, True,
                             reads=[("Sbf", hl), ("CTs", hl)], writes=[("po", hl)])
                if kb < 32:
                    for hl in HL:
                        P.mm(pst[:, hl * 64:(hl + 1) * 64], Bk[0:nk, kb, :], Vw[hl][0:nk, :], True, True,
                             reads=[("Bk", kb), ("Vw", hl)], writes=[("pst",)])
                    for hl in HL:
                        P.stt("dve", S32[:, hl * 64:(hl + 1) * 64], S32[:, hl * 64:(hl + 1) * 64], wv[hl][:, 1:2], pst[:, hl * 64:(hl + 1) * 64],
                              ALU.mult, ALU.add, reads=[("S32", hl), ("wv", hl, 1), ("pst",)], writes=[("S32", hl)])
                    for hl in HL:
                        P.copy("act", Sbf[:, hl * 64:(hl + 1) * 64], S32[:, hl * 64:(hl + 1) * 64], reads=[("S32", hl)], writes=[("Sbf", hl)])
            for hl in range(4):
                h = 4 * g + hl
                P.tt("dve", bcS[hl][0:64, 0:nq], po[hl][0:64, 0:nq], szs[hl][0][0:64, 0:nq], ALU.mult,
                     reads=[("po", hl), szs[hl][1], ("bcS", hl)], writes=[("bcS", hl)])
                P.act(sq[0:64, 0:nq], bcS[hl][0:64, 0:nq], AF.Square, reads=[("bcS", hl)], writes=[("sqq",)])
                P.mm(pst[0:64, 0:nq], ones64[:], sq[0:64, 0:nq], hl == 0, hl == 3, reads=[("sqq",), ("ones64",)], writes=[("pst",)])
            P.ts("dve", rstd[0:64, 0:nq], pst[0:64, 0:nq], RMS_EPS, None, ALU.add, reads=[("pst",)], writes=[("rstd",)])
            P.act(rstd[0:64, 0:nq], rstd[0:64, 0:nq], AF.Sqrt, reads=[("rstd",)], writes=[("rstd",)])
            P.op("dve", lambda e, o=rstd[0:64, 0:nq]: e.reciprocal(o, o), reads=[("rstd",)], writes=[("rstd",)])
            for hl in range(4):
                h = 4 * g + hl
                oi = nx() % 2
                P.tt("dve", bcS[hl][0:64, 0:nq], bcS[hl][0:64, 0:nq], rstd[0:64, 0:nq], ALU.mult,
                     reads=[("bcS", hl), ("rstd",)], writes=[("bcS", hl)])
                P.act(osb[oi][0:64, 0:nq], bcS[hl][0:64, 0:nq], AF.Identity, scale=ngt[:, h:h + 1],
                      reads=[("bcS", hl), ("ngt",)], writes=[("osb", oi)])
                P.dma("sp", oT.rows(h * 64, 64)[:, qs:qs + nq], osb[oi][0:64, 0:nq], reads=[("osb", oi)], writes=[("oin", h // 2, h, c)], group=f"oout{oi}")
    if ctx is None:
        P.emit()
        return nc


def ssd_inputs(d, hbT, hh):
    w = d['ssd_w_in'][0]
    DI = 2048
    sl = slice(hh * 1024, (hh + 1) * 1024)
    w_z = w[:, 0:DI][:, sl]
    w_x = w[:, DI:2 * DI][:, sl]
    w_B = w[:, 2 * DI:2 * DI + 1024][:, hh * 512:(hh + 1) * 512]
    w_C = w[:, 2 * DI + 1024:2 * DI + 2048][:, hh * 512:(hh + 1) * 512]
    w_dt = w[:, 2 * DI + 2048:][:, hh * 16:(hh + 1) * 16]
    cw = d['ssd_conv_w'][0]; cbv = d['ssd_conv_b'][0]
    convw = np.concatenate([cw[:, 0:DI][:, sl], cw[:, DI:DI + 1024][:, hh * 512:(hh + 1) * 512], cw[:, DI + 1024:][:, hh * 512:(hh + 1) * 512]], 1)
    convb = np.concatenate([cbv[0:DI][sl], cbv[DI:DI + 1024][hh * 512:(hh + 1) * 512], cbv[DI + 1024:][hh * 512:(hh + 1) * 512]])
    kk = np.arange(128)[:, None]; qq = np.arange(128)[None, :]
    negm = np.where(kk <= qq, 0.0, NEG).astype(np.float32)
    sel = np.zeros((16, 16, 128), np.float32)
    for h in range(16):
        sel[h, h, :] = 1.0
    c = np.ascontiguousarray
    return dict(hbT=hbT, w_z=c(w_z), w_x=c(w_x), w_B=c(w_B), w_C=c(w_C), w_dt=c(w_dt), convw=c(convw), convb=c(convb),
                dtb=c(d['ssd_dt_bias'][0][hh * 16:(hh + 1) * 16]), alog=c(d['ssd_a_log'][0][hh * 16:(hh + 1) * 16]),
                dskip=c(d['ssd_d'][0][hh * 16:(hh + 1) * 16]), ng=c(d['ssd_norm_g'][0][sl]),
                ident=np.eye(128, dtype=np.float32), negm=negm, sel=sel.reshape(16, 2048))


def build_t(prev, n_ffn, want_bf):
    nc = bass.Bass("TRN2", target_bir_lowering=False)
    dt = lambda n, s, d=F32: nc.dram_tensor(n, s, d, kind="ExternalInput").ap()
    hT = dt("hT", [D, NT])
    nln = n_ffn + (1 if prev else 0)
    lg = dt("ln_g", [nln, D]); lb = dt("ln_b", [nln, D])
    ws = [(dt(f"w1_{i}", [D, FF]), dt(f"w3_{i}", [D, FF]), dt(f"w2_{i}", [FF, D])) for i in range(n_ffn)]
    if prev in ("lin8", "lin16"):
        kc = 8 if prev == "lin8" else 16
        oT = dt("oT", [kc * 128, NT], BF16); w_out = dt("w_out", [kc * 128, D])
    elif prev == "glu":
        oT = dt("oT", [D, NT], BF16); w_glu = dt("w_glu", [D, 2 * D]); b_glu = dt("b_glu", [2 * D])
    o32 = nc.dram_tensor("o32", [D, NT], F32, kind="ExternalOutput").ap()
    obf = nc.dram_tensor("obf", [D, NT], BF16, kind="ExternalOutput").ap() if want_bf else None
    P = Prog(nc)
    T = TPhase(P, nc)
    T.load_ln(lg, lb, nln)
    T.load_h(hT)
    s = 0
    if prev in ("lin8", "lin16"):
        T.outproj(oT, w_out, kc); T.layer_norm(s); s += 1
    elif prev == "glu":
        T.glu(oT, w_glu, b_glu); T.layer_norm(s); s += 1
    for i in range(n_ffn):
        T.ffn(*ws[i]); T.layer_norm(s); s += 1
    T.store_h(o32, obf)
    P.emit()
    return nc


def s5_inputs(d, hbT, hh):
    G0 = hh * 32
    lam_re = d['s5_lam_re'][0][G0:G0 + 32]; lam_im = d['s5_lam_im'][0][G0:G0 + 32]
    lstep = np.repeat(d['s5_log_step'][0][G0:G0 + 32][:, None], 64, 1)

    def pl(a):
        return np.ascontiguousarray(a.reshape(16, 2, 64).transpose(1, 2, 0).reshape(128, 16))
    pl3 = np.stack([pl(lam_re), pl(lam_im), pl(lstep)]).astype(np.float32)
    bc3 = np.stack([lam_re.reshape(-1), lam_im.reshape(-1), lstep.reshape(-1)]).astype(np.float32)
    bpad = np.zeros((2, 128, 16, 128), np.float32)
    cpad = np.zeros((2, 128, 16, 128), np.float32)
    for k, (bn, cn) in enumerate([('s5_b_re', 's5_c_re'), ('s5_b_im', 's5_c_im')]):
        B = d[bn][0][G0:G0 + 32]
        C = d[cn][0][G0:G0 + 32]
        for g in range(32):
            j, gl, g8 = g // 2, g % 2, g % 8
            bpad[k, g8 * 16:(g8 + 1) * 16, j, gl * 64:(gl + 1) * 64] = B[g].T
            cpad[k, gl * 64:(gl + 1) * 64, j, g8 * 16:(g8 + 1) * 16] = C[g].T
    dv = d['s5_d'][0][G0:G0 + 32].reshape(4, 128)
    dpad = np.zeros((128, 4, 128), np.float32)
    for c in range(4):
        dpad[np.arange(128), c, np.arange(128)] = dv[c]
    return dict(hbh=None if hbT is None else np.ascontiguousarray(hbT[hh * 512:(hh + 1) * 512]), pl3=pl3, bc3=bc3,
                bpad=bpad.reshape(2, 128, 2048), cpad=cpad.reshape(2, 128, 2048), dpad=dpad.reshape(128, 512),
                iota=np.arange(129, dtype=np.float32))


_DBG = None


def _run(nc, in_maps):
    res = run_bass_kernel_spmd(nc, in_maps, core_ids=list(range(NCORES)))
    if _DBG is not None:
        _DBG.append(res.results)
    return res.results


def _seq_from_shards(shards):
    return [np.ascontiguousarray(np.concatenate([shards[2 * b], shards[2 * b + 1][:, 16:]], axis=1)) for b in range(4)]


def _shards_from_seq(seqs_halves):
    out = []
    for r in range(NCORES):
        b, half = r // 2, r % 2
        full = np.concatenate([seqs_halves[2 * b], seqs_halves[2 * b + 1]], axis=0)
        if half == 0:
            out.append(np.ascontiguousarray(full[:, 0:NT]))
        else:
            z = np.zeros((full.shape[0], 16), full.dtype)
            out.append(np.ascontiguousarray(np.concatenate([z, full[:, NT:]], axis=1)))
    return out


def _bucket_table(n):
    dist = np.arange(n)
    max_exact = 16
    df = np.maximum(dist, max_exact).astype(np.float32)
    large = max_exact + (np.log(df / np.float32(max_exact)) / np.float32(math.log(128 / max_exact)) * np.float32(32 - max_exact)).astype(np.int32)
    large = np.minimum(large, 31)
    return np.where(dist < max_exact, dist, large)


def diff_consts():
    tab = _bucket_table(400)
    ki = np.arange(128)[:, None]; qi = np.arange(256)[None, :]
    dist = qi - ki
    bk = tab[np.maximum(dist, 0)]
    m_dd = np.stack([((bk == b) & (dist >= 0)) for b in range(32)]).astype(np.float32)
    n_dd = np.where(dist >= 0, 0.0, NEG).astype(np.float32)
    k2 = np.arange(16)[:, None]; q2 = np.arange(16)[None, :]
    d2 = q2 - k2
    b2 = tab[np.maximum(d2, 0)]
    m_md = np.stack([((b2 == b) & (d2 >= 0)) for b in range(32)]).astype(np.float32)
    n_md = np.where(d2 >= 0, 0.0, NEG).astype(np.float32)
    q3 = np.arange(128)[None, :]
    d3 = (16 + q3) - k2
    b3 = tab[d3]
    m_mq1 = np.stack([(b3 == b) for b in range(32)]).astype(np.float32)
    return dict(m_dd=m_dd, n_dd=n_dd, m_md=m_md, n_md=n_md, m_mq1=m_mq1)


def diff_inputs(d, hbT, hh, consts):
    w = d['diff_w_qkv'][0]
    sl = slice(hh * 512, (hh + 1) * 512)
    lam4 = np.stack([d['diff_lam_q1'][0], d['diff_lam_k1'][0], d['diff_lam_q2'][0], d['diff_lam_k2'][0]])
    rb = np.ascontiguousarray(d['rel_bias'][:, hh * 4:(hh + 1) * 4]).reshape(-1)
    c = np.ascontiguousarray
    r = dict(hbT=hbT, w_q=c(w[:, 0:1024][:, sl]), w_k=c(w[:, 1024:2048][:, sl]), w_v=c(w[:, 2048:3072][:, sl]),
             lam4=c(lam4), subg=d['diff_subln_g'][0], rb=rb)
    r.update(consts)
    return r


def mla_inputs(d, hbT, hh):
    w_in = d['mla_w_in'][0]
    kr = w_in[:, 640:672]
    krs = np.concatenate([kr[:, 16:], kr[:, :16]], 1)
    z = np.zeros((1024, 64), np.float32)
    w_in_ext = np.concatenate([w_in[:, :640], z, kr, z, krs], 1)
    wq = d['mla_w_uq'][0].reshape(384, 16, 96)[:, hh * 8:(hh + 1) * 8]
    wqs = np.concatenate([wq[:, :, :64], wq[:, :, 80:96], wq[:, :, 64:80]], 2)
    wkv = d['mla_w_ukv'][0].reshape(256, 16, 128)[:, hh * 8:(hh + 1) * 8]
    pos = np.arange(LSEQ, dtype=np.float32)
    inv = (np.float32(10000.0) ** (-np.arange(0, 32, 2, dtype=np.float32) / np.float32(32))).astype(np.float32)
    ang = pos[None, :] * inv[:, None]
    cos = np.cos(ang).astype(np.float32); sin = np.sin(ang).astype(np.float32)
    ropeC = np.concatenate([cos, cos], 0); ropeS = np.concatenate([-sin, sin], 0)
    kk = np.arange(128)[:, None]; qq = np.arange(128)[None, :]
    mask = np.where(kk <= qq, 0.0, NEG).astype(np.float32)
    c = np.ascontiguousarray
    return dict(hbT=hbT, w_in_ext=c(w_in_ext), w_uq=c(wq.reshape(384, 768)), w_uq_sw=c(wqs.reshape(384, 768)),
                w_uk=c(wkv[:, :, :64].reshape(256, 512)), w_uv=c(wkv[:, :, 64:].reshape(256, 512)),
                qg=d['mla_q_norm_g'][0], kvg=d['mla_kv_norm_g'][0], ropeC=c(ropeC), ropeS=c(ropeS), mask_dd=mask)


MIX = ["ssd", "diff", "s5", "mla"]
MIX_FH = {"ssd": 1024, "diff": 512, "s5": 512, "mla": 512}
MIX_PREV = {"ssd": ("lin16", 16), "diff": ("lin8", 8), "s5": ("glu", 8), "mla": ("lin8", 8)}


def build_fused(nstage=9):
    nc = bass.Bass("TRN2", target_bir_lowering=False)
    ein = lambda n, s, d=F32: nc.dram_tensor(n, s, d, kind="ExternalInput").ap()
    hT = ein("hT", [D, NT])
    selv_d = ein("selv", [128, 2])
    out32 = nc.dram_tensor("out32", [D, NT], F32, kind="ExternalOutput").ap()
    hres = nc.dram_tensor("hres", [D, NT], F32).ap()
    xin = [[nc.dram_tensor(f"xin{i}_{k}", [256, NT], BF16).ap() for k in range(4)] for i in range(4)]
    xall = [[nc.dram_tensor(f"xall{i}_{k}", [512, NT], BF16).ap() for k in range(4)] for i in range(4)]
    oin = [[nc.dram_tensor(f"oin{i}_{j}", [128, LSEQ], BF16).ap() for j in range(MIX_FH[MIX[i]] // 128)] for i in range(4)]
    oall = [[nc.dram_tensor(f"oall{i}_{j}", [256, LSEQ], BF16).ap() for j in range(MIX_FH[MIX[i]] // 128)] for i in range(4)]
    P = Prog(nc, arena=True)

    def t_phase(ti):
        pre = f"t{ti}_"
        last = (ti == 4)
        n_ffn = 1 if (ti == 0 or last) else 2
        nln = n_ffn + (0 if ti == 0 else 1)
        lg = ein(pre + "ln_g", [nln, D]); lb = ein(pre + "ln_b", [nln, D])
        ws = [(ein(pre + f"w1_{k}", [D, FF]), ein(pre + f"w3_{k}", [D, FF]), ein(pre + f"w2_{k}", [FF, D])) for k in range(n_ffn)]
        T = TPhase(P, nc)
        T.load_ln(lg, lb, nln)
        s = 0
        if ti == 0:
            T.load_h(hT)
        else:
            T.load_h(hres)
            sv = P.sbuf([128, 2], F32, "selv")
            P.dma("sp", sv[:], selv_d, writes=[("selv",)], group="const")
            T.selv = sv
            prev, kc = MIX_PREV[MIX[ti - 1]]
            if prev == "glu":
                w_glu = ein(pre + "w_glu", [D, 2 * D]); b_glu = ein(pre + "b_glu", [2 * D])
                T.glu(oall[ti - 1], w_glu, b_glu)
            else:
                w_out = ein(pre + "w_out", [kc * 128, D])
                T.outproj(oall[ti - 1], w_out, kc)
            T.layer_norm(s); s += 1
        for k in range(n_ffn):
            T.ffn(*ws[k]); T.layer_norm(s); s += 1
        if last or (2 * ti + 1 >= nstage):
            T.store_h(out32, None)
        else:
            T.store_h(hres, xin[ti])
            for k in range(4):
                P.cc_allgather(xin[ti][k], xall[ti][k], reads=[("xin", 2 * k), ("xin", 2 * k + 1)], writes=[("xall", k)])

    t_phase(0)
    for i in range(4):
        if 2 * i + 1 >= nstage:
            break
        P.phase_begin()
        pre = f"m{i}_"
        sv = P.sbuf([128, 2], F32, "selv")
        P.dma("sp", sv[:], selv_d, writes=[("selv",)], group="const")
        ctx = Ctx(nc, P, pre, xall=xall[i], oin=oin[i], selv=sv)
        if MIX[i] == "ssd":
            build_ssd(ctx)
        elif MIX[i] == "diff":
            build_diff(0.8 - 0.6 * math.exp(-0.3 * i), ctx)
        elif MIX[i] == "s5":
            build_s5(ctx)
        else:
            build_mla(ctx)
        for j in range(len(oin[i])):
            okeys = [k for k in P.last_w if isinstance(k, tuple) and len(k) > 1 and k[0] == "oin" and k[1] == j]
            P.cc_allgather(oin[i][j], oall[i][j], reads=okeys, writes=[("oall", j)])
        P.phase_begin()
        t_phase(i + 1)
    P.emit()
    return nc


def fused_inputs(d, r, hs, consts, nstage=9):
    c = np.ascontiguousarray
    b, half = r // 2, r % 2
    m = dict(hT=hs[r], selv=np.tile(np.array([[1.0, 0.0]] if half == 0 else [[0.0, 1.0]], np.float32), (128, 1)))

    def ffnw(pre, i, j, k):
        m[pre + f"w1_{k}"] = c(d['ffn_w1'][i, j]); m[pre + f"w3_{k}"] = c(d['ffn_w3'][i, j]); m[pre + f"w2_{k}"] = c(d['ffn_w2'][i, j])

    m["t0_ln_g"] = c(d['ln_g'][0, 0:1]); m["t0_ln_b"] = c(d['ln_b'][0, 0:1])
    ffnw("t0_", 0, 0, 0)
    for i in range(DEPTH):
        if 2 * i + 1 >= nstage:
            break
        pre = f"m{i}_"
        if i == 0:
            mi = ssd_inputs(d, None, half)
        elif i == 1:
            mi = diff_inputs(d, None, half, consts)
        elif i == 2:
            mi = s5_inputs(d, None, half)
        else:
            mi = mla_inputs(d, None, half)
        for k_, v in mi.items():
            if k_ in ("hbT", "hbh"):
                continue
            m[pre + k_] = v
        tp = f"t{i + 1}_"
        last = (i == DEPTH - 1)
        lng = [d['ln_g'][i, 1], d['ln_g'][i, 2]] + ([] if last else [d['ln_g'][i + 1, 0]])
        lnb = [d['ln_b'][i, 1], d['ln_b'][i, 2]] + ([] if last else [d['ln_b'][i + 1, 0]])
        m[tp + "ln_g"] = c(np.stack(lng)); m[tp + "ln_b"] = c(np.stack(lnb))
        ffnw(tp, i, 1, 0)
        if not last:
            ffnw(tp, i + 1, 0, 1)
        if i == 0:
            m[tp + "w_out"] = c(d['ssd_w_out'][0])
        elif i == 1:
            m[tp + "w_out"] = c(d['diff_w_out'][0])
        elif i == 2:
            m[tp + "w_glu"] = c(d['s5_w_glu'][0]); m[tp + "b_glu"] = c(d['s5_b_glu'][0])
        else:
            m[tp + "w_out"] = c(d['mla_w_out'][0])
    return m


NSTAGE = 9


def kernel(**inputs):
    d = {k: np.asarray(v) for k, v in inputs.items()}
    x, meta = d['x'], d['meta']
    c = np.ascontiguousarray
    hs = []
    for r in range(NCORES):
        b, half = r // 2, r % 2
        if half == 0:
            t = np.concatenate([meta, x[b, :2048]], 0)
        else:
            t = np.concatenate([np.zeros_like(meta), x[b, 2048:]], 0)
        hs.append(c(t.T))
    consts = diff_consts()
    nc = build_fused(NSTAGE)
    ims = [fused_inputs(d, r, hs, consts, NSTAGE) for r in range(NCORES)]
    res = _run(nc, ims)
    if _DBG is not None:
        return None
    out = np.empty((4, 4096, D), np.float32)
    for r in range(NCORES):
        b, half = r // 2, r % 2
        out[b, half * 2048:(half + 1) * 2048, :] = res[r]['out32'][:, 16:].T
    return out
```
